# Optimizing a Trainium2 kernel written in Bass

```python
import jax
import jax.numpy as jnp
from jax import lax
import numpy as np

D_MODEL = 1024
BATCH = 16
SEQ = 2048
DEPTH = 4

N_META = 16
N_MIXERS = 3
CHUNK = 64
LEAD_PAD = (-N_META) % CHUNK
RMS_EPS = 1e-6
D_FF = 4 * D_MODEL

GDN_QK_HEADS = 8
GDN_V_HEADS = 16
GDN_HEAD_DIM = 128
GDN_CONV = 4
GDN_QK_DIM = GDN_QK_HEADS * GDN_HEAD_DIM
GDN_V_DIM = GDN_V_HEADS * GDN_HEAD_DIM
GDN_CONV_DIM = 2 * GDN_QK_DIM + GDN_V_DIM
GDN_IN = GDN_CONV_DIM + GDN_V_DIM + 2 * GDN_V_HEADS

MLSTM_HEADS = 8
MLSTM_QK_DIM = D_MODEL // 2
MLSTM_V_DIM = D_MODEL
MLSTM_DQK = MLSTM_QK_DIM // MLSTM_HEADS
MLSTM_DV = MLSTM_V_DIM // MLSTM_HEADS
MLSTM_IN = 2 * MLSTM_QK_DIM + 2 * MLSTM_V_DIM + 2 * MLSTM_HEADS
GATE_SOFTCAP = 15.0

MLA_HEADS = 16
MLA_NOPE = 64
MLA_ROPE = 32
MLA_QK = MLA_NOPE + MLA_ROPE
MLA_V = 64
MLA_Q_RANK = 384
MLA_KV_RANK = 256
MLA_IN = MLA_Q_RANK + MLA_KV_RANK + MLA_ROPE
ROPE_THETA = 10000.0
Q_BLOCK = 128

N_GDN_LAYERS = len(range(0, DEPTH, N_MIXERS))
N_MLSTM_LAYERS = len(range(1, DEPTH, N_MIXERS))
N_MLA_LAYERS = len(range(2, DEPTH, N_MIXERS))

kernel_name = 'hybrid_gdn_mlstm_mla_trunk'


def rms_norm(x, w):
    xf = x.astype(jnp.float32)
    y = xf * lax.rsqrt(jnp.mean(xf * xf, axis=-1, keepdims=True) + RMS_EPS)
    return (y * w.astype(jnp.float32)).astype(x.dtype)


def l2_norm(x):
    xf = x.astype(jnp.float32)
    return xf * lax.rsqrt(jnp.sum(xf * xf, axis=-1, keepdims=True) + RMS_EPS)


def to_chunks(x, pad_value=0.0):
    widths = [(0, 0), (LEAD_PAD, 0)] + [(0, 0)] * (x.ndim - 2)
    x = jnp.pad(x, widths, constant_values=pad_value)
    b, tp = x.shape[:2]
    x = x.reshape((b, tp // CHUNK, CHUNK) + x.shape[2:])
    return jnp.moveaxis(x, 3, 1)


def from_chunks(x):
    b, nh, n, c, d = x.shape
    x = jnp.moveaxis(x, 1, 3).reshape(b, n * c, nh, d)
    return x[:, LEAD_PAD:]


def causal_conv_silu(x, w):
    k, c = w.shape
    y = lax.conv_general_dilated(x, w[:, None, :].astype(x.dtype), window_strides=(1,),
                                 padding=[(k - 1, 0)], dimension_numbers=('NWC', 'WIO', 'NWC'),
                                 feature_group_count=c)
    return jax.nn.silu(y)


def chunk_gated_delta_rule(q, k, v, g, beta):
    q, k, v = to_chunks(q), to_chunks(k), to_chunks(v)
    g, beta = to_chunks(g), to_chunks(beta)
    dv = v.shape[-1]
    causal = jnp.tril(jnp.ones((CHUNK, CHUNK), dtype=bool))
    strict = jnp.tril(jnp.ones((CHUNK, CHUNK), dtype=bool), k=-1)
    cum = jnp.cumsum(g, axis=-1)
    decay = jnp.exp(jnp.where(causal, cum[..., :, None] - cum[..., None, :], -jnp.inf))
    k_beta = k * beta[..., None]
    a = jnp.where(strict, jnp.einsum('bhnid,bhnjd->bhnij', k_beta, k) * decay, 0.0)
    eye = jnp.eye(CHUNK, dtype=a.dtype)
    rhs = jnp.concatenate([v * beta[..., None], k_beta * jnp.exp(cum)[..., None]], axis=-1)
    sol = lax.linalg.triangular_solve(a + eye, rhs, left_side=True, lower=True, unit_diagonal=True)
    u, w = sol[..., :dv], sol[..., dv:]
    attn = jnp.einsum('bhnid,bhnjd->bhnij', q, k) * decay
    q_dec = q * jnp.exp(cum)[..., None]
    k_dec = k * jnp.exp(cum[..., -1:] - cum)[..., None]
    last = jnp.exp(cum[..., -1])

    def step(state, xs):
        u_c, w_c, q_c, k_c, attn_c, last_c = xs
        v_new = u_c - jnp.einsum('bhck,bhkv->bhcv', w_c, state)
        o = jnp.einsum('bhck,bhkv->bhcv', q_c, state) + jnp.einsum('bhij,bhjv->bhiv', attn_c, v_new)
        state = state * last_c[..., None, None] + jnp.einsum('bhck,bhcv->bhkv', k_c, v_new)
        return state, o

    xs = tuple(jnp.moveaxis(t, 2, 0) for t in (u, w, q_dec, k_dec, attn, last))
    b, nh = q.shape[:2]
    state0 = jnp.zeros((b, nh, q.shape[-1], dv), jnp.float32)
    _, o = lax.scan(step, state0, xs)
    return from_chunks(jnp.moveaxis(o, 0, 2))


def gated_deltanet(h, w_in, w_conv, a_log, dt_bias, w_norm, w_out):
    b, t, _ = h.shape
    proj = h @ w_in
    qkv, z, beta_pre, a_pre = jnp.split(
        proj, [GDN_CONV_DIM, GDN_CONV_DIM + GDN_V_DIM, GDN_CONV_DIM + GDN_V_DIM + GDN_V_HEADS], axis=-1)
    qkv = causal_conv_silu(qkv, w_conv)
    q, k, v = jnp.split(qkv, [GDN_QK_DIM, 2 * GDN_QK_DIM], axis=-1)
    rep = GDN_V_HEADS // GDN_QK_HEADS
    q = l2_norm(q.reshape(b, t, GDN_QK_HEADS, GDN_HEAD_DIM)) * (GDN_HEAD_DIM ** -0.5)
    k = l2_norm(k.reshape(b, t, GDN_QK_HEADS, GDN_HEAD_DIM))
    q = jnp.repeat(q, rep, axis=2)
    k = jnp.repeat(k, rep, axis=2)
    v = v.reshape(b, t, GDN_V_HEADS, GDN_HEAD_DIM).astype(jnp.float32)
    beta = jax.nn.sigmoid(beta_pre.astype(jnp.float32))
    g = -jnp.exp(a_log.astype(jnp.float32)) * jax.nn.softplus(a_pre.astype(jnp.float32) + dt_bias.astype(jnp.float32))
    o = chunk_gated_delta_rule(q, k, v, g, beta)
    o = rms_norm(o, w_norm) * jax.nn.silu(z.reshape(b, t, GDN_V_HEADS, GDN_HEAD_DIM).astype(jnp.float32))
    return o.reshape(b, t, GDN_V_DIM).astype(h.dtype) @ w_out


def chunk_mlstm(q, k, v, i_pre, log_f):
    q, k, v = to_chunks(q), to_chunks(k), to_chunks(v)
    i_pre = to_chunks(i_pre, -jnp.inf)
    log_f = to_chunks(log_f)
    causal = jnp.tril(jnp.ones((CHUNK, CHUNK), dtype=bool))
    cum = jnp.cumsum(log_f, axis=-1)
    log_w = jnp.where(causal, cum[..., :, None] - cum[..., None, :] + i_pre[..., None, :], -jnp.inf)
    log_w_end = cum[..., -1:] - cum + i_pre
    qk = jnp.einsum('bhnid,bhnjd->bhnij', q, k)

    def step(carry, xs):
        c_mat, n_vec, m = carry
        q_c, k_c, v_c, cum_c, lw_c, lwe_c, qk_c = xs
        log_inter = cum_c + m[..., None]
        m_row = jnp.maximum(log_inter, jnp.max(lw_c, axis=-1))
        s_inter = jnp.exp(log_inter - m_row)
        w_intra = jnp.exp(lw_c - m_row[..., None]) * qk_c
        num = s_inter[..., None] * jnp.einsum('bhck,bhkv->bhcv', q_c, c_mat) \
            + jnp.einsum('bhij,bhjv->bhiv', w_intra, v_c)
        den = s_inter * jnp.einsum('bhck,bhk->bhc', q_c, n_vec) + jnp.sum(w_intra, axis=-1)
        h_c = num / jnp.maximum(jnp.abs(den), jnp.exp(-m_row))[..., None]
        log_keep = cum_c[..., -1] + m
        m_new = jnp.maximum(log_keep, jnp.max(lwe_c, axis=-1))
        s_keep = jnp.exp(log_keep - m_new)
        w_end = jnp.exp(lwe_c - m_new[..., None])
        c_mat = s_keep[..., None, None] * c_mat + jnp.einsum('bhc,bhck,bhcv->bhkv', w_end, k_c, v_c)
        n_vec = s_keep[..., None] * n_vec + jnp.einsum('bhc,bhck->bhk', w_end, k_c)
        return (c_mat, n_vec, m_new), h_c

    xs = tuple(jnp.moveaxis(t, 2, 0) for t in (q, k, v, cum, log_w, log_w_end, qk))
    b, nh, _, _, dk = q.shape
    dv = v.shape[-1]
    carry0 = (jnp.zeros((b, nh, dk, dv), jnp.float32), jnp.zeros((b, nh, dk), jnp.float32),
              jnp.zeros((b, nh), jnp.float32))
    _, hs = lax.scan(step, carry0, xs)
    return from_chunks(jnp.moveaxis(hs, 0, 2))


def mlstm(h, w_in, gate_bias, w_norm, w_out):
    b, t, _ = h.shape
    q, k, v, o, gates = jnp.split(
        h @ w_in, [MLSTM_QK_DIM, 2 * MLSTM_QK_DIM, 2 * MLSTM_QK_DIM + MLSTM_V_DIM,
                   2 * MLSTM_QK_DIM + 2 * MLSTM_V_DIM], axis=-1)
    q = q.reshape(b, t, MLSTM_HEADS, MLSTM_DQK).astype(jnp.float32)
    k = k.reshape(b, t, MLSTM_HEADS, MLSTM_DQK).astype(jnp.float32) * (MLSTM_DQK ** -0.5)
    v = v.reshape(b, t, MLSTM_HEADS, MLSTM_DV).astype(jnp.float32)
    gates = gates.astype(jnp.float32) + gate_bias.astype(jnp.float32)
    gates = GATE_SOFTCAP * jnp.tanh(gates / GATE_SOFTCAP)
    i_pre, f_pre = gates[..., :MLSTM_HEADS], gates[..., MLSTM_HEADS:]
    hs = chunk_mlstm(q, k, v, i_pre, jax.nn.log_sigmoid(f_pre))
    hs = rms_norm(hs, w_norm.reshape(MLSTM_HEADS, MLSTM_DV)) \
        * jax.nn.sigmoid(o.reshape(b, t, MLSTM_HEADS, MLSTM_DV).astype(jnp.float32))
    return hs.reshape(b, t, MLSTM_V_DIM).astype(h.dtype) @ w_out


def rope_tables(t):
    pos = jnp.arange(t, dtype=jnp.float32)
    inv_freq = ROPE_THETA ** (-jnp.arange(0, MLA_ROPE, 2, dtype=jnp.float32) / MLA_ROPE)
    ang = pos[:, None] * inv_freq[None, :]
    return jnp.cos(ang), jnp.sin(ang)


def apply_rope(x, cos, sin):
    x_pass, x_rot = x[..., :MLA_NOPE], x[..., MLA_NOPE:]
    x1, x2 = x_rot[..., :MLA_ROPE // 2], x_rot[..., MLA_ROPE // 2:]
    c = cos[None, :, None, :].astype(x.dtype)
    s = sin[None, :, None, :].astype(x.dtype)
    return jnp.concatenate([x_pass, x1 * c - x2 * s, x2 * c + x1 * s], axis=-1)


def causal_block_attention(q, k, v):
    b, t, nh, dq = q.shape
    dv = v.shape[-1]
    n_blk = -(-t // Q_BLOCK)
    qp = jnp.pad(q, ((0, 0), (0, n_blk * Q_BLOCK - t), (0, 0), (0, 0)))
    scale = dq ** -0.5
    k_pos = jnp.arange(t)

    def one_block(i):
        start = i * Q_BLOCK
        qb = lax.dynamic_slice_in_dim(qp, start, Q_BLOCK, axis=1)
        s = jnp.einsum('bqhd,bkhd->bhqk', qb, k).astype(jnp.float32) * scale
        q_pos = start + jnp.arange(Q_BLOCK)
        s = jnp.where(k_pos[None, :] <= q_pos[:, None], s, -jnp.inf)
        p = jax.nn.softmax(s, axis=-1).astype(v.dtype)
        return jnp.einsum('bhqk,bkhd->bqhd', p, v)

    o = lax.map(one_block, jnp.arange(n_blk))
    return jnp.moveaxis(o, 0, 1).reshape(b, n_blk * Q_BLOCK, nh, dv)[:, :t]


def mla(h, w_in, q_norm, w_uq, kv_norm, w_ukv, q_head_norm, k_head_norm, w_out):
    b, t, _ = h.shape
    c_q, c_kv, k_rope = jnp.split(h @ w_in, [MLA_Q_RANK, MLA_Q_RANK + MLA_KV_RANK], axis=-1)
    q = (rms_norm(c_q, q_norm) @ w_uq).reshape(b, t, MLA_HEADS, MLA_QK)
    kv = (rms_norm(c_kv, kv_norm) @ w_ukv).reshape(b, t, MLA_HEADS, MLA_NOPE + MLA_V)
    k_nope, v = kv[..., :MLA_NOPE], kv[..., MLA_NOPE:]
    k_rope = jnp.broadcast_to(k_rope[:, :, None, :], (b, t, MLA_HEADS, MLA_ROPE))
    k = jnp.concatenate([k_nope, k_rope], axis=-1)
    cos, sin = rope_tables(t)
    q = apply_rope(rms_norm(q, q_head_norm), cos, sin)
    k = apply_rope(rms_norm(k, k_head_norm), cos, sin)
    o = causal_block_attention(q, k, v)
    return o.reshape(b, t, MLA_HEADS * MLA_V) @ w_out


def squared_relu_mlp(h, w_up, w_down):
    a = jax.nn.relu(h @ w_up)
    return (a * a) @ w_down


def setup_inputs(seed: int = 0) -> dict:
    key = jax.random.key(seed)
    keys = iter(jax.random.split(key, 40))
    f32 = jnp.float32

    def dense(shape, fan_in):
        return jax.random.normal(next(keys), shape, f32) * (fan_in ** -0.5)

    def gain(shape):
        return 1.0 + 0.02 * jax.random.normal(next(keys), shape, f32)

    x = jax.random.normal(next(keys), (BATCH, SEQ, D_MODEL), f32)
    meta_tokens = jax.random.normal(next(keys), (N_META, D_MODEL), f32)
    attn_norm = gain((DEPTH, D_MODEL))
    ffn_norm = gain((DEPTH, D_MODEL))
    ff_up = dense((DEPTH, D_MODEL, D_FF), D_MODEL)
    ff_down = dense((DEPTH, D_FF, D_MODEL), D_FF)
    gdn_in = dense((N_GDN_LAYERS, D_MODEL, GDN_IN), D_MODEL)
    gdn_conv = dense((N_GDN_LAYERS, GDN_CONV, GDN_CONV_DIM), GDN_CONV)
    gdn_a_log = jnp.log(jax.random.uniform(next(keys), (N_GDN_LAYERS, GDN_V_HEADS), f32, 1.0, 16.0))
    dt = jnp.exp(jax.random.uniform(next(keys), (N_GDN_LAYERS, GDN_V_HEADS), f32,
                                    np.float32(np.log(1e-3)), np.float32(np.log(1e-1))))
    gdn_dt_bias = dt + jnp.log(-jnp.expm1(-dt))
    gdn_norm = gain((N_GDN_LAYERS, GDN_HEAD_DIM))
    gdn_out = dense((N_GDN_LAYERS, GDN_V_DIM, D_MODEL), GDN_V_DIM)
    mlstm_in = dense((N_MLSTM_LAYERS, D_MODEL, MLSTM_IN), D_MODEL)
    ig_bias = 0.1 * jax.random.normal(next(keys), (N_MLSTM_LAYERS, MLSTM_HEADS), f32)
    fg_bias = jnp.linspace(3.0, 6.0, MLSTM_HEADS, dtype=f32)[None, :] \
        + 0.1 * jax.random.normal(next(keys), (N_MLSTM_LAYERS, MLSTM_HEADS), f32)
    mlstm_gate_bias = jnp.concatenate([ig_bias, fg_bias], axis=-1)
    mlstm_norm = gain((N_MLSTM_LAYERS, MLSTM_V_DIM))
    mlstm_out = dense((N_MLSTM_LAYERS, MLSTM_V_DIM, D_MODEL), MLSTM_V_DIM)
    mla_in = dense((N_MLA_LAYERS, D_MODEL, MLA_IN), D_MODEL)
    mla_q_norm = gain((N_MLA_LAYERS, MLA_Q_RANK))
    mla_uq = dense((N_MLA_LAYERS, MLA_Q_RANK, MLA_HEADS * MLA_QK), MLA_Q_RANK)
    mla_kv_norm = gain((N_MLA_LAYERS, MLA_KV_RANK))
    mla_ukv = dense((N_MLA_LAYERS, MLA_KV_RANK, MLA_HEADS * (MLA_NOPE + MLA_V)), MLA_KV_RANK)
    mla_q_head_norm = gain((N_MLA_LAYERS, MLA_QK))
    mla_k_head_norm = gain((N_MLA_LAYERS, MLA_QK))
    mla_out = dense((N_MLA_LAYERS, MLA_HEADS * MLA_V, D_MODEL), MLA_HEADS * MLA_V)
    return {'x': x, 'meta_tokens': meta_tokens, 'attn_norm': attn_norm, 'ffn_norm': ffn_norm,
            'ff_up': ff_up, 'ff_down': ff_down,
            'gdn_in': gdn_in, 'gdn_conv': gdn_conv, 'gdn_a_log': gdn_a_log, 'gdn_dt_bias': gdn_dt_bias,
            'gdn_norm': gdn_norm, 'gdn_out': gdn_out,
            'mlstm_in': mlstm_in, 'mlstm_gate_bias': mlstm_gate_bias, 'mlstm_norm': mlstm_norm,
            'mlstm_out': mlstm_out,
            'mla_in': mla_in, 'mla_q_norm': mla_q_norm, 'mla_uq': mla_uq, 'mla_kv_norm': mla_kv_norm,
            'mla_ukv': mla_ukv, 'mla_q_head_norm': mla_q_head_norm, 'mla_k_head_norm': mla_k_head_norm,
            'mla_out': mla_out}


def reference(x, meta_tokens, attn_norm, ffn_norm, ff_up, ff_down,
              gdn_in, gdn_conv, gdn_a_log, gdn_dt_bias, gdn_norm, gdn_out,
              mlstm_in, mlstm_gate_bias, mlstm_norm, mlstm_out,
              mla_in, mla_q_norm, mla_uq, mla_kv_norm, mla_ukv, mla_q_head_norm, mla_k_head_norm,
              mla_out):
    b = x.shape[0]
    meta = jnp.broadcast_to(meta_tokens[None].astype(x.dtype), (b, N_META, x.shape[-1]))
    h = jnp.concatenate([meta, x], axis=1)
    for layer in range(DEPTH):
        kind, j = layer % N_MIXERS, layer // N_MIXERS
        hn = rms_norm(h, attn_norm[layer])
        if kind == 0:
            mix = gated_deltanet(hn, gdn_in[j], gdn_conv[j], gdn_a_log[j], gdn_dt_bias[j],
                                 gdn_norm[j], gdn_out[j])
        elif kind == 1:
            mix = mlstm(hn, mlstm_in[j], mlstm_gate_bias[j], mlstm_norm[j], mlstm_out[j])
        else:
            mix = mla(hn, mla_in[j], mla_q_norm[j], mla_uq[j], mla_kv_norm[j], mla_ukv[j],
                      mla_q_head_norm[j], mla_k_head_norm[j], mla_out[j])
        h = h + mix
        h = h + squared_relu_mlp(rms_norm(h, ffn_norm[layer]), ff_up[layer], ff_down[layer])
    return h[:, N_META:]
```

```python
import contextlib
import numpy as np
import concourse.bass as bass
import concourse.mybir as mybir

F32 = mybir.dt.float32
BF16 = mybir.dt.bfloat16
AF = mybir.ActivationFunctionType
ALU = mybir.AluOpType
AX = mybir.AxisListType

SEM_LIMIT = 30000
_ISZ = {}


def _isz(dt):
    k = str(dt)
    if k not in _ISZ:
        _ISZ[k] = {"dt.float32": 4, "dt.bfloat16": 2, "dt.int32": 4, "dt.uint32": 4,
                   "dt.float16": 2, "dt.uint8": 1, "dt.int8": 1, "dt.uint16": 2,
                   "dt.int16": 2}.get(k, 4)
    return _ISZ[k]


class SemCounter:
    def __init__(self, prog, name):
        self.prog = prog
        self.name = name
        self.retired = []
        self.sem = None
        self.val = 0
        self.n = 0

    def _new(self):
        self.sem = self.prog.new_sem(f"{self.name}_{self.n}")
        self.n += 1
        self.val = 0

    def next(self, inc):
        if self.sem is None:
            self._new()
        elif self.val + inc > SEM_LIMIT:
            self.retired.append((self.sem, self.val))
            self._new()
        self.val += inc
        return self.sem, self.val


class Ev:
    __slots__ = ("sem", "val", "clock", "eng", "dma_ctr")

    def __init__(self, sem, val, clock, eng, dma_ctr=None):
        self.sem = sem
        self.val = val
        self.clock = clock
        self.eng = eng
        self.dma_ctr = dma_ctr


class Eng:
    def __init__(self, prog, name):
        self.name = name
        self.ops = []
        self.clock = {}
        self.ctr = SemCounter(prog, "s" + name)


def region(ap):
    t = ap.tensor
    name = t.name
    pat = ap.ap
    isz = _isz(ap.dtype)
    space = str(ap.space) if hasattr(ap, "space") else ""
    off = ap.offset
    if "DRAM" in space.upper() or "HBM" in space.upper():
        lo = off
        hi = off
        for st, cnt in pat:
            if st >= 0:
                hi += st * (cnt - 1)
            else:
                lo += st * (cnt - 1)
        return (name, 0, 1, lo * isz, (hi + 1) * isz)
    if "PSUM" in space.upper():
        return (name, 0, 128, 0, 1 << 30, "psum")
    pstride, pcnt = pat[0]
    if pstride == 0:
        pstride = 1 << 40
    p0 = ap.start_partition()
    fo = off - p0 * pstride if pstride < (1 << 40) else off
    lo = fo
    hi = fo
    for st, cnt in pat[1:]:
        if st >= 0:
            hi += st * (cnt - 1)
        else:
            lo += st * (cnt - 1)
    return (name, p0, p0 + pcnt, lo * isz, (hi + 1) * isz)


class Prog:
    def __init__(self, nc, same_engine_sync=True):
        self.nc = nc
        self.stack = contextlib.ExitStack()
        self.engs = {n: Eng(self, n) for n in ("pe", "act", "dve", "pool", "sp")}
        self.recs = {}
        self.dma_ctrs = {}
        self.same_engine_sync = same_engine_sync
        self.nsem = 0
        self.n_ops = 0
        self.pe_last = {}

    def new_sem(self, name):
        self.nsem += 1
        return self.stack.enter_context(self.nc.semaphore(name))

    def sbuf(self, name, shape, dt):
        return self.stack.enter_context(self.nc.sbuf_tensor(name, list(shape), dt))

    def psum(self, name, shape, dt=F32):
        return self.stack.enter_context(self.nc.psum_tensor(name, list(shape), dt))

    def _conflicts(self, reg, is_write, eng=None):
        name, p0, p1, f0, f1 = reg[:5]
        out = []
        if len(reg) > 5:
            for r in self.recs.get(name, ()):
                if r[5].eng != eng:
                    out.append((r[5], True))
                elif is_write or r[4]:
                    out.append((r[5], r[4]))
            return out
        for r in self.recs.get(name, ()):
            if r[0] < p1 and p0 < r[1] and r[2] < f1 and f0 < r[3]:
                if is_write or r[4]:
                    out.append((r[5], r[4]))
        return out

    def _record(self, reg, is_write, ev):
        name, p0, p1, f0, f1 = reg[:5]
        lst = self.recs.setdefault(name, [])
        if len(reg) > 5:
            lst[:] = [r for r in lst if not (r[5].eng == ev.eng and r[4] == is_write)]
            lst.append([p0, p1, f0, f1, is_write, ev])
            return
        if is_write:
            lst[:] = [r for r in lst if not (p0 <= r[0] and r[1] <= p1 and f0 <= r[2] and r[3] <= f1)]
        else:
            lst[:] = [r for r in lst if not ((not r[4]) and r[5].eng == ev.eng and r[5].dma_ctr is None
                                             and ev.dma_ctr is None
                                             and r[0] == p0 and r[1] == p1 and r[2] == f0 and r[3] == f1)]
        lst.append([p0, p1, f0, f1, is_write, ev])

    def _gather_waits(self, eng, reads, writes):
        e = self.engs[eng]
        need = {}
        deps = []
        for reg in reads:
            for ev, was_write in self._conflicts(reg, False, eng):
                deps.append((ev, True))
        for reg in writes:
            for ev, was_write in self._conflicts(reg, True, eng):
                deps.append((ev, was_write))
        for ev, hard in deps:
            if ev.dma_ctr is None and ev.eng == eng:
                if eng in ("pe", "sp"):
                    continue
                if not hard or not self.same_engine_sync:
                    continue
            pairs = []
            if ev.dma_ctr is not None:
                c = ev.dma_ctr
                pairs.extend(c.retired)
                pairs.append((c.sem, c.val))
            else:
                pairs.append((ev.sem, ev.val))
            for sem, val in pairs:
                if e.clock.get(id(sem), 0) >= val:
                    continue
                k = id(sem)
                if k not in need or need[k][1] < val:
                    need[k] = (sem, val)
            for k, v in ev.clock.items():
                if e.clock.get(k, 0) < v:
                    e.clock[k] = v
        waits = list(need.values())
        for sem, val in waits:
            if e.clock.get(id(sem), 0) < val:
                e.clock[id(sem)] = val
        return waits

    def op(self, eng, fn, reads=(), writes=(), extra_reads=(), extra_writes=()):
        e = self.engs[eng]
        rr = [region(a) for a in reads] + list(extra_reads)
        ww = [region(a) for a in writes] + list(extra_writes)
        waits = self._gather_waits(eng, rr, ww)
        sem, val = e.ctr.next(1)
        clock = dict(e.clock)
        clock[id(sem)] = val
        ev = Ev(sem, val, clock, eng)
        e.ops.append((waits, fn, sem, 1))
        for reg in rr:
            self._record(reg, False, ev)
        for reg in ww:
            self._record(reg, True, ev)
        self.n_ops += 1
        return ev

    def dma(self, q, out, in_, extra_reads=(), extra_writes=(), sem_key=None, in_reg=None, out_reg=None, **kw):
        e = self.engs[q]
        rr = [in_reg if in_reg is not None else region(in_)] + list(extra_reads)
        ww = [out_reg if out_reg is not None else region(out)] + list(extra_writes)
        waits = self._gather_waits(q, rr, ww)
        key = sem_key or out.tensor.name
        if key not in self.dma_ctrs:
            self.dma_ctrs[key] = SemCounter(self, "d" + key[:20])
        c = self.dma_ctrs[key]
        sem, val = c.next(16)
        clock = dict(e.clock)
        ev = Ev(sem, val, clock, q, dma_ctr=c)

        def fn(engh, out=out, in_=in_, kw=kw):
            return engh.dma_start(out=out, in_=in_, **kw)
        e.ops.append((waits, fn, sem, 16))
        for reg in rr:
            self._record(reg, False, ev)
        for reg in ww:
            self._record(reg, True, ev)
        self.n_ops += 1
        return ev

    def final_wait(self, eng="sp"):
        e = self.engs[eng]
        waits = []
        for c in self.dma_ctrs.values():
            for sem, val in c.retired + [(c.sem, c.val)]:
                if e.clock.get(id(sem), 0) < val:
                    waits.append((sem, val))
                    e.clock[id(sem)] = val
        for n, o in self.engs.items():
            if o.ctr.sem is not None and n != eng:
                for sem, val in o.ctr.retired + [(o.ctr.sem, o.ctr.val)]:
                    if e.clock.get(id(sem), 0) < val:
                        waits.append((sem, val))
                        e.clock[id(sem)] = val
        e.ops.append((waits, None, None, 0))

    def emit(self):
        nc = self.nc
        with nc.Block() as block:
            def run(name):
                def body(h):
                    for waits, fn, sem, inc in self.engs[name].ops:
                        for s, v in waits:
                            h.wait_ge(s, v)
                        if fn is not None:
                            ins = fn(h)
                            ins.then_inc(sem, inc)
                return body
            if self.engs["sp"].ops:
                block.sync(run("sp"))
            if self.engs["act"].ops:
                block.scalar(run("act"))
            if self.engs["dve"].ops:
                block.vector(run("dve"))
            if self.engs["pool"].ops:
                block.gpsimd(run("pool"))
            if self.engs["pe"].ops:
                block.tensor(run("pe"))
        self.stack.close()

    def mm(self, out, lhsT, rhs, start=True, stop=True, **kw):
        rd = [lhsT, rhs]
        bank = out.tensor.name
        lr = region(lhsT)
        g0, g1 = lr[1] // 32, (lr[2] + 31) // 32
        prev = self.pe_last.get(bank)
        e = self.engs["pe"]
        extra = []
        if prev is not None and (prev[1] <= g0 or g1 <= prev[0]):
            sem, val = prev[2].sem, prev[2].val
            if e.clock.get(id(sem), 0) < val:
                extra.append((sem, val))
                e.clock[id(sem)] = val
        ev = self.op("pe", lambda h: h.matmul(out, lhsT, rhs, start=start, stop=stop, **kw),
                     reads=rd if start else rd + [out], writes=[out])
        if extra:
            w, fn, sm, inc = e.ops[-1]
            e.ops[-1] = (list(w) + extra, fn, sm, inc)
        self.pe_last[bank] = (g0, g1, ev)
        return ev

    def transpose(self, out, in_, ident):
        return self.op("pe", lambda h: h.transpose(out, in_, ident), reads=[in_, ident], writes=[out])

    def act(self, out, in_, func, bias=None, scale=None, accum_out=None, eng="act"):
        kw = {}
        rd = [in_]
        wr = [out]
        if bias is not None:
            kw["bias"] = bias
            if not isinstance(bias, (int, float)):
                rd.append(bias)
        if scale is not None:
            kw["scale"] = scale
            if not isinstance(scale, (int, float)):
                rd.append(scale)
        if accum_out is not None:
            kw["accum_out"] = accum_out
            wr.append(accum_out)
        return self.op("act", lambda h: h.activation(out, in_, func, **kw), reads=rd, writes=wr)

    def tt(self, eng, out, in0, in1, op):
        return self.op(eng, lambda h: h.tensor_tensor(out, in0, in1, op), reads=[in0, in1], writes=[out])

    def ts(self, eng, out, in0, s1, op0, s2=None, op1=None, accum_out=None):
        rd = [in0]
        if not isinstance(s1, (int, float)):
            rd.append(s1)
        if s2 is not None and not isinstance(s2, (int, float)):
            rd.append(s2)
        wr = [out]
        kw = {}
        if s2 is None:
            s2 = 0.0
            op1 = ALU.add
        if op1 is not None:
            kw["op1"] = op1
        if accum_out is not None:
            kw["accum_out"] = accum_out
            wr.append(accum_out)
        return self.op(eng, lambda h: h.tensor_scalar(out, in0, s1, s2, op0, **kw), reads=rd, writes=wr)

    def stt(self, eng, out, in0, scalar, in1, op0, op1):
        rd = [in0, in1]
        if not isinstance(scalar, (int, float)):
            rd.append(scalar)
        return self.op("dve", lambda h: h.scalar_tensor_tensor(out, in0, scalar, in1, op0, op1),
                       reads=rd, writes=[out])

    def copy(self, eng, out, in_):
        if eng == "act":
            return self.op("act", lambda h: h.copy(out, in_), reads=[in_], writes=[out])
        return self.op(eng, lambda h: h.tensor_copy(out, in_), reads=[in_], writes=[out])

    def memset(self, eng, out, val):
        return self.op(eng, lambda h: h.memset(out, val), reads=[], writes=[out])

    def recip(self, out, in_):
        return self.op("dve", lambda h: h.reciprocal(out, in_), reads=[in_], writes=[out])

    def reduce(self, eng, out, in_, op, axis=None):
        axis = axis or AX.X
        return self.op(eng, lambda h: h.tensor_reduce(out, in_, axis, op), reads=[in_], writes=[out])

from concourse.bass_utils import run_bass_kernel_spmd

NS = 2
TX = 2048
NM = 16
TS = TX + NM
NTOK = NS * TS
D = 1024
KC = 8
EPS = 1e-6
WA_N = 66304
FA_N = 11800


def bcast(ap, shape, axis):
    return ap.unsqueeze(axis).broadcast_to(list(shape))


class Arena:
    def __init__(self, t, n, base=0):
        self.t = t
        self.n = n
        self.base = base
        self.off = base

    def reset(self, base=None):
        if base is not None:
            self.base = base
        self.off = self.base

    def get(self, shape, parts=128):
        size = 1
        for s in shape:
            size *= s
        assert self.off + size <= self.n, (self.off, size, self.n)
        a = self.t[0:parts, self.off:self.off + size]
        self.off += size
        if len(shape) == 2:
            a = a.rearrange("p (a b) -> p a b", a=shape[0])
        elif len(shape) == 3:
            a = a.rearrange("p (a b c) -> p a b c", a=shape[0], b=shape[1])
        elif len(shape) == 4:
            a = a.rearrange("p (a b c d) -> p a b c d", a=shape[0], b=shape[1], c=shape[2])
        return a


class C:
    pass


def seq_blocks(TB):
    out = [(TX, NM)]
    for b in range(TX // TB):
        out.append((b * TB, TB))
    return out


def build_program(n_layers=4, debug=False, stop_after=None):
    nc = bass.Bass("TRN2", target_bir_lowering=False)
    P = Prog(nc)
    c = C()
    c.P = P
    c.nc = nc
    dt = nc.dram_tensor

    def din(name, shape):
        return dt(name, list(shape), F32, kind="ExternalInput").ap()

    c.x = din("x", [NS, TX, D])
    c.meta = din("meta_tokens", [NM, D])
    c.attn_norm = din("attn_norm", [4, D])
    c.ffn_norm = din("ffn_norm", [4, D])
    c.ff_up = din("ff_up", [4, D, 4096])
    c.ff_down = din("ff_down", [4, 4096, D])
    c.gdn_in = din("gdn_in", [2, D, 6176])
    c.gdn_conv = din("gdn_conv", [2, 4, 4096])
    c.gdn_a_log = din("gdn_a_log", [2, 16])
    c.gdn_dt_bias = din("gdn_dt_bias", [2, 16])
    c.gdn_norm = din("gdn_norm", [2, 128])
    c.gdn_out = din("gdn_out", [2, 2048, D])
    c.mlstm_in = din("mlstm_in", [1, D, 3088])
    c.mlstm_gate_bias = din("mlstm_gate_bias", [1, 16])
    c.mlstm_norm = din("mlstm_norm", [1, D])
    c.mlstm_out = din("mlstm_out", [1, D, D])
    c.mla_in = din("mla_in", [1, D, 672])
    c.mla_q_norm = din("mla_q_norm", [1, 384])
    c.mla_uq = din("mla_uq", [1, 384, 1536])
    c.mla_kv_norm = din("mla_kv_norm", [1, 256])
    c.mla_ukv = din("mla_ukv", [1, 256, 2048])
    c.mla_q_head_norm = din("mla_q_head_norm", [1, 96])
    c.mla_k_head_norm = din("mla_k_head_norm", [1, 96])
    c.mla_out = din("mla_out", [1, D, D])
    c.y = dt("y", [NS, TX, D], F32, kind="ExternalOutput").ap()
    c.hT = dt("hT", [KC, 128, NTOK], F32, kind=("ExternalOutput" if debug else "Internal")).ap()
    c.hTv = c.hT.rearrange("c p t -> p c t")
    if debug:
        c.dbg = dt("dbg", [8, KC, 128, NTOK], F32, kind="ExternalOutput").ap()
    c.s_fm = dt("s_fm", [32, 128, NTOK], BF16, kind="Internal").ap()
    c.s_fmv = c.s_fm.rearrange("c p t -> p c t")
    c.s_ktm = dt("s_ktm", [NTOK, 1024], BF16, kind="Internal").ap()
    c.s_vtm = dt("s_vtm", [NTOK, 2048], BF16, kind="Internal").ap()
    c.s_gb = dt("s_gb", [NTOK, 32], F32, kind="Internal").ap()

    c.WAt = P.sbuf("WA", [128, WA_N], BF16)
    c.FAt = P.sbuf("FA", [128, FA_N], F32)
    c.WA = Arena(c.WAt, WA_N)
    c.FA = Arena(c.FAt, FA_N)
    c.ident = P.sbuf("ident", [128, 128], F32)
    c.identb = P.sbuf("identb", [128, 128], BF16)
    c.Ui = P.sbuf("Ui", [128, 128], F32)
    c.Us = P.sbuf("Us", [128, 128], F32)
    c.Ls = P.sbuf("Ls", [128, 128], F32)
    c.ones = P.sbuf("ones", [128, 128], F32)
    c.onesb = P.sbuf("onesb", [128, 128], BF16)
    c.onesD = P.sbuf("onesD", [128, 128], F32)
    c.onesH = P.sbuf("onesH", [128, 128], F32)
    c.small = P.sbuf("small", [128, 256], F32)
    c.ffn_hn = P.sbuf("ffn_hn", [128, 8, 256], BF16)
    c.ffn_a = P.sbuf("ffn_a", [128, 32, 256], BF16)
    c.pb = [P.psum(f"pb{i}", [128, 512], F32) for i in range(7)]
    c.pT = P.psum("pT", [128, 1024], BF16)

    def sel(t, pattern, op, cm):
        P.op("pool", lambda h: h.affine_select(out=t[:], in_=t[:], pattern=pattern, compare_op=op,
                                               fill=0.0, base=0, channel_multiplier=cm),
             reads=[t[:]], writes=[t[:]])
    for t in (c.ident, c.Ui, c.Us, c.Ls, c.ones):
        P.memset("pool", t[:], 1.0)
    P.memset("pool", c.onesD[:], 1.0 / 1024.0)
    P.memset("pool", c.onesH[:], 1.0 / 128.0)
    P.memset("pool", c.onesb[:], 1.0)
    sel(c.ident, [[-1, 128]], ALU.is_equal, 1)
    sel(c.Ui, [[1, 128]], ALU.is_ge, -1)
    sel(c.Us, [[1, 128]], ALU.is_gt, -1)
    sel(c.Ls, [[-1, 128]], ALU.is_gt, 1)
    P.copy("dve", c.identb[:], c.ident[:])
    c.eps = c.small[:, 0:1]
    P.memset("pool", c.eps, EPS)

    import os
    c.kstop = os.environ.get("K_STOP", "")
    c.kp2 = os.environ.get("K_P2", "")
    c.knblk = int(os.environ.get("K_NBLK", "100"))
    phase_input(c)
    for layer in range(n_layers if c.kstop != "input" else 0):
        kind, j = layer % 3, layer // 3
        if kind == 0:
            gdn_layer(c, layer, j)
        elif kind == 1:
            mlstm_layer(c, layer, j)
        else:
            mla_layer(c, layer, j)
        if debug:
            P.dma("sp", c.dbg[2 * layer], c.hT, in_reg=hreg(0, NTOK))
        if stop_after == ("mix", layer):
            break
        ffn_layer(c, layer)
        if debug:
            P.dma("sp", c.dbg[2 * layer + 1], c.hT, in_reg=hreg(0, NTOK))
    if not debug:
        phase_output(c)
    P.final_wait("sp")
    P.emit()
    return nc


def hreg(t0, t1):
    return ("hT", 0, 1, t0, t1)


def sreg(name, t0, t1):
    return (name, 0, 1, t0, t1)


def phase_input(c):
    P = c.P
    c.FA.reset(0)
    xt = [c.FA.get([1024]) for _ in range(2)]
    hblk = [c.FA.get([8, 128]) for _ in range(2)]
    n = 0
    for s in range(NS):
        for i in range(TX // 128 + 1):
            tp = 128 if i < TX // 128 else NM
            src = c.x[s, i * 128:(i + 1) * 128, :] if tp == 128 else c.meta
            t0 = s * TS + i * 128
            xb, hb = xt[n % 2], hblk[n % 2]
            P.dma("sp", xb[0:tp, :], src)
            for g in range(2):
                ps = c.pb[(n * 2 + g) % 4]
                for j in range(4):
                    k = g * 4 + j
                    P.mm(ps[:, j * 128:j * 128 + tp], xb[0:tp, k * 128:(k + 1) * 128], c.ident[0:tp, 0:tp])
                src_ps = ps[:].rearrange("p (a b) -> p a b", a=4)[:, :, 0:tp]
                P.copy("dve" if g == 0 else "act", hb[:, g * 4:(g + 1) * 4, 0:tp], src_ps)
            P.dma("act", c.hTv[:, :, t0:t0 + tp], hb[:, :, 0:tp], out_reg=hreg(t0, t0 + tp))
            n += 1


def phase_output(c):
    P = c.P
    c.FA.reset(0)
    hblk = [c.FA.get([8, 128]) for _ in range(2)]
    yt = [c.FA.get([1024]) for _ in range(2)]
    n = 0
    for s in range(NS):
        for i in range(TX // 128):
            t0 = s * TS + i * 128
            hb, yb = hblk[n % 2], yt[n % 2]
            P.dma("sp", hb, c.hTv[:, :, t0:t0 + 128], in_reg=hreg(t0, t0 + 128))
            for g in range(2):
                ps = c.pb[(n * 2 + g) % 4]
                for j in range(4):
                    k = g * 4 + j
                    P.mm(ps[:, j * 128:(j + 1) * 128], hb[:, k, :], c.ident[:])
                P.copy("dve" if g == 0 else "act", yb[:, g * 512:(g + 1) * 512], ps[:])
            P.dma("act", c.y[s, i * 128:(i + 1) * 128, :], yb)
            n += 1


def load_vec_cols(c, dst, src_1d, nchunk):
    P = c.P
    tmp = c.small[0:nchunk, 128:256]
    P.dma("sp", tmp, src_1d.rearrange("(k p) -> k p", p=128))
    ps = c.pb[6]
    P.mm(ps[:, 0:nchunk], tmp, c.ident[0:nchunk, 0:nchunk])
    P.copy("dve", dst, ps[:, 0:nchunk])


def load_weight(c, dst, src2d, kchunks, cols, q="pool"):
    P = c.P
    sv = src2d.rearrange("(k p) f -> p k f", p=128)
    step = max(1, 8192 // cols)
    for k0 in range(0, kchunks, step):
        k1 = min(kchunks, k0 + step)
        P.dma(q, dst[:, k0:k1, :], sv[:, k0:k1, :])


def rmsnorm_block(c, hb, TB, wcol, hn, sq, rstd, ps, nk=KC, ones=None, eng2="dve"):
    P = c.P
    ones = ones if ones is not None else c.onesD
    for k in range(nk):
        s = sq[k % 2]
        P.act(s[:, 0:TB], hb[:, k, :], AF.Square)
        P.mm(ps[:, 0:TB], ones[:], s[:, 0:TB], start=(k == 0), stop=(k == nk - 1))
    P.act(rstd[:, 0:TB], ps[:, 0:TB], AF.Ln, bias=c.eps)
    P.act(rstd[:, 0:TB], rstd[:, 0:TB], AF.Exp, scale=-0.5)
    for k in range(nk):
        P.stt(eng2 if k % 2 == 0 else "dve", hn[:, k, :], hb[:, k, :], wcol[:, k:k + 1], rstd[:, 0:TB], ALU.mult, ALU.mult)


def ffn_layer(c, layer):
    P = c.P
    TB = 256
    c.WA.reset(0)
    c.FA.reset(0)
    wup = c.WA.get([8, 4096])
    wdn = c.WA.get([32, 1024])
    load_weight(c, wup, c.ff_up[layer], 8, 4096)
    load_weight(c, wdn, c.ff_down[layer], 32, 1024)
    wcol = c.FA.get([8])
    load_vec_cols(c, wcol, c.ffn_norm[layer], 8)
    hbs = [c.FA.get([8, TB]) for _ in range(2)]
    sq = [c.FA.get([TB]) for _ in range(2)]
    rstd = c.FA.get([TB])
    rl = [c.FA.get([TB]) for _ in range(2)]
    hn = c.ffn_hn
    a = c.ffn_a
    blocks = []
    for s in range(NS):
        for (o, tb) in seq_blocks(TB):
            blocks.append((s * TS + o, tb))
    for bi, (t0, tb) in enumerate(blocks):
        hb = hbs[bi % 2]
        P.dma("sp", hb[:, :, 0:tb], c.hTv[:, :, t0:t0 + tb], in_reg=hreg(t0, t0 + tb))
        rmsnorm_block(c, hb[:, :, 0:tb], tb, wcol, hn[:, :, 0:tb], sq, rstd, c.pb[6])
        for m in range(32):
            ps = c.pb[m % 3]
            for k in range(8):
                P.mm(ps[:, 0:tb], wup[:, k, m * 128:(m + 1) * 128], hn[:, k, 0:tb], start=(k == 0), stop=(k == 7))
            r = rl[m % 2]
            P.act(r[:, 0:tb], ps[:, 0:tb], AF.Relu)
            P.tt("pool", a[:, m, 0:tb], r[:, 0:tb], r[:, 0:tb], ALU.mult)
        for n in range(8):
            ps = c.pb[3 + n % 3]
            for m in range(32):
                P.mm(ps[:, 0:tb], wdn[:, m, n * 128:(n + 1) * 128], a[:, m, 0:tb], start=(m == 0), stop=(m == 31))
            P.tt("dve", hb[:, n, 0:tb], hb[:, n, 0:tb], ps[:, 0:tb], ALU.add)
        P.dma("act", c.hTv[:, :, t0:t0 + tb], hb[:, :, 0:tb], out_reg=hreg(t0, t0 + tb))


def gdn_layer(c, layer, j):
    gdn_pass1(c, layer, j)
    if c.kstop == "p1":
        return
    gdn_pass2(c, layer, j)


def gdn_pass1(c, layer, j):
    P = c.P
    TB = 256
    c.WA.reset(0)
    c.FA.reset(0)
    win = c.WA.get([8, 6176])
    load_weight(c, win, c.gdn_in[j], 8, 6176)
    hn = c.WA.get([8, TB])
    qk_st = c.WA.get([16, TB])
    zs_st = c.WA.get([16, TB])
    ktm_st = c.WA.get([2, 8, 128])
    vtm_st = c.WA.get([2, 16, 128])
    wcol = c.FA.get([8])
    load_vec_cols(c, wcol, c.attn_norm[layer], 8)
    wconv = c.FA.get([32, 4])
    cw = c.FA.get([4096])
    P.dma("sp", cw[0:4, :], c.gdn_conv[j])
    for m in range(32):
        ps = c.pb[6]
        P.mm(ps[:, 0:4], cw[0:4, m * 128:(m + 1) * 128], c.ident[0:4, 0:4])
        P.copy("dve", wconv[:, m, :], ps[:, 0:4])
    c.FA.off -= 4096
    dtb = c.FA.get([16])
    nalog = c.FA.get([16])
    P.dma("sp", dtb, c.gdn_dt_bias[j].partition_broadcast(128))
    P.dma("sp", nalog, c.gdn_a_log[j].partition_broadcast(128))
    P.act(nalog, nalog, AF.Exp)
    P.ts("dve", nalog, nalog, -1.0, ALU.mult)
    hbs = [c.FA.get([8, TB]) for _ in range(1)]
    sq = [c.FA.get([TB]) for _ in range(2)]
    rstd = c.FA.get([TB])
    pre = [c.FA.get([TB + 3]) for _ in range(2)]
    yv = [c.FA.get([TB]) for _ in range(2)]
    sv = [c.FA.get([TB]) for _ in range(2)]
    rn = c.FA.get([TB])
    halo = c.FA.get([32, 3])
    gb_st = c.FA.get([2, 32])
    gt = c.FA.get([16])
    vb = [c.WAt[:, WA_N - 512:WA_N - 256], c.WAt[:, WA_N - 256:WA_N]]
    assert c.WA.off <= WA_N - 512
    QSC = 128.0 ** -0.5
    for s in range(NS):
        P.memset("pool", halo, 0.0)
        for (o, tb) in seq_blocks(TB):
            t0 = s * TS + o
            nt = (tb + 127) // 128
            tp = min(tb, 128)
            hb = hbs[0]
            P.dma("sp", hb[:, :, 0:tb], c.hTv[:, :, t0:t0 + tb], in_reg=hreg(t0, t0 + tb))
            rmsnorm_block(c, hb[:, :, 0:tb], tb, wcol, hn[:, :, 0:tb], sq, rstd, c.pb[6])
            for m in range(32):
                ps = c.pb[m % 2]
                for k in range(8):
                    P.mm(ps[:, 0:tb], win[:, k, m * 128:(m + 1) * 128], hn[:, k, 0:tb], start=(k == 0), stop=(k == 7))
                pr, y, sx = pre[m % 2], yv[m % 2], sv[m % 2]
                P.copy("pool", pr[:, 0:3], halo[:, m, :])
                P.copy("act", pr[:, 3:3 + tb], ps[:, 0:tb])
                P.copy("pool", halo[:, m, :], pr[:, tb:tb + 3])
                P.ts("dve", y[:, 0:tb], pr[:, 0:tb], wconv[:, m, 0:1], ALU.mult)
                for i in range(1, 4):
                    P.stt("dve", y[:, 0:tb], pr[:, i:i + tb], wconv[:, m, i:i + 1], y[:, 0:tb], ALU.mult, ALU.add)
                if m < 16:
                    P.act(sx[:, 0:tb], y[:, 0:tb], AF.Silu)
                    sqq = sq[m % 2]
                    P.act(sqq[:, 0:tb], sx[:, 0:tb], AF.Square)
                    ps2 = c.pb[2 + m % 2]
                    P.mm(ps2[:, 0:tb], c.ones[:], sqq[:, 0:tb])
                    P.act(rn[:, 0:tb], ps2[:, 0:tb], AF.Ln, bias=c.eps)
                    P.act(rn[:, 0:tb], rn[:, 0:tb], AF.Exp, scale=-0.5)
                    P.stt("pool", qk_st[:, m, 0:tb], sx[:, 0:tb], (QSC if m < 8 else 1.0), rn[:, 0:tb], ALU.mult, ALU.mult)
                    if m >= 8:
                        for ti in range(nt):
                            P.transpose(c.pT[0:tp, (m % 8) * 128:(m % 8) * 128 + 128], qk_st[:, m, ti * 128:ti * 128 + tp], c.identb[:])
                            P.copy("dve", ktm_st[0:tp, ti, m - 8, :], c.pT[0:tp, (m % 8) * 128:(m % 8) * 128 + 128])
                else:
                    vbb = vb[m % 2]
                    P.act(vbb[:, 0:tb], y[:, 0:tb], AF.Silu)
                    for ti in range(nt):
                        P.transpose(c.pT[0:tp, (m % 8) * 128:(m % 8) * 128 + 128], vbb[:, ti * 128:ti * 128 + tp], c.identb[:])
                        P.copy("dve", vtm_st[0:tp, ti, m - 16, :], c.pT[0:tp, (m % 8) * 128:(m % 8) * 128 + 128])
            for m in range(16):
                ps = c.pb[m % 2]
                for k in range(8):
                    P.mm(ps[:, 0:tb], win[:, k, 4096 + m * 128:4096 + (m + 1) * 128], hn[:, k, 0:tb], start=(k == 0), stop=(k == 7))
                P.act(zs_st[:, m, 0:tb], ps[:, 0:tb], AF.Silu)
            for ti in range(nt):
                ps = c.pb[4 + ti % 2]
                for k in range(8):
                    P.mm(ps[0:tp, 0:32], hn[:, k, ti * 128:ti * 128 + tp], win[:, k, 6144:6176], start=(k == 0), stop=(k == 7))
                P.act(gb_st[0:tp, ti, 0:16], ps[0:tp, 0:16], AF.Sigmoid)
                P.tt("dve", gt[0:tp, :], ps[0:tp, 16:32], dtb[0:tp, :], ALU.add)
                P.act(gt[0:tp, :], gt[0:tp, :], AF.Exp)
                P.act(gt[0:tp, :], gt[0:tp, :], AF.Ln, bias=1.0)
                P.tt("dve", gb_st[0:tp, ti, 16:32], gt[0:tp, :], nalog[0:tp, :], ALU.mult)
            P.dma("act", c.s_fmv[:, 0:16, t0:t0 + tb], qk_st[:, :, 0:tb], out_reg=sreg("s_fm", t0, t0 + tb))
            P.dma("act", c.s_fmv[:, 16:32, t0:t0 + tb], zs_st[:, :, 0:tb], out_reg=sreg("s_fm", t0, t0 + tb))
            P.dma("sp", c.s_ktm[t0:t0 + tb, :].rearrange("(n p) f -> p n f", p=tp),
                  ktm_st[0:tp, 0:nt, :, :].rearrange("p n h d -> p n (h d)"), out_reg=sreg("s_ktm", t0, t0 + tb))
            P.dma("sp", c.s_vtm[t0:t0 + tb, :].rearrange("(n p) f -> p n f", p=tp),
                  vtm_st[0:tp, 0:nt, :, :].rearrange("p n h d -> p n (h d)"), out_reg=sreg("s_vtm", t0, t0 + tb))
            P.dma("sp", c.s_gb[t0:t0 + tb, :].rearrange("(n p) f -> p n f", p=tp), gb_st[0:tp, 0:nt, :],
                  out_reg=sreg("s_gb", t0, t0 + tb))


def gdn_pass2(c, layer, j):
    P = c.P
    TB = 256
    H = 16
    c.WA.reset(0)
    c.FA.reset(0)
    wout = c.WA.get([16, 1024])
    load_weight(c, wout, c.gdn_out[j], 16, 1024)
    qk = c.WA.get([16, TB])
    zs = c.WA.get([16, TB])
    ktm = c.WA.get([4, 8, 128])
    vtm = c.WA.get([4, 16, 128])
    ogT = c.WA.get([16, TB])
    kbT = c.WA.get([16, 64])
    qdT = c.WA.get([16, 64])
    rhsB = c.WA.get([2, 16, 64])
    bv = c.WA.get([16, 128])
    kbg = c.WA.get([16, 128])
    kdec = c.WA.get([16, 128])
    Rb = c.WA.get([16, 64])
    attnT = c.WA.get([16, 64])
    vnew = c.WA.get([16, 128])
    nwT = c.WA.get([16, 64])
    Sb = c.WA.get([16, 128])
    hb = c.FA.get([8, TB])
    S = c.FA.get([16, 128])
    gb = c.FA.get([4, 32])
    gam = c.FA.get([16])
    Gam = c.FA.get([16])
    last = c.FA.get([16])
    elm = c.FA.get([16])
    bG = c.FA.get([16])
    wn = c.FA.get([1])
    rhsG = c.FA.get([16, 64])
    Dm = c.FA.get([16, 64])
    DT = c.FA.get([16, 64])
    tA = c.FA.get([16, 64])
    tB = c.FA.get([16, 64])
    Rf = c.FA.get([16, 64])
    P.dma("sp", wn, c.gdn_norm[j].rearrange("(p o) -> p o", o=1))
    pb = c.pb

    def hp(ap3, n=2):
        return ap3.rearrange("p (a b) x -> p a b x", b=2)

    for s in range(NS):
        P.memset("pool", S, 0.0)
        P.memset("pool", Sb, 0.0)
        for (o, tb) in seq_blocks(TB)[:c.knblk]:
            t0 = s * TS + o
            Cc = min(tb, 64)
            nch = tb // Cc
            P.dma("sp", qk[:, :, 0:tb], c.s_fmv[:, 0:16, t0:t0 + tb], in_reg=sreg("s_fm", t0, t0 + tb))
            P.dma("sp", zs[:, :, 0:tb], c.s_fmv[:, 16:32, t0:t0 + tb], in_reg=sreg("s_fm", t0, t0 + tb))
            P.dma("act", ktm[0:Cc, 0:nch, :, :].rearrange("p n h d -> p n (h d)"),
                  c.s_ktm[t0:t0 + tb, :].rearrange("(n p) f -> p n f", p=Cc), in_reg=sreg("s_ktm", t0, t0 + tb))
            P.dma("act", vtm[0:Cc, 0:nch, :, :].rearrange("p n h d -> p n (h d)"),
                  c.s_vtm[t0:t0 + tb, :].rearrange("(n p) f -> p n f", p=Cc), in_reg=sreg("s_vtm", t0, t0 + tb))
            P.dma("act", gb[0:Cc, 0:nch, :], c.s_gb[t0:t0 + tb, :].rearrange("(n p) f -> p n f", p=Cc),
                  in_reg=sreg("s_gb", t0, t0 + tb))
            P.dma("sp", hb[:, :, 0:tb], c.hTv[:, :, t0:t0 + tb], in_reg=hreg(t0, t0 + tb))
            L = 5 if Cc == 64 else 3
            for n in range(nch):
                cs = slice(n * Cc, (n + 1) * Cc)
                beta = gb[0:Cc, n, 0:16]
                g = gb[0:Cc, n, 16:32]
                qT = qk[:, 0:8, cs]
                kT = qk[:, 8:16, cs]
                P.mm(pb[0][0:Cc, 0:16], c.Ui[0:Cc, 0:Cc], g)
                P.mm(pb[0][:, 16:32], c.ones[0:Cc, :], g)
                P.copy("dve", gam[0:Cc, :], pb[0][0:Cc, 0:16])
                P.act(Gam[0:Cc, :], pb[0][0:Cc, 0:16], AF.Exp)
                P.act(last[:, :], pb[0][:, 16:32], AF.Exp)
                P.tt("dve", elm[0:Cc, :], pb[0][0:Cc, 16:32], gam[0:Cc, :], ALU.subtract)
                P.act(elm[0:Cc, :], elm[0:Cc, :], AF.Exp)
                P.tt("pool", bG[0:Cc, :], beta, Gam[0:Cc, :], ALU.mult)
                if c.kp2 == 'A':
                    continue
                P.tt("pool", rhsG[0:Cc, :, 0:Cc], bcast(c.Ui[0:Cc, 0:Cc], [Cc, H, Cc], 1), bcast(g, [Cc, H, Cc], 2), ALU.mult)
                P.tt("pool", rhsB[0:Cc, 0, :, 0:Cc], bcast(c.ident[0:Cc, 0:Cc], [Cc, H, Cc], 1), bcast(beta, [Cc, H, Cc], 2), ALU.mult)
                P.tt("pool", rhsB[0:Cc, 1, :, 0:Cc], bcast(c.ident[0:Cc, 0:Cc], [Cc, H, Cc], 1), bcast(Gam[0:Cc, :], [Cc, H, Cc], 2), ALU.mult)
                for hh in range(2):
                    hs = slice(hh * 8, hh * 8 + 8)
                    P.mm(pb[1 + hh][:, 0:8 * Cc].rearrange("p (a b) -> p a b", a=8), c.onesb[0:Cc, :], rhsB[0:Cc, 0, hs, 0:Cc])
                    P.mm(pb[3 + hh][:, 0:8 * Cc].rearrange("p (a b) -> p a b", a=8), c.onesb[0:Cc, :], rhsB[0:Cc, 1, hs, 0:Cc])
                    P.mm(pb[5 + hh][0:Cc, 0:8 * Cc].rearrange("p (a b) -> p a b", a=8), c.ones[0:Cc, 0:Cc], rhsG[0:Cc, hs, 0:Cc])
                for hh in range(2):
                    hs = slice(hh * 8, hh * 8 + 8)
                    hq = slice(hh * 4, hh * 4 + 4)
                    Bp = pb[1 + hh][:, 0:8 * Cc].rearrange("p (a b x) -> p a b x", a=4, b=2)
                    Gp = pb[3 + hh][:, 0:8 * Cc].rearrange("p (a b x) -> p a b x", a=4, b=2)
                    P.tt("dve", hp(kbT[:, hs, 0:Cc]), Bp, bcast(kT[:, hq, :], [128, 4, 2, Cc], 2), ALU.mult)
                    P.tt("dve", hp(qdT[:, hs, 0:Cc]), Gp, bcast(qT[:, hq, :], [128, 4, 2, Cc], 2), ALU.mult)
                if c.kp2 == 'B':
                    continue
                for hh in range(2):
                    hs = slice(hh * 8, hh * 8 + 8)
                    Gps = pb[5 + hh][0:Cc, 0:8 * Cc].rearrange("p (a b) -> p a b", a=8)
                    gbc = bcast(gam[0:Cc, hs], [Cc, 8, Cc], 2)
                    P.stt("dve", tA[0:Cc, hs, 0:Cc], Gps, -1.0, gbc, ALU.mult, ALU.add)
                    P.ts("pool", tA[0:Cc, hs, 0:Cc], tA[0:Cc, hs, 0:Cc], 0.0, ALU.min)
                    P.act(Dm[0:Cc, hs, 0:Cc], tA[0:Cc, hs, 0:Cc], AF.Exp)
                    P.tt("dve", tB[0:Cc, hs, 0:Cc], Gps, gbc, ALU.subtract)
                    P.ts("pool", tB[0:Cc, hs, 0:Cc], tB[0:Cc, hs, 0:Cc], 0.0, ALU.min)
                    P.act(DT[0:Cc, hs, 0:Cc], tB[0:Cc, hs, 0:Cc], AF.Exp)
                if c.kp2 == 'C':
                    continue
                for h in range(H):
                    hh, hl = h // 8, h % 8
                    P.mm(pb[1 + hh][0:Cc, hl * Cc:(hl + 1) * Cc], kbT[:, h, 0:Cc], kT[:, h // 2, :])
                    P.mm(pb[3 + hh][0:Cc, hl * Cc:(hl + 1) * Cc], kT[:, h // 2, :], kbT[:, h, 0:Cc])
                for hq_ in range(8):
                    P.mm(pb[0][0:Cc, hq_ * Cc:(hq_ + 1) * Cc], kT[:, hq_, :], qT[:, hq_, :])
                for hh in range(2):
                    hs = slice(hh * 8, hh * 8 + 8)
                    KKb = pb[1 + hh][0:Cc, 0:8 * Cc].rearrange("p (a b) -> p a b", a=8)
                    KKbT = pb[3 + hh][0:Cc, 0:8 * Cc].rearrange("p (a b) -> p a b", a=8)
                    P.stt("dve", tA[0:Cc, hs, 0:Cc], KKb, -1.0, bcast(c.Ls[0:Cc, 0:Cc], [Cc, 8, Cc], 1), ALU.mult, ALU.mult)
                    P.tt("pool", tA[0:Cc, hs, 0:Cc], tA[0:Cc, hs, 0:Cc], Dm[0:Cc, hs, 0:Cc], ALU.mult)
                    P.stt("dve", tB[0:Cc, hs, 0:Cc], KKbT, -1.0, bcast(c.Us[0:Cc, 0:Cc], [Cc, 8, Cc], 1), ALU.mult, ALU.mult)
                    P.tt("pool", tB[0:Cc, hs, 0:Cc], tB[0:Cc, hs, 0:Cc], DT[0:Cc, hs, 0:Cc], ALU.mult)
                    P.tt("pool", Rf[0:Cc, hs, 0:Cc], tB[0:Cc, hs, 0:Cc], bcast(c.ident[0:Cc, 0:Cc], [Cc, 8, Cc], 1), ALU.add)
                P.tt("pool", DT[0:Cc, :, 0:Cc], DT[0:Cc, :, 0:Cc], bcast(c.Ui[0:Cc, 0:Cc], [Cc, H, Cc], 1), ALU.mult)
                QKT = pb[0][0:Cc, 0:8 * Cc].rearrange("p (a x) -> p a x", a=8)
                P.tt("dve", hp(attnT[0:Cc, :, 0:Cc]), bcast(QKT, [Cc, 8, 2, Cc], 2), hp(DT[0:Cc, :, 0:Cc]), ALU.mult)
                if c.kp2 == 'D':
                    continue
                Xf = [tB, DT]
                Yf = [tA, Dm]
                cur = 0
                for l in range(1, L + 1):
                    nxt = 1 - cur
                    for h in range(H):
                        hh, hl = h // 8, h % 8
                        P.mm(pb[1 + hh][0:Cc, hl * Cc:(hl + 1) * Cc], Xf[cur][0:Cc, h, 0:Cc], Yf[cur][0:Cc, h, 0:Cc])
                    if l < L:
                        for h in range(H):
                            hh, hl = h // 8, h % 8
                            P.mm(pb[3 + hh][0:Cc, hl * Cc:(hl + 1) * Cc], Yf[cur][0:Cc, h, 0:Cc], Xf[cur][0:Cc, h, 0:Cc])
                    for hh in range(2):
                        hs = slice(hh * 8, hh * 8 + 8)
                        P.copy("act", Yf[nxt][0:Cc, hs, 0:Cc], pb[1 + hh][0:Cc, 0:8 * Cc].rearrange("p (a b) -> p a b", a=8))
                        if l < L:
                            P.copy("dve", Xf[nxt][0:Cc, hs, 0:Cc], pb[3 + hh][0:Cc, 0:8 * Cc].rearrange("p (a b) -> p a b", a=8))
                    for h in range(H):
                        hh, hl = h // 8, h % 8
                        P.mm(pb[5 + hh][0:Cc, hl * Cc:(hl + 1) * Cc], Yf[nxt][0:Cc, h, 0:Cc], Rf[0:Cc, h, 0:Cc])
                    for hh in range(2):
                        hs = slice(hh * 8, hh * 8 + 8)
                        P.tt("dve", Rf[0:Cc, hs, 0:Cc], pb[5 + hh][0:Cc, 0:8 * Cc].rearrange("p (a b) -> p a b", a=8), Rf[0:Cc, hs, 0:Cc], ALU.add)
                    cur = nxt
                P.copy("pool", Rb[0:Cc, :, 0:Cc], Rf[0:Cc, :, 0:Cc])
                if c.kp2 == 'E':
                    continue
                P.tt("pool", bv[0:Cc, :, :], vtm[0:Cc, n, :, :], bcast(beta, [Cc, H, 128], 2), ALU.mult)
                P.tt("pool", hp(kbg[0:Cc, :, :]), bcast(ktm[0:Cc, n, :, :], [Cc, 8, 2, 128], 2), hp(bcast(bG[0:Cc, :], [Cc, H, 128], 2)), ALU.mult)
                P.tt("pool", hp(kdec[0:Cc, :, :]), bcast(ktm[0:Cc, n, :, :], [Cc, 8, 2, 128], 2), hp(bcast(elm[0:Cc, :], [Cc, H, 128], 2)), ALU.mult)
                if c.kp2 == 'F1':
                    continue
                for h in range(H):
                    hh, hl = h // 8, h % 8
                    P.mm(pb[1 + hh][:, hl * Cc:(hl + 1) * Cc], kbg[0:Cc, h, :], Rb[0:Cc, h, 0:Cc])
                for hh in range(2):
                    hs = slice(hh * 8, hh * 8 + 8)
                    P.ts("dve", nwT[:, hs, 0:Cc], pb[1 + hh][:, 0:8 * Cc].rearrange("p (a b) -> p a b", a=8), -1.0, ALU.mult)
                if c.kp2 == 'F2':
                    continue
                for hg in range(4):
                    ps = pb[3 + hg % 2]
                    for hl in range(4):
                        h = hg * 4 + hl
                        P.mm(ps[0:Cc, hl * 128:(hl + 1) * 128], Rb[0:Cc, h, 0:Cc], bv[0:Cc, h, :], start=True, stop=False)
                        P.mm(ps[0:Cc, hl * 128:(hl + 1) * 128], nwT[:, h, 0:Cc], Sb[:, h, :], start=False, stop=True)
                    P.copy("act" if hg % 2 == 0 else "dve", vnew[0:Cc, hg * 4:hg * 4 + 4, :], ps[0:Cc, :].rearrange("p (a b) -> p a b", a=4))
                if c.kp2 == 'F3':
                    continue
                for h in range(H):
                    hh, hl = h // 8, h % 8
                    P.mm(pb[5 + hh][:, hl * Cc:(hl + 1) * Cc], Sb[:, h, :], qdT[:, h, 0:Cc], start=True, stop=False)
                    P.mm(pb[5 + hh][:, hl * Cc:(hl + 1) * Cc], vnew[0:Cc, h, :], attnT[0:Cc, h, 0:Cc], start=False, stop=True)
                if c.kp2 == 'F4':
                    continue
                for hg in range(4):
                    ps = pb[3 + hg % 2]
                    for hl in range(4):
                        h = hg * 4 + hl
                        P.mm(ps[:, hl * 128:(hl + 1) * 128], kdec[0:Cc, h, :], vnew[0:Cc, h, :])
                    for hl in range(4):
                        h = hg * 4 + hl
                        P.stt("dve", S[:, h, :], S[:, h, :], last[:, h:h + 1], ps[:, hl * 128:(hl + 1) * 128], ALU.mult, ALU.add)
                P.copy("pool", Sb[:, :, :], S[:, :, :])
                if c.kp2 == 'F':
                    continue
                for hh in range(2):
                    hs = slice(hh * 8, hh * 8 + 8)
                    oT = pb[5 + hh][:, 0:8 * Cc].rearrange("p (a b) -> p a b", a=8)
                    sqv = tA[:, hs, 0:Cc]
                    P.act(sqv, oT, AF.Square)
                    P.mm(pb[1 + hh][:, 0:8 * Cc].rearrange("p (a b) -> p a b", a=8), c.onesH[:], sqv)
                    rs = tB[:, hs, 0:Cc]
                    P.act(rs, pb[1 + hh][:, 0:8 * Cc].rearrange("p (a b) -> p a b", a=8), AF.Ln, bias=c.eps)
                    P.act(rs, rs, AF.Exp, scale=-0.5)
                    P.tt("dve", rs, oT, rs, ALU.mult)
                    P.stt("pool", ogT[:, hs, cs], rs, wn[:, 0:1], zs[:, hs, cs], ALU.mult, ALU.mult)
            for m in range(8):
                ps = pb[m % 2]
                for h in range(H):
                    P.mm(ps[:, 0:tb], wout[:, h, m * 128:(m + 1) * 128], ogT[:, h, 0:tb], start=(h == 0), stop=(h == H - 1))
                P.tt("dve", hb[:, m, 0:tb], hb[:, m, 0:tb], ps[:, 0:tb], ALU.add)
            P.dma("act", c.hTv[:, :, t0:t0 + tb], hb[:, :, 0:tb], out_reg=hreg(t0, t0 + tb))


def mlstm_layer(c, layer, j):
    mlstm_pass1(c, layer, j)
    if c.kstop == "p1":
        return
    mlstm_pass2(c, layer, j)


def mlstm_pass1(c, layer, j):
    P = c.P
    TB = 256
    c.WA.reset(0)
    c.FA.reset(0)
    win = c.WA.get([8, 3088])
    load_weight(c, win, c.mlstm_in[j], 8, 3088)
    hn = c.WA.get([8, TB])
    fm_st = c.WA.get([16, TB])
    ktm_st = c.WA.get([2, 512])
    vtm_st = c.WA.get([2, 1024])
    vb = [c.WA.get([TB]) for _ in range(2)]
    wcol = c.FA.get([8])
    load_vec_cols(c, wcol, c.attn_norm[layer], 8)
    gbias = c.FA.get([16])
    P.dma("sp", gbias, c.mlstm_gate_bias[j].partition_broadcast(128))
    hb = c.FA.get([8, TB])
    sq = [c.FA.get([TB]) for _ in range(2)]
    rstd = c.FA.get([TB])
    gb_st = c.FA.get([2, 16])
    gt = c.FA.get([16])
    for s in range(NS):
        for (o, tb) in seq_blocks(TB):
            t0 = s * TS + o
            nt = (tb + 127) // 128
            tp = min(tb, 128)
            P.dma("sp", hb[:, :, 0:tb], c.hTv[:, :, t0:t0 + tb], in_reg=hreg(t0, t0 + tb))
            rmsnorm_block(c, hb[:, :, 0:tb], tb, wcol, hn[:, :, 0:tb], sq, rstd, c.pb[6])
            for m in range(24):
                ps = c.pb[m % 2]
                for k in range(8):
                    P.mm(ps[:, 0:tb], win[:, k, m * 128:(m + 1) * 128], hn[:, k, 0:tb], start=(k == 0), stop=(k == 7))
                if m < 4:
                    P.copy("act", fm_st[:, m, 0:tb], ps[:, 0:tb])
                elif m < 8:
                    P.ts("dve", fm_st[:, m, 0:tb], ps[:, 0:tb], 0.125, ALU.mult)
                    for ti in range(nt):
                        pt = c.pT[0:tp, (m % 8) * 128:(m % 8) * 128 + 128]
                        P.transpose(pt, fm_st[:, m, ti * 128:ti * 128 + tp], c.identb[:])
                        P.copy("dve", ktm_st[0:tp, ti, (m - 4) * 128:(m - 3) * 128], pt)
                elif m < 16:
                    vbb = vb[m % 2]
                    P.copy("act", vbb[:, 0:tb], ps[:, 0:tb])
                    for ti in range(nt):
                        pt = c.pT[0:tp, (m % 8) * 128:(m % 8) * 128 + 128]
                        P.transpose(pt, vbb[:, ti * 128:ti * 128 + tp], c.identb[:])
                        P.copy("dve", vtm_st[0:tp, ti, (m - 8) * 128:(m - 7) * 128], pt)
                else:
                    P.act(fm_st[:, m - 8, 0:tb], ps[:, 0:tb], AF.Sigmoid)
            for ti in range(nt):
                ps = c.pb[4 + ti % 2]
                for k in range(8):
                    P.mm(ps[0:tp, 0:16], hn[:, k, ti * 128:ti * 128 + tp], win[:, k, 3072:3088], start=(k == 0), stop=(k == 7))
                P.tt("dve", gt[0:tp, :], ps[0:tp, 0:16], gbias[0:tp, :], ALU.add)
                P.act(gt[0:tp, :], gt[0:tp, :], AF.Tanh, scale=1.0 / 15.0)
                P.ts("dve", gb_st[0:tp, ti, 0:8], gt[0:tp, 0:8], 15.0, ALU.mult)
                P.act(gt[0:tp, 8:16], gt[0:tp, 8:16], AF.Exp, scale=-15.0)
                P.act(gt[0:tp, 8:16], gt[0:tp, 8:16], AF.Ln, bias=1.0)
                P.ts("dve", gb_st[0:tp, ti, 8:16], gt[0:tp, 8:16], -1.0, ALU.mult)
            P.dma("act", c.s_fmv[:, 0:16, t0:t0 + tb], fm_st[:, :, 0:tb], out_reg=sreg("s_fm", t0, t0 + tb))
            P.dma("sp", c.s_ktm[t0:t0 + tb, 0:512].rearrange("(n p) f -> p n f", p=tp), ktm_st[0:tp, 0:nt, :],
                  out_reg=sreg("s_ktm", t0, t0 + tb))
            P.dma("sp", c.s_vtm[t0:t0 + tb, 0:1024].rearrange("(n p) f -> p n f", p=tp), vtm_st[0:tp, 0:nt, :],
                  out_reg=sreg("s_vtm", t0, t0 + tb))
            P.dma("sp", c.s_gb[t0:t0 + tb, 0:16].rearrange("(n p) f -> p n f", p=tp), gb_st[0:tp, 0:nt, :],
                  out_reg=sreg("s_gb", t0, t0 + tb))


def mlstm_pass2(c, layer, j):
    P = c.P
    TB = 256
    H = 8
    c.WA.reset(0)
    c.FA.reset(0)
    wout = c.WA.get([8, 1024])
    load_weight(c, wout, c.mlstm_out[j], 8, 1024)
    fm = c.WA.get([16, TB])
    ktm = c.WA.get([4, 8, 64])
    v2 = c.WA.get([4, 8, 256])
    ogT = c.WA.get([8, TB])
    qdT = c.WA.get([4, 64])
    rhsB = c.WA.get([8, 64])
    kdecz = c.WA.get([8, 128])
    attnT = c.WA.get([8, 64])
    Sb = c.WA.get([4, 256])
    hb = c.FA.get([8, TB])
    S = c.FA.get([4, 256])
    gb = c.FA.get([4, 16])
    gam = c.FA.get([8])
    Gam = c.FA.get([8])
    last = c.FA.get([8])
    elm = c.FA.get([8])
    bj = c.FA.get([8])
    wn = c.FA.get([8])
    rhsG = c.FA.get([8, 64])
    DT = c.FA.get([8, 64])
    tA = c.FA.get([8, 64])
    tB = c.FA.get([8, 64])
    load_vec_cols(c, wn, c.mlstm_norm[j], 8)
    pb = c.pb
    P.memset("pool", v2[:, :, :, 128:256], 1.0)
    P.memset("pool", kdecz, 0.0)
    for s in range(NS):
        P.memset("pool", S, 0.0)
        P.memset("pool", Sb, 0.0)
        for (o, tb) in seq_blocks(TB)[:c.knblk]:
            t0 = s * TS + o
            Cc = min(tb, 64)
            nch = tb // Cc
            P.dma("sp", fm[:, :, 0:tb], c.s_fmv[:, 0:16, t0:t0 + tb], in_reg=sreg("s_fm", t0, t0 + tb))
            P.dma("act", ktm[0:Cc, 0:nch, :, :].rearrange("p n h d -> p n (h d)"),
                  c.s_ktm[t0:t0 + tb, 0:512].rearrange("(n p) f -> p n f", p=Cc), in_reg=sreg("s_ktm", t0, t0 + tb))
            for n in range(nch):
                P.dma("act", v2[0:Cc, n, :, 0:128],
                      c.s_vtm[t0 + n * Cc:t0 + (n + 1) * Cc, 0:1024].rearrange("p (h d) -> p h d", h=8),
                      in_reg=sreg("s_vtm", t0, t0 + tb))
            P.dma("act", gb[0:Cc, 0:nch, :], c.s_gb[t0:t0 + tb, 0:16].rearrange("(n p) f -> p n f", p=Cc),
                  in_reg=sreg("s_gb", t0, t0 + tb))
            P.dma("sp", hb[:, :, 0:tb], c.hTv[:, :, t0:t0 + tb], in_reg=hreg(t0, t0 + tb))
            for n in range(nch):
                cs = slice(n * Cc, (n + 1) * Cc)
                ipre = gb[0:Cc, n, 0:8]
                lf = gb[0:Cc, n, 8:16]
                P.mm(pb[0][0:Cc, 0:8], c.Ui[0:Cc, 0:Cc], lf)
                P.mm(pb[0][:, 8:16], c.ones[0:Cc, :], lf)
                P.copy("dve", gam[0:Cc, :], pb[0][0:Cc, 0:8])
                P.act(Gam[0:Cc, :], pb[0][0:Cc, 0:8], AF.Exp)
                P.act(last[:, :], pb[0][:, 8:16], AF.Exp)
                P.tt("dve", elm[0:Cc, :], pb[0][0:Cc, 8:16], gam[0:Cc, :], ALU.subtract)
                P.tt("dve", elm[0:Cc, :], elm[0:Cc, :], ipre, ALU.add)
                P.act(elm[0:Cc, :], elm[0:Cc, :], AF.Exp)
                P.tt("pool", rhsG[0:Cc, :, 0:Cc], bcast(c.Ui[0:Cc, 0:Cc], [Cc, H, Cc], 1), bcast(lf, [Cc, H, Cc], 2), ALU.mult)
                P.tt("pool", rhsB[0:Cc, :, 0:Cc], bcast(c.ident[0:Cc, 0:Cc], [Cc, H, Cc], 1), bcast(Gam[0:Cc, :], [Cc, H, Cc], 2), ALU.mult)
                P.mm(pb[1][:, 0:8 * Cc].rearrange("p (a b) -> p a b", a=8), c.onesb[0:Cc, :], rhsB[0:Cc, :, 0:Cc])
                P.mm(pb[2][0:Cc, 0:8 * Cc].rearrange("p (a b) -> p a b", a=8), c.ones[0:Cc, 0:Cc], rhsG[0:Cc, :, 0:Cc])
                Gm4 = pb[1][:, 0:8 * Cc].rearrange("p (a b x) -> p a b x", a=4, b=2)
                for b_ in range(2):
                    pr = slice(b_ * 64, (b_ + 1) * 64)
                    P.tt("dve", qdT[pr, :, 0:Cc], Gm4[pr, :, b_, :], fm[pr, 0:4, cs], ALU.mult)
                Gps = pb[2][0:Cc, 0:8 * Cc].rearrange("p (a b) -> p a b", a=8)
                P.tt("dve", tB[0:Cc, :, 0:Cc], Gps, bcast(gam[0:Cc, :], [Cc, H, Cc], 2), ALU.subtract)
                P.stt("dve", tB[0:Cc, :, 0:Cc], tB[0:Cc, :, 0:Cc], 0.0, bcast(ipre, [Cc, H, Cc], 2), ALU.min, ALU.add)
                P.act(DT[0:Cc, :, 0:Cc], tB[0:Cc, :, 0:Cc], AF.Exp)
                P.tt("pool", DT[0:Cc, :, 0:Cc], DT[0:Cc, :, 0:Cc], bcast(c.Ui[0:Cc, 0:Cc], [Cc, H, Cc], 1), ALU.mult)
                for h in range(H):
                    a_, b_ = h // 2, h % 2
                    pr = slice(b_ * 64, (b_ + 1) * 64)
                    P.mm(pb[3][0:Cc, h * Cc:(h + 1) * Cc], fm[pr, 4 + a_, cs], fm[pr, a_, cs])
                P.tt("dve", attnT[0:Cc, :, 0:Cc], pb[3][0:Cc, 0:8 * Cc].rearrange("p (a b) -> p a b", a=8), DT[0:Cc, :, 0:Cc], ALU.mult)
                for b_ in range(2):
                    kz = kdecz[0:Cc, :, :].rearrange("p (a b) x -> p a b x", b=2)[:, :, b_, b_ * 64:(b_ + 1) * 64]
                    kin = ktm[0:Cc, n, :, :].rearrange("p (a b) x -> p a b x", b=2)[:, :, b_, :]
                    ein = elm[0:Cc, :].rearrange("p (a b) -> p a b", b=2)[:, :, b_]
                    P.tt("pool", kz, kin, bcast(ein, [Cc, 4, 64], 2), ALU.mult)
                for h in range(H):
                    a_, b_ = h // 2, h % 2
                    pr = slice(b_ * 64, (b_ + 1) * 64)
                    P.mm(pb[4][:, h * Cc:(h + 1) * Cc], Sb[pr, a_, 0:128], qdT[pr, a_, 0:Cc], start=True, stop=False)
                    P.mm(pb[4][:, h * Cc:(h + 1) * Cc], v2[0:Cc, n, h, 0:128], attnT[0:Cc, h, 0:Cc], start=False, stop=True)
                    P.mm(pb[5][:, h * Cc:(h + 1) * Cc], Sb[pr, a_, 128:256], qdT[pr, a_, 0:Cc], start=True, stop=False)
                    P.mm(pb[5][:, h * Cc:(h + 1) * Cc], v2[0:Cc, n, h, 128:256], attnT[0:Cc, h, 0:Cc], start=False, stop=True)
                for a_ in range(4):
                    ps = pb[1 + a_ % 2]
                    P.mm(ps[:, 0:256], kdecz[0:Cc, 2 * a_, :], v2[0:Cc, n, 2 * a_, :], start=True, stop=False)
                    P.mm(ps[:, 0:256], kdecz[0:Cc, 2 * a_ + 1, :], v2[0:Cc, n, 2 * a_ + 1, :], start=False, stop=True)
                    for b_ in range(2):
                        pr = slice(b_ * 64, (b_ + 1) * 64)
                        h = 2 * a_ + b_
                        P.stt("dve", S[pr, a_, :], S[pr, a_, :], last[pr, h:h + 1], ps[pr, 0:256], ALU.mult, ALU.add)
                P.copy("pool", Sb[:, :, :], S[:, :, :])
                num = pb[4][:, 0:8 * Cc].rearrange("p (a b) -> p a b", a=8)
                den = pb[5][:, 0:8 * Cc].rearrange("p (a b) -> p a b", a=8)
                dn = tA[:, :, 0:Cc]
                P.act(dn, den, AF.Abs)
                P.ts("pool", dn, dn, 1.0, ALU.max)
                P.recip(dn, dn)
                hh_ = tB[:, :, 0:Cc]
                P.tt("dve", hh_, num, dn, ALU.mult)
                P.act(dn, hh_, AF.Square)
                P.mm(pb[6][:, 0:8 * Cc].rearrange("p (a b) -> p a b", a=8), c.onesH[:], dn)
                P.act(dn, pb[6][:, 0:8 * Cc].rearrange("p (a b) -> p a b", a=8), AF.Ln, bias=c.eps)
                P.act(dn, dn, AF.Exp, scale=-0.5)
                P.tt("pool", hh_, hh_, dn, ALU.mult)
                P.tt("pool", hh_, hh_, bcast(wn[:, :], [128, H, Cc], 2), ALU.mult)
                P.tt("pool", ogT[:, :, cs], hh_, fm[:, 8:16, cs], ALU.mult)
            for m in range(8):
                ps = pb[m % 2]
                for h in range(H):
                    P.mm(ps[:, 0:tb], wout[:, h, m * 128:(m + 1) * 128], ogT[:, h, 0:tb], start=(h == 0), stop=(h == H - 1))
                P.tt("dve", hb[:, m, 0:tb], hb[:, m, 0:tb], ps[:, 0:tb], ALU.add)
            P.dma("act", c.hTv[:, :, t0:t0 + tb], hb[:, :, 0:tb], out_reg=hreg(t0, t0 + tb))


def mla_layer(c, layer, j):
    mla_pass1(c, layer, j)
    if c.kstop == "p1":
        return
    mla_pass2(c, layer, j)


def mla_pass1(c, layer, j):
    import math
    P = c.P
    TB = 256
    c.WA.reset(0)
    c.FA.reset(0)
    win = c.WA.get([8, 672])
    wuq = c.WA.get([3, 1536])
    wukv = c.WA.get([2, 2048])
    wkr = c.WA.get([8, 96])
    load_weight(c, win, c.mla_in[j], 8, 672)
    load_weight(c, wuq, c.mla_uq[j], 3, 1536)
    load_weight(c, wukv, c.mla_ukv[j], 2, 2048)
    P.memset("pool", wkr, 0.0)
    P.copy("pool", wkr[:, :, 64:96], win[:, :, 640:672])
    hn = c.WA.get([8, TB])
    qlat = c.WA.get([3, TB])
    kvlat = c.WA.get([2, TB])
    qk_st = c.WA.get([32, TB])
    vtm_st = c.WA.get([2, 16, 128])
    wcol = c.FA.get([8])
    load_vec_cols(c, wcol, c.attn_norm[layer], 8)
    qn_w = c.FA.get([3])
    load_vec_cols(c, qn_w, c.mla_q_norm[j], 3)
    kvn_w = c.FA.get([2])
    load_vec_cols(c, kvn_w, c.mla_kv_norm[j], 2)
    hw = c.FA.get([2])
    P.dma("sp", hw[0:96, 0:1], c.mla_q_head_norm[j].rearrange("(p o) -> p o", o=1))
    P.dma("sp", hw[0:96, 1:2], c.mla_k_head_norm[j].rearrange("(p o) -> p o", o=1))
    P.ts("dve", hw[0:96, 0:1], hw[0:96, 0:1], 96.0 ** -0.5, ALU.mult)
    ones384 = c.FA.get([128])
    ones256 = c.FA.get([128])
    ones96 = c.FA.get([128])
    P.memset("pool", ones384, 1.0 / 384.0)
    P.memset("pool", ones256, 1.0 / 256.0)
    P.memset("pool", ones96, 1.0 / 96.0)
    Rm = c.FA.get([128])
    P.memset("pool", Rm, 0.0)
    P.copy("pool", Rm[64:96, 80:96], c.ident[64:96, 64:80])
    P.copy("pool", Rm[64:96, 64:80], c.ident[64:96, 80:96])
    P.ts("pool", Rm[64:96, 64:80], Rm[64:96, 64:80], -1.0, ALU.mult)
    Cf = c.FA.get([TS])
    Sf = c.FA.get([TS])
    fa_mark = c.FA.off
    pos_i = c.FA.get([TS]).bitcast(mybir.dt.int32)
    posf = c.FA.get([TS])
    pidx_i = c.FA.get([1]).bitcast(mybir.dt.int32)
    pf = c.FA.get([4])
    P.op("pool", lambda h: h.iota(pos_i[:, 0:TX], [[1, TX]], base=NM, channel_multiplier=0), writes=[pos_i[:, 0:TX]])
    P.op("pool", lambda h: h.iota(pos_i[:, TX:TS], [[1, NM]], base=0, channel_multiplier=0), writes=[pos_i[:, TX:TS]])
    P.op("pool", lambda h: h.iota(pidx_i[:, 0:1], [[1, 1]], base=0, channel_multiplier=1), writes=[pidx_i[:, 0:1]])
    P.copy("dve", posf, pos_i)
    P.copy("dve", pf[:, 0:1], pidx_i[:, 0:1])
    P.ts("dve", pf[:, 1:2], pf[:, 0:1], 80.0, ALU.is_ge, -16.0, ALU.mult)
    P.tt("dve", pf[:, 1:2], pf[:, 1:2], pf[:, 0:1], ALU.add)
    P.ts("dve", pf[:, 1:2], pf[:, 1:2], -64.0, ALU.add)
    P.act(pf[:, 2:3], pf[:, 1:2], AF.Exp, scale=-math.log(10000.0) / 16.0)
    P.ts("dve", pf[:, 2:3], pf[:, 2:3], 1.0 / (2.0 * math.pi), ALU.mult)
    P.ts("dve", posf, posf, pf[:, 2:3], ALU.mult)
    P.copy("dve", pos_i, posf)
    P.copy("dve", Cf, pos_i)
    P.tt("dve", posf, posf, Cf, ALU.subtract)
    P.act(Sf, posf, AF.Sin, scale=math.pi)
    P.act(Cf, posf, AF.Sin, scale=math.pi / 2.0)
    P.tt("dve", Cf, Cf, Cf, ALU.mult)
    P.ts("dve", Cf, Cf, -2.0, ALU.mult, 1.0, ALU.add)
    P.tt("dve", posf, Sf, Cf, ALU.mult)
    P.ts("dve", posf, posf, 2.0, ALU.mult)
    P.tt("dve", Cf, Sf, Sf, ALU.mult)
    P.ts("dve", Cf, Cf, -2.0, ALU.mult, 1.0, ALU.add)
    P.copy("dve", Sf, posf)
    P.memset("pool", Cf[0:64, :], 1.0)
    P.memset("pool", Sf[0:64, :], 0.0)
    c.FA.off = fa_mark
    hb = c.FA.get([8, TB])
    sq = [c.FA.get([TB]) for _ in range(2)]
    rstd = c.FA.get([TB])
    lat = c.FA.get([3, TB])
    kr = c.FA.get([TB])
    kp = c.FA.get([TB])
    qn = c.FA.get([TB])
    t1 = c.FA.get([TB])
    QS = 96.0 ** -0.5
    wv = wukv[:, :, :].rearrange("p k (h x) -> p k h x", x=128)
    P.memset("pool", vtm_st[:, :, :, 64:128], 1.0)

    def head_norm_rope(src_ps, wi, dst, cols, tb, scale):
        P.act(sq[0][0:96, 0:tb], src_ps, AF.Square)
        P.mm(c.pb[5][0:96, 0:tb], ones96[0:96, 0:96], sq[0][0:96, 0:tb])
        P.act(rstd[0:96, 0:tb], c.pb[5][0:96, 0:tb], AF.Ln, bias=c.eps[0:96, :])
        P.act(rstd[0:96, 0:tb], rstd[0:96, 0:tb], AF.Exp, scale=-0.5)
        P.stt("dve", qn[0:96, 0:tb], src_ps, hw[0:96, wi:wi + 1], rstd[0:96, 0:tb], ALU.mult, ALU.mult)
        P.mm(c.pb[4][0:96, 0:tb], Rm[0:96, 0:96], qn[0:96, 0:tb])
        P.tt("dve", t1[0:96, 0:tb], c.pb[4][0:96, 0:tb], Sf[0:96, cols], ALU.mult)
        P.tt("pool", qn[0:96, 0:tb], qn[0:96, 0:tb], Cf[0:96, cols], ALU.mult)
        P.tt("dve", dst, qn[0:96, 0:tb], t1[0:96, 0:tb], ALU.add)

    for s in range(NS):
        for (o, tb) in seq_blocks(TB):
            t0 = s * TS + o
            nt = (tb + 127) // 128
            tp = min(tb, 128)
            cols = slice(o, o + tb)
            P.dma("sp", hb[:, :, 0:tb], c.hTv[:, :, t0:t0 + tb], in_reg=hreg(t0, t0 + tb))
            rmsnorm_block(c, hb[:, :, 0:tb], tb, wcol, hn[:, :, 0:tb], sq, rstd, c.pb[6])
            for m in range(5):
                ps = c.pb[m % 2]
                for k in range(8):
                    P.mm(ps[:, 0:tb], win[:, k, m * 128:(m + 1) * 128], hn[:, k, 0:tb], start=(k == 0), stop=(k == 7))
                P.copy("act", lat[:, m % 3, 0:tb], ps[:, 0:tb])
                if m == 2:
                    rmsnorm_block(c, lat[:, 0:3, 0:tb], tb, qn_w, qlat[:, :, 0:tb], sq, rstd, c.pb[6], nk=3, ones=ones384)
                if m == 4:
                    rmsnorm_block(c, lat[:, 0:2, 0:tb], tb, kvn_w, kvlat[:, :, 0:tb], sq, rstd, c.pb[6], nk=2, ones=ones256)
            ps = c.pb[2]
            for k in range(8):
                P.mm(ps[0:96, 0:tb], wkr[:, k, :], hn[:, k, 0:tb], start=(k == 0), stop=(k == 7))
            P.copy("act", kr[64:96, 0:tb], ps[64:96, 0:tb])
            for h in range(16):
                ps = c.pb[h % 2]
                for k in range(3):
                    P.mm(ps[0:96, 0:tb], wuq[:, k, h * 96:(h + 1) * 96], qlat[:, k, 0:tb], start=(k == 0), stop=(k == 2))
                head_norm_rope(ps[0:96, 0:tb], 0, qk_st[0:96, h, 0:tb], cols, tb, QS)
                ps2 = c.pb[2 + h % 2]
                for k in range(2):
                    P.mm(ps2[0:64, 0:tb], wukv[:, k, h * 128:h * 128 + 64], kvlat[:, k, 0:tb], start=(k == 0), stop=(k == 1))
                P.copy("act", kp[0:64, 0:tb], ps2[0:64, 0:tb])
                P.copy("pool", kp[64:96, 0:tb], kr[64:96, 0:tb])
                head_norm_rope(kp[0:96, 0:tb], 1, qk_st[0:96, 16 + h, 0:tb], cols, tb, 1.0)
            for ti in range(nt):
                for g in range(2):
                    ps = c.pb[g]
                    for k in range(2):
                        P.mm(ps[0:tp, :].rearrange("p (h x) -> p h x", h=8), kvlat[:, k, ti * 128:ti * 128 + tp],
                             wv[:, k, g * 8:(g + 1) * 8, 64:128], start=(k == 0), stop=(k == 1))
                    P.copy("act" if g == 0 else "dve", vtm_st[0:tp, ti, g * 8:(g + 1) * 8, 0:64],
                           ps[0:tp, :].rearrange("p (h x) -> p h x", h=8))
            P.dma("act", c.s_fmv[0:96, :, t0:t0 + tb], qk_st[0:96, :, 0:tb], out_reg=sreg("s_fm", t0, t0 + tb))
            P.dma("sp", c.s_vtm[t0:t0 + tb, :].rearrange("(n p) f -> p n f", p=tp),
                  vtm_st[0:tp, 0:nt, :, :].rearrange("p n h d -> p n (h d)"), out_reg=sreg("s_vtm", t0, t0 + tb))


def mla_pass2(c, layer, j):
    P = c.P
    c.WA.reset(0)
    c.FA.reset(0)
    wout = c.WA.get([16, 1024])
    P = c.P
    sv = c.mla_out[j].rearrange("(h p) f -> p h f", p=64)
    for h0 in range(0, 16, 4):
        P.dma("pool", wout[0:64, h0:h0 + 4, :], sv[:, h0:h0 + 4, :])
    og = c.WA.get([16, TS])
    kTs = [c.WA.get([TS]) for _ in range(2)]
    qTs = [c.WA.get([TS]) for _ in range(2)]
    vxs = [c.WA.get([16, 128]) for _ in range(2)]
    vms = [c.WA.get([128]) for _ in range(2)]
    pts = [c.WA.get([512]) for _ in range(3)]
    Uib = c.WA.get([128])
    P.copy("dve", Uib, c.Ui[:])
    rr = c.FA.get([512])
    cp = c.FA.get([512])
    hb = c.FA.get([8, 512])
    pb = c.pb
    npt = 0
    for s in range(NS):
        sb = s * TS
        for h in range(16):
            kT, qT, vx, vm = kTs[h % 2], qTs[h % 2], vxs[h % 2], vms[h % 2]
            P.dma("sp", qT[0:96, :], c.s_fmv[0:96, h, sb:sb + TS], in_reg=sreg("s_fm", sb, sb + TS))
            P.dma("sp", kT[0:96, :], c.s_fmv[0:96, 16 + h, sb:sb + TS], in_reg=sreg("s_fm", sb, sb + TS))
            P.dma("act", vx, c.s_vtm[sb:sb + TX, h * 128:(h + 1) * 128].rearrange("(n p) f -> p n f", p=128),
                  in_reg=sreg("s_vtm", sb, sb + TS))
            P.dma("act", vm[0:NM, :], c.s_vtm[sb + TX:sb + TS, h * 128:(h + 1) * 128], in_reg=sreg("s_vtm", sb, sb + TS))
            qblocks = [(TX, NM)] + [(qb * 512, 512) for qb in range(4)]
            for bi, (q0, nq) in enumerate(qblocks):
                acc = pb[4 + bi % 2]
                st = pb[npt % 4]
                pt = pts[npt % 3]
                npt += 1
                P.mm(st[0:NM, 0:nq], kT[0:96, TX:TS], qT[0:96, q0:q0 + nq])
                P.act(pt[0:NM, 0:nq], st[0:NM, 0:nq], AF.Exp)
                if q0 == TX:
                    P.tt("pool", pt[0:NM, 0:NM], pt[0:NM, 0:NM], Uib[0:NM, 0:NM], ALU.mult)
                P.mm(acc[:, 0:nq], vm[0:NM, :], pt[0:NM, 0:nq], start=True, stop=(q0 == TX))
                if q0 != TX:
                    nkt = q0 // 128 + 4
                    for kt in range(nkt):
                        jd = kt - q0 // 128
                        c0 = 128 * jd if jd > 0 else 0
                        st = pb[npt % 4]
                        pt = pts[npt % 3]
                        npt += 1
                        P.mm(st[:, c0:nq], kT[0:96, kt * 128:(kt + 1) * 128], qT[0:96, q0 + c0:q0 + nq])
                        P.act(pt[:, c0:nq], st[:, c0:nq], AF.Exp)
                        if jd >= 0:
                            P.tt("pool", pt[:, c0:c0 + 128], pt[:, c0:c0 + 128], Uib, ALU.mult)
                        P.mm(acc[:, c0:nq], vx[:, kt, :], pt[:, c0:nq], start=False, stop=(kt == nkt - 1))
                P.recip(rr[64:128, 0:nq], acc[64:128, 0:nq])
                P.memset("pool", rr[0:64, 0:nq], 0.0)
                st = pb[npt % 4]
                npt += 1
                P.mm(st[0:64, 0:nq], c.ident[:, 64:128], rr[:, 0:nq])
                P.copy("act", cp[0:64, 0:nq], st[0:64, 0:nq])
                P.tt("dve", og[0:64, h, q0:q0 + nq], acc[0:64, 0:nq], cp[0:64, 0:nq], ALU.mult)
        for (q0, nq) in [(TX, NM)] + [(qb * 512, 512) for qb in range(4)]:
            t0 = sb + q0
            P.dma("sp", hb[:, :, 0:nq], c.hTv[:, :, t0:t0 + nq], in_reg=hreg(t0, t0 + nq))
            for m in range(8):
                ps = pb[6] if m % 2 == 0 else pb[0]
                for h in range(16):
                    P.mm(ps[:, 0:nq], wout[0:64, h, m * 128:(m + 1) * 128], og[0:64, h, q0:q0 + nq], start=(h == 0), stop=(h == 15))
                P.tt("dve", hb[:, m, 0:nq], hb[:, m, 0:nq], ps[:, 0:nq], ALU.add)
            P.dma("act", c.hTv[:, :, t0:t0 + nq], hb[:, :, 0:nq], out_reg=hreg(t0, t0 + nq))


_NC_CACHE = {}
W_NAMES = ["meta_tokens", "attn_norm", "ffn_norm", "ff_up", "ff_down", "gdn_in", "gdn_conv", "gdn_a_log",
           "gdn_dt_bias", "gdn_norm", "gdn_out", "mlstm_in", "mlstm_gate_bias", "mlstm_norm", "mlstm_out",
           "mla_in", "mla_q_norm", "mla_uq", "mla_kv_norm", "mla_ukv", "mla_q_head_norm", "mla_k_head_norm",
           "mla_out"]


def kernel(**inputs):
    nc = build_program()
    x = np.ascontiguousarray(np.asarray(inputs["x"], dtype=np.float32))
    shared = {k: np.ascontiguousarray(np.asarray(inputs[k], dtype=np.float32)) for k in W_NAMES}
    in_maps = []
    for core in range(8):
        m = dict(shared)
        m["x"] = x[core * NS:(core + 1) * NS]
        in_maps.append(m)
    res = run_bass_kernel_spmd(nc, in_maps, core_ids=list(range(8)))
    return np.concatenate([r["y"] for r in res.results], axis=0)
```

```python
import contextlib
import numpy as np
import concourse.bass as bass
import concourse.mybir as mybir

F32 = mybir.dt.float32
BF16 = mybir.dt.bfloat16
AF = mybir.ActivationFunctionType
ALU = mybir.AluOpType
AX = mybir.AxisListType

SEM_LIMIT = 30000
_ISZ = {}


def _isz(dt):
    k = str(dt)
    if k not in _ISZ:
        _ISZ[k] = {"dt.float32": 4, "dt.bfloat16": 2, "dt.int32": 4, "dt.uint32": 4,
                   "dt.float16": 2, "dt.uint8": 1, "dt.int8": 1, "dt.uint16": 2,
                   "dt.int16": 2}.get(k, 4)
    return _ISZ[k]


class SemCounter:
    def __init__(self, prog, name):
        self.prog = prog
        self.name = name
        self.retired = []
        self.sem = None
        self.val = 0
        self.n = 0

    def _new(self):
        self.sem = self.prog.new_sem(f"{self.name}_{self.n}")
        self.n += 1
        self.val = 0

    def next(self, inc):
        if self.sem is None:
            self._new()
        elif self.val + inc > SEM_LIMIT:
            self.retired.append((self.sem, self.val))
            self._new()
        self.val += inc
        return self.sem, self.val


class Ev:
    __slots__ = ("sem", "val", "clock", "eng", "dma_ctr")

    def __init__(self, sem, val, clock, eng, dma_ctr=None):
        self.sem = sem
        self.val = val
        self.clock = clock
        self.eng = eng
        self.dma_ctr = dma_ctr


class Eng:
    def __init__(self, prog, name):
        self.name = name
        self.ops = []
        self.clock = {}
        self.ctr = SemCounter(prog, "s" + name)


def region(ap):
    t = ap.tensor
    name = t.name
    pat = ap.ap
    isz = _isz(ap.dtype)
    space = str(ap.space) if hasattr(ap, "space") else ""
    off = ap.offset
    if "DRAM" in space.upper() or "HBM" in space.upper():
        lo = off
        hi = off
        for st, cnt in pat:
            if st >= 0:
                hi += st * (cnt - 1)
            else:
                lo += st * (cnt - 1)
        return (name, 0, 1, lo * isz, (hi + 1) * isz)
    if "PSUM" in space.upper():
        return (name, 0, 128, 0, 1 << 30, "psum")
    pstride, pcnt = pat[0]
    if pstride == 0:
        pstride = 1 << 40
    p0 = ap.start_partition()
    fo = off - p0 * pstride if pstride < (1 << 40) else off
    lo = fo
    hi = fo
    for st, cnt in pat[1:]:
        if st >= 0:
            hi += st * (cnt - 1)
        else:
            lo += st * (cnt - 1)
    return (name, p0, p0 + pcnt, lo * isz, (hi + 1) * isz)


class Prog:
    def __init__(self, nc, same_engine_sync=True):
        self.nc = nc
        self.stack = contextlib.ExitStack()
        self.engs = {n: Eng(self, n) for n in ("pe", "act", "dve", "pool", "sp")}
        self.recs = {}
        self.dma_ctrs = {}
        self.same_engine_sync = same_engine_sync
        self.nsem = 0
        self.n_ops = 0
        self.pe_last = {}

    def new_sem(self, name):
        self.nsem += 1
        return self.stack.enter_context(self.nc.semaphore(name))

    def sbuf(self, name, shape, dt):
        return self.stack.enter_context(self.nc.sbuf_tensor(name, list(shape), dt))

    def psum(self, name, shape, dt=F32):
        return self.stack.enter_context(self.nc.psum_tensor(name, list(shape), dt))

    def _conflicts(self, reg, is_write, eng=None):
        name, p0, p1, f0, f1 = reg[:5]
        out = []
        if len(reg) > 5:
            for r in self.recs.get(name, ()):
                if r[5].eng != eng:
                    out.append((r[5], True))
                elif is_write or r[4]:
                    out.append((r[5], r[4]))
            return out
        for r in self.recs.get(name, ()):
            if r[0] < p1 and p0 < r[1] and r[2] < f1 and f0 < r[3]:
                if is_write or r[4]:
                    out.append((r[5], r[4]))
        return out

    def _record(self, reg, is_write, ev):
        name, p0, p1, f0, f1 = reg[:5]
        lst = self.recs.setdefault(name, [])
        if len(reg) > 5:
            lst[:] = [r for r in lst if not (r[5].eng == ev.eng and r[4] == is_write)]
            lst.append([p0, p1, f0, f1, is_write, ev])
            return
        if is_write:
            lst[:] = [r for r in lst if not (p0 <= r[0] and r[1] <= p1 and f0 <= r[2] and r[3] <= f1)]
        else:
            lst[:] = [r for r in lst if not ((not r[4]) and r[5].eng == ev.eng and r[5].dma_ctr is None
                                             and ev.dma_ctr is None
                                             and r[0] == p0 and r[1] == p1 and r[2] == f0 and r[3] == f1)]
        lst.append([p0, p1, f0, f1, is_write, ev])

    def _gather_waits(self, eng, reads, writes):
        e = self.engs[eng]
        need = {}
        deps = []
        for reg in reads:
            for ev, was_write in self._conflicts(reg, False, eng):
                deps.append((ev, True))
        for reg in writes:
            for ev, was_write in self._conflicts(reg, True, eng):
                deps.append((ev, was_write))
        for ev, hard in deps:
            if ev.dma_ctr is None and ev.eng == eng:
                if eng in ("pe", "sp"):
                    continue
                if not hard or not self.same_engine_sync:
                    continue
            pairs = []
            if ev.dma_ctr is not None:
                c = ev.dma_ctr
                pairs.extend(c.retired)
                pairs.append((c.sem, c.val))
            else:
                pairs.append((ev.sem, ev.val))
            for sem, val in pairs:
                if e.clock.get(id(sem), 0) >= val:
                    continue
                k = id(sem)
                if k not in need or need[k][1] < val:
                    need[k] = (sem, val)
            for k, v in ev.clock.items():
                if e.clock.get(k, 0) < v:
                    e.clock[k] = v
        waits = list(need.values())
        for sem, val in waits:
            if e.clock.get(id(sem), 0) < val:
                e.clock[id(sem)] = val
        return waits

    def op(self, eng, fn, reads=(), writes=(), extra_reads=(), extra_writes=()):
        e = self.engs[eng]
        rr = [region(a) for a in reads] + list(extra_reads)
        ww = [region(a) for a in writes] + list(extra_writes)
        waits = self._gather_waits(eng, rr, ww)
        sem, val = e.ctr.next(1)
        clock = dict(e.clock)
        clock[id(sem)] = val
        ev = Ev(sem, val, clock, eng)
        e.ops.append((waits, fn, sem, 1))
        for reg in rr:
            self._record(reg, False, ev)
        for reg in ww:
            self._record(reg, True, ev)
        self.n_ops += 1
        return ev

    def dma(self, q, out, in_, extra_reads=(), extra_writes=(), sem_key=None, in_reg=None, out_reg=None, **kw):
        e = self.engs[q]
        rr = [in_reg if in_reg is not None else region(in_)] + list(extra_reads)
        ww = [out_reg if out_reg is not None else region(out)] + list(extra_writes)
        waits = self._gather_waits(q, rr, ww)
        key = sem_key or out.tensor.name
        if key not in self.dma_ctrs:
            self.dma_ctrs[key] = SemCounter(self, "d" + key[:20])
        c = self.dma_ctrs[key]
        sem, val = c.next(16)
        clock = dict(e.clock)
        ev = Ev(sem, val, clock, q, dma_ctr=c)

        def fn(engh, out=out, in_=in_, kw=kw):
            return engh.dma_start(out=out, in_=in_, **kw)
        e.ops.append((waits, fn, sem, 16))
        for reg in rr:
            self._record(reg, False, ev)
        for reg in ww:
            self._record(reg, True, ev)
        self.n_ops += 1
        return ev

    def final_wait(self, eng="sp"):
        e = self.engs[eng]
        waits = []
        for c in self.dma_ctrs.values():
            for sem, val in c.retired + [(c.sem, c.val)]:
                if e.clock.get(id(sem), 0) < val:
                    waits.append((sem, val))
                    e.clock[id(sem)] = val
        for n, o in self.engs.items():
            if o.ctr.sem is not None and n != eng:
                for sem, val in o.ctr.retired + [(o.ctr.sem, o.ctr.val)]:
                    if e.clock.get(id(sem), 0) < val:
                        waits.append((sem, val))
                        e.clock[id(sem)] = val
        e.ops.append((waits, None, None, 0))

    def emit(self):
        nc = self.nc
        with nc.Block() as block:
            def run(name):
                def body(h):
                    for waits, fn, sem, inc in self.engs[name].ops:
                        for s, v in waits:
                            h.wait_ge(s, v)
                        if fn is not None:
                            ins = fn(h)
                            ins.then_inc(sem, inc)
                return body
            if self.engs["sp"].ops:
                block.sync(run("sp"))
            if self.engs["act"].ops:
                block.scalar(run("act"))
            if self.engs["dve"].ops:
                block.vector(run("dve"))
            if self.engs["pool"].ops:
                block.gpsimd(run("pool"))
            if self.engs["pe"].ops:
                block.tensor(run("pe"))
        self.stack.close()

    def mm(self, out, lhsT, rhs, start=True, stop=True, **kw):
        rd = [lhsT, rhs]
        bank = out.tensor.name
        lr = region(lhsT)
        g0, g1 = lr[1] // 32, (lr[2] + 31) // 32
        prev = self.pe_last.get(bank)
        e = self.engs["pe"]
        extra = []
        if prev is not None and (prev[1] <= g0 or g1 <= prev[0]):
            sem, val = prev[2].sem, prev[2].val
            if e.clock.get(id(sem), 0) < val:
                extra.append((sem, val))
                e.clock[id(sem)] = val
        ev = self.op("pe", lambda h: h.matmul(out, lhsT, rhs, start=start, stop=stop, **kw),
                     reads=rd if start else rd + [out], writes=[out])
        if extra:
            w, fn, sm, inc = e.ops[-1]
            e.ops[-1] = (list(w) + extra, fn, sm, inc)
        self.pe_last[bank] = (g0, g1, ev)
        return ev

    def transpose(self, out, in_, ident):
        return self.op("pe", lambda h: h.transpose(out, in_, ident), reads=[in_, ident], writes=[out])

    def act(self, out, in_, func, bias=None, scale=None, accum_out=None, eng="act"):
        kw = {}
        rd = [in_]
        wr = [out]
        if bias is not None:
            kw["bias"] = bias
            if not isinstance(bias, (int, float)):
                rd.append(bias)
        if scale is not None:
            kw["scale"] = scale
            if not isinstance(scale, (int, float)):
                rd.append(scale)
        if accum_out is not None:
            kw["accum_out"] = accum_out
            wr.append(accum_out)
        return self.op("act", lambda h: h.activation(out, in_, func, **kw), reads=rd, writes=wr)

    def tt(self, eng, out, in0, in1, op):
        return self.op(eng, lambda h: h.tensor_tensor(out, in0, in1, op), reads=[in0, in1], writes=[out])

    def ts(self, eng, out, in0, s1, op0, s2=None, op1=None, accum_out=None):
        rd = [in0]
        if not isinstance(s1, (int, float)):
            rd.append(s1)
        if s2 is not None and not isinstance(s2, (int, float)):
            rd.append(s2)
        wr = [out]
        kw = {}
        if s2 is None:
            s2 = 0.0
            op1 = ALU.add
        if op1 is not None:
            kw["op1"] = op1
        if accum_out is not None:
            kw["accum_out"] = accum_out
            wr.append(accum_out)
        return self.op(eng, lambda h: h.tensor_scalar(out, in0, s1, s2, op0, **kw), reads=rd, writes=wr)

    def stt(self, eng, out, in0, scalar, in1, op0, op1):
        rd = [in0, in1]
        if not isinstance(scalar, (int, float)):
            rd.append(scalar)
        return self.op("dve", lambda h: h.scalar_tensor_tensor(out, in0, scalar, in1, op0, op1),
                       reads=rd, writes=[out])

    def copy(self, eng, out, in_):
        if eng == "act":
            return self.op("act", lambda h: h.copy(out, in_), reads=[in_], writes=[out])
        return self.op(eng, lambda h: h.tensor_copy(out, in_), reads=[in_], writes=[out])

    def memset(self, eng, out, val):
        return self.op(eng, lambda h: h.memset(out, val), reads=[], writes=[out])

    def recip(self, out, in_):
        return self.op("dve", lambda h: h.reciprocal(out, in_), reads=[in_], writes=[out])

    def reduce(self, eng, out, in_, op, axis=None):
        axis = axis or AX.X
        return self.op(eng, lambda h: h.tensor_reduce(out, in_, axis, op), reads=[in_], writes=[out])

from concourse.bass_utils import run_bass_kernel_spmd

NS = 2
TX = 2048
NM = 16
TS = TX + NM
NTOK = NS * TS
D = 1024
KC = 8
EPS = 1e-6
WA_N = 66304
FA_N = 11800


def bcast(ap, shape, axis):
    return ap.unsqueeze(axis).broadcast_to(list(shape))


class Arena:
    def __init__(self, t, n, base=0):
        self.t = t
        self.n = n
        self.base = base
        self.off = base

    def reset(self, base=None):
        if base is not None:
            self.base = base
        self.off = self.base

    def get(self, shape, parts=128):
        size = 1
        for s in shape:
            size *= s
        assert self.off + size <= self.n, (self.off, size, self.n)
        a = self.t[0:parts, self.off:self.off + size]
        self.off += size
        if len(shape) == 2:
            a = a.rearrange("p (a b) -> p a b", a=shape[0])
        elif len(shape) == 3:
            a = a.rearrange("p (a b c) -> p a b c", a=shape[0], b=shape[1])
        elif len(shape) == 4:
            a = a.rearrange("p (a b c d) -> p a b c d", a=shape[0], b=shape[1], c=shape[2])
        return a


class C:
    pass


def seq_blocks(TB):
    out = [(TX, NM)]
    for b in range(TX // TB):
        out.append((b * TB, TB))
    return out


def build_program(n_layers=4, debug=False, stop_after=None):
    nc = bass.Bass("TRN2", target_bir_lowering=False)
    P = Prog(nc)
    c = C()
    c.P = P
    c.nc = nc
    dt = nc.dram_tensor

    def din(name, shape):
        return dt(name, list(shape), F32, kind="ExternalInput").ap()

    c.x = din("x", [NS, TX, D])
    c.meta = din("meta_tokens", [NM, D])
    c.attn_norm = din("attn_norm", [4, D])
    c.ffn_norm = din("ffn_norm", [4, D])
    c.ff_up = din("ff_up", [4, D, 4096])
    c.ff_down = din("ff_down", [4, 4096, D])
    c.gdn_in = din("gdn_in", [2, D, 6176])
    c.gdn_conv = din("gdn_conv", [2, 4, 4096])
    c.gdn_a_log = din("gdn_a_log", [2, 16])
    c.gdn_dt_bias = din("gdn_dt_bias", [2, 16])
    c.gdn_norm = din("gdn_norm", [2, 128])
    c.gdn_out = din("gdn_out", [2, 2048, D])
    c.mlstm_in = din("mlstm_in", [1, D, 3088])
    c.mlstm_gate_bias = din("mlstm_gate_bias", [1, 16])
    c.mlstm_norm = din("mlstm_norm", [1, D])
    c.mlstm_out = din("mlstm_out", [1, D, D])
    c.mla_in = din("mla_in", [1, D, 672])
    c.mla_q_norm = din("mla_q_norm", [1, 384])
    c.mla_uq = din("mla_uq", [1, 384, 1536])
    c.mla_kv_norm = din("mla_kv_norm", [1, 256])
    c.mla_ukv = din("mla_ukv", [1, 256, 2048])
    c.mla_q_head_norm = din("mla_q_head_norm", [1, 96])
    c.mla_k_head_norm = din("mla_k_head_norm", [1, 96])
    c.mla_out = din("mla_out", [1, D, D])
    c.y = dt("y", [NS, TX, D], F32, kind="ExternalOutput").ap()
    c.hT = dt("hT", [KC, 128, NTOK], F32, kind=("ExternalOutput" if debug else "Internal")).ap()
    c.hTv = c.hT.rearrange("c p t -> p c t")
    if debug:
        c.dbg = dt("dbg", [8, KC, 128, NTOK], F32, kind="ExternalOutput").ap()
    c.s_fm = dt("s_fm", [32, 128, NTOK], BF16, kind="Internal").ap()
    c.s_fmv = c.s_fm.rearrange("c p t -> p c t")
    c.s_ktm = dt("s_ktm", [NTOK, 1024], BF16, kind="Internal").ap()
    c.s_vtm = dt("s_vtm", [NTOK, 2048], BF16, kind="Internal").ap()
    c.s_gb = dt("s_gb", [NTOK, 32], F32, kind="Internal").ap()

    c.WAt = P.sbuf("WA", [128, WA_N], BF16)
    c.FAt = P.sbuf("FA", [128, FA_N], F32)
    c.WA = Arena(c.WAt, WA_N)
    c.FA = Arena(c.FAt, FA_N)
    c.ident = P.sbuf("ident", [128, 128], F32)
    c.identb = P.sbuf("identb", [128, 128], BF16)
    c.Ui = P.sbuf("Ui", [128, 128], F32)
    c.Us = P.sbuf("Us", [128, 128], F32)
    c.Ls = P.sbuf("Ls", [128, 128], F32)
    c.ones = P.sbuf("ones", [128, 128], F32)
    c.onesb = P.sbuf("onesb", [128, 128], BF16)
    c.onesD = P.sbuf("onesD", [128, 128], F32)
    c.onesH = P.sbuf("onesH", [128, 128], F32)
    c.small = P.sbuf("small", [128, 256], F32)
    c.ffn_hn = P.sbuf("ffn_hn", [128, 8, 256], BF16)
    c.ffn_a = P.sbuf("ffn_a", [128, 32, 256], BF16)
    c.pb = [P.psum(f"pb{i}", [128, 512], F32) for i in range(7)]
    c.pT = P.psum("pT", [128, 1024], BF16)

    def sel(t, pattern, op, cm):
        P.op("pool", lambda h: h.affine_select(out=t[:], in_=t[:], pattern=pattern, compare_op=op,
                                               fill=0.0, base=0, channel_multiplier=cm),
             reads=[t[:]], writes=[t[:]])
    for t in (c.ident, c.Ui, c.Us, c.Ls, c.ones):
        P.memset("pool", t[:], 1.0)
    P.memset("pool", c.onesD[:], 1.0 / 1024.0)
    P.memset("pool", c.onesH[:], 1.0 / 128.0)
    P.memset("pool", c.onesb[:], 1.0)
    sel(c.ident, [[-1, 128]], ALU.is_equal, 1)
    sel(c.Ui, [[1, 128]], ALU.is_ge, -1)
    sel(c.Us, [[1, 128]], ALU.is_gt, -1)
    sel(c.Ls, [[-1, 128]], ALU.is_gt, 1)
    P.copy("dve", c.identb[:], c.ident[:])
    c.eps = c.small[:, 0:1]
    P.memset("pool", c.eps, EPS)

    import os
    c.kstop = os.environ.get("K_STOP", "")
    c.kp2 = os.environ.get("K_P2", "")
    c.knblk = int(os.environ.get("K_NBLK", "100"))
    phase_input(c)
    for layer in range(n_layers if c.kstop != "input" else 0):
        kind, j = layer % 3, layer // 3
        if kind == 0:
            gdn_layer(c, layer, j)
        elif kind == 1:
            mlstm_layer(c, layer, j)
        else:
            mla_layer(c, layer, j)
        if debug:
            P.dma("sp", c.dbg[2 * layer], c.hT, in_reg=hreg(0, NTOK))
        if stop_after == ("mix", layer):
            break
        ffn_layer(c, layer)
        if debug:
            P.dma("sp", c.dbg[2 * layer + 1], c.hT, in_reg=hreg(0, NTOK))
    if not debug:
        phase_output(c)
    P.final_wait("sp")
    P.emit()
    return nc


def hreg(t0, t1):
    return ("hT", 0, 1, t0, t1)


def sreg(name, t0, t1):
    return (name, 0, 1, t0, t1)


def phase_input(c):
    P = c.P
    c.FA.reset(0)
    xt = [c.FA.get([1024]) for _ in range(2)]
    hblk = [c.FA.get([8, 128]) for _ in range(2)]
    n = 0
    for s in range(NS):
        for i in range(TX // 128 + 1):
            tp = 128 if i < TX // 128 else NM
            src = c.x[s, i * 128:(i + 1) * 128, :] if tp == 128 else c.meta
            t0 = s * TS + i * 128
            xb, hb = xt[n % 2], hblk[n % 2]
            P.dma("sp", xb[0:tp, :], src)
            for g in range(2):
                ps = c.pb[(n * 2 + g) % 4]
                for j in range(4):
                    k = g * 4 + j
                    P.mm(ps[:, j * 128:j * 128 + tp], xb[0:tp, k * 128:(k + 1) * 128], c.ident[0:tp, 0:tp])
                src_ps = ps[:].rearrange("p (a b) -> p a b", a=4)[:, :, 0:tp]
                P.copy("dve" if g == 0 else "act", hb[:, g * 4:(g + 1) * 4, 0:tp], src_ps)
            P.dma("act", c.hTv[:, :, t0:t0 + tp], hb[:, :, 0:tp], out_reg=hreg(t0, t0 + tp))
            n += 1


def phase_output(c):
    P = c.P
    c.FA.reset(0)
    hblk = [c.FA.get([8, 128]) for _ in range(2)]
    yt = [c.FA.get([1024]) for _ in range(2)]
    n = 0
    for s in range(NS):
        for i in range(TX // 128):
            t0 = s * TS + i * 128
            hb, yb = hblk[n % 2], yt[n % 2]
            P.dma("sp", hb, c.hTv[:, :, t0:t0 + 128], in_reg=hreg(t0, t0 + 128))
            for g in range(2):
                ps = c.pb[(n * 2 + g) % 4]
                for j in range(4):
                    k = g * 4 + j
                    P.mm(ps[:, j * 128:(j + 1) * 128], hb[:, k, :], c.ident[:])
                P.copy("dve" if g == 0 else "act", yb[:, g * 512:(g + 1) * 512], ps[:])
            P.dma("act", c.y[s, i * 128:(i + 1) * 128, :], yb)
            n += 1


def load_vec_cols(c, dst, src_1d, nchunk):
    P = c.P
    tmp = c.small[0:nchunk, 128:256]
    P.dma("sp", tmp, src_1d.rearrange("(k p) -> k p", p=128))
    ps = c.pb[6]
    P.mm(ps[:, 0:nchunk], tmp, c.ident[0:nchunk, 0:nchunk])
    P.copy("dve", dst, ps[:, 0:nchunk])


def load_weight(c, dst, src2d, kchunks, cols, q="pool"):
    P = c.P
    sv = src2d.rearrange("(k p) f -> p k f", p=128)
    step = max(1, 8192 // cols)
    for k0 in range(0, kchunks, step):
        k1 = min(kchunks, k0 + step)
        P.dma(q, dst[:, k0:k1, :], sv[:, k0:k1, :])


def rmsnorm_block(c, hb, TB, wcol, hn, sq, rstd, ps, nk=KC, ones=None, eng2="dve"):
    P = c.P
    ones = ones if ones is not None else c.onesD
    for k in range(nk):
        s = sq[k % 2]
        P.act(s[:, 0:TB], hb[:, k, :], AF.Square)
        P.mm(ps[:, 0:TB], ones[:], s[:, 0:TB], start=(k == 0), stop=(k == nk - 1))
    P.act(rstd[:, 0:TB], ps[:, 0:TB], AF.Ln, bias=c.eps)
    P.act(rstd[:, 0:TB], rstd[:, 0:TB], AF.Exp, scale=-0.5)
    for k in range(nk):
        P.stt(eng2 if k % 2 == 0 else "dve", hn[:, k, :], hb[:, k, :], wcol[:, k:k + 1], rstd[:, 0:TB], ALU.mult, ALU.mult)


def ffn_layer(c, layer):
    P = c.P
    TB = 256
    c.WA.reset(0)
    c.FA.reset(0)
    wup = c.WA.get([8, 4096])
    wdn = c.WA.get([32, 1024])
    load_weight(c, wup, c.ff_up[layer], 8, 4096)
    load_weight(c, wdn, c.ff_down[layer], 32, 1024)
    wcol = c.FA.get([8])
    load_vec_cols(c, wcol, c.ffn_norm[layer], 8)
    hbs = [c.FA.get([8, TB]) for _ in range(2)]
    sq = [c.FA.get([TB]) for _ in range(2)]
    rstd = c.FA.get([TB])
    rl = [c.FA.get([TB]) for _ in range(2)]
    hn = c.ffn_hn
    a = c.ffn_a
    blocks = []
    for s in range(NS):
        for (o, tb) in seq_blocks(TB):
            blocks.append((s * TS + o, tb))
    for bi, (t0, tb) in enumerate(blocks):
        hb = hbs[bi % 2]
        P.dma("sp", hb[:, :, 0:tb], c.hTv[:, :, t0:t0 + tb], in_reg=hreg(t0, t0 + tb))
        rmsnorm_block(c, hb[:, :, 0:tb], tb, wcol, hn[:, :, 0:tb], sq, rstd, c.pb[6])
        for m in range(32):
            ps = c.pb[m % 3]
            for k in range(8):
                P.mm(ps[:, 0:tb], wup[:, k, m * 128:(m + 1) * 128], hn[:, k, 0:tb], start=(k == 0), stop=(k == 7))
            r = rl[m % 2]
            P.act(r[:, 0:tb], ps[:, 0:tb], AF.Relu)
            P.tt("pool", a[:, m, 0:tb], r[:, 0:tb], r[:, 0:tb], ALU.mult)
        for n in range(8):
            ps = c.pb[3 + n % 3]
            for m in range(32):
                P.mm(ps[:, 0:tb], wdn[:, m, n * 128:(n + 1) * 128], a[:, m, 0:tb], start=(m == 0), stop=(m == 31))
            P.tt("dve", hb[:, n, 0:tb], hb[:, n, 0:tb], ps[:, 0:tb], ALU.add)
        P.dma("act", c.hTv[:, :, t0:t0 + tb], hb[:, :, 0:tb], out_reg=hreg(t0, t0 + tb))


def gdn_layer(c, layer, j):
    gdn_pass1(c, layer, j)
    if c.kstop == "p1":
        return
    gdn_pass2(c, layer, j)


def gdn_pass1(c, layer, j):
    P = c.P
    TB = 256
    c.WA.reset(0)
    c.FA.reset(0)
    win = c.WA.get([8, 6176])
    load_weight(c, win, c.gdn_in[j], 8, 6176)
    hn = c.WA.get([8, TB])
    qk_st = c.WA.get([16, TB])
    zs_st = c.WA.get([16, TB])
    ktm_st = c.WA.get([2, 8, 128])
    vtm_st = c.WA.get([2, 16, 128])
    wcol = c.FA.get([8])
    load_vec_cols(c, wcol, c.attn_norm[layer], 8)
    wconv = c.FA.get([32, 4])
    cw = c.FA.get([4096])
    P.dma("sp", cw[0:4, :], c.gdn_conv[j])
    for m in range(32):
        ps = c.pb[6]
        P.mm(ps[:, 0:4], cw[0:4, m * 128:(m + 1) * 128], c.ident[0:4, 0:4])
        P.copy("dve", wconv[:, m, :], ps[:, 0:4])
    c.FA.off -= 4096
    dtb = c.FA.get([16])
    nalog = c.FA.get([16])
    P.dma("sp", dtb, c.gdn_dt_bias[j].partition_broadcast(128))
    P.dma("sp", nalog, c.gdn_a_log[j].partition_broadcast(128))
    P.act(nalog, nalog, AF.Exp)
    P.ts("dve", nalog, nalog, -1.0, ALU.mult)
    hbs = [c.FA.get([8, TB]) for _ in range(1)]
    sq = [c.FA.get([TB]) for _ in range(2)]
    rstd = c.FA.get([TB])
    pre = [c.FA.get([TB + 3]) for _ in range(4)]
    yv = [c.FA.get([TB]) for _ in range(4)]
    sv = [c.FA.get([TB]) for _ in range(4)]
    rns = [c.FA.get([TB]) for _ in range(4)]
    sq2 = [c.FA.get([TB]) for _ in range(4)]
    halo = c.FA.get([32, 3])
    gb_st = c.FA.get([2, 32])
    gt = c.FA.get([16])
    vb = [c.WAt[:, WA_N - 512:WA_N - 256], c.WAt[:, WA_N - 256:WA_N]]
    assert c.WA.off <= WA_N - 512
    QSC = 128.0 ** -0.5
    for s in range(NS):
        P.memset("pool", halo, 0.0)
        for (o, tb) in seq_blocks(TB):
            t0 = s * TS + o
            nt = (tb + 127) // 128
            tp = min(tb, 128)
            hb = hbs[0]
            P.dma("sp", hb[:, :, 0:tb], c.hTv[:, :, t0:t0 + tb], in_reg=hreg(t0, t0 + tb))
            rmsnorm_block(c, hb[:, :, 0:tb], tb, wcol, hn[:, :, 0:tb], sq, rstd, c.pb[6])
            for m in range(32):
                ps = c.pb[m % 3]
                for k in range(8):
                    P.mm(ps[:, 0:tb], win[:, k, m * 128:(m + 1) * 128], hn[:, k, 0:tb], start=(k == 0), stop=(k == 7))
                pr, y, sx = pre[m % 4], yv[m % 4], sv[m % 4]
                rn = rns[m % 4]
                P.copy("pool", pr[:, 0:3], halo[:, m, :])
                P.copy("act", pr[:, 3:3 + tb], ps[:, 0:tb])
                P.copy("pool", halo[:, m, :], pr[:, tb:tb + 3])
                P.ts("dve", y[:, 0:tb], pr[:, 0:tb], wconv[:, m, 0:1], ALU.mult)
                for i in range(1, 4):
                    P.stt("dve", y[:, 0:tb], pr[:, i:i + tb], wconv[:, m, i:i + 1], y[:, 0:tb], ALU.mult, ALU.add)
                if m < 16:
                    P.act(sx[:, 0:tb], y[:, 0:tb], AF.Silu)
                    sqq = sq2[m % 4]
                    P.act(sqq[:, 0:tb], sx[:, 0:tb], AF.Square)
                    ps2 = c.pb[3 + m % 2]
                    P.mm(ps2[:, 0:tb], c.ones[:], sqq[:, 0:tb])
                    P.act(rn[:, 0:tb], ps2[:, 0:tb], AF.Ln, bias=c.eps)
                    P.act(rn[:, 0:tb], rn[:, 0:tb], AF.Exp, scale=-0.5)
                    P.stt("pool", qk_st[:, m, 0:tb], sx[:, 0:tb], (QSC if m < 8 else 1.0), rn[:, 0:tb], ALU.mult, ALU.mult)
                    if m >= 8:
                        for ti in range(nt):
                            P.transpose(c.pT[0:tp, (m % 8) * 128:(m % 8) * 128 + 128], qk_st[:, m, ti * 128:ti * 128 + tp], c.identb[:])
                            P.copy("dve", ktm_st[0:tp, ti, m - 8, :], c.pT[0:tp, (m % 8) * 128:(m % 8) * 128 + 128])
                else:
                    vbb = vb[m % 2]
                    P.act(vbb[:, 0:tb], y[:, 0:tb], AF.Silu)
                    for ti in range(nt):
                        P.transpose(c.pT[0:tp, (m % 8) * 128:(m % 8) * 128 + 128], vbb[:, ti * 128:ti * 128 + tp], c.identb[:])
                        P.copy("dve", vtm_st[0:tp, ti, m - 16, :], c.pT[0:tp, (m % 8) * 128:(m % 8) * 128 + 128])
            for m in range(16):
                ps = c.pb[m % 3]
                for k in range(8):
                    P.mm(ps[:, 0:tb], win[:, k, 4096 + m * 128:4096 + (m + 1) * 128], hn[:, k, 0:tb], start=(k == 0), stop=(k == 7))
                P.act(zs_st[:, m, 0:tb], ps[:, 0:tb], AF.Silu)
            for ti in range(nt):
                ps = c.pb[5]
                for k in range(8):
                    P.mm(ps[0:tp, 0:32], hn[:, k, ti * 128:ti * 128 + tp], win[:, k, 6144:6176], start=(k == 0), stop=(k == 7))
                P.act(gb_st[0:tp, ti, 0:16], ps[0:tp, 0:16], AF.Sigmoid)
                P.tt("dve", gt[0:tp, :], ps[0:tp, 16:32], dtb[0:tp, :], ALU.add)
                P.act(gt[0:tp, :], gt[0:tp, :], AF.Exp)
                P.act(gt[0:tp, :], gt[0:tp, :], AF.Ln, bias=1.0)
                P.tt("dve", gb_st[0:tp, ti, 16:32], gt[0:tp, :], nalog[0:tp, :], ALU.mult)
            P.dma("act", c.s_fmv[:, 0:16, t0:t0 + tb], qk_st[:, :, 0:tb], out_reg=sreg("s_fm", t0, t0 + tb))
            P.dma("act", c.s_fmv[:, 16:32, t0:t0 + tb], zs_st[:, :, 0:tb], out_reg=sreg("s_fm", t0, t0 + tb))
            P.dma("sp", c.s_ktm[t0:t0 + tb, :].rearrange("(n p) f -> p n f", p=tp),
                  ktm_st[0:tp, 0:nt, :, :].rearrange("p n h d -> p n (h d)"), out_reg=sreg("s_ktm", t0, t0 + tb))
            P.dma("sp", c.s_vtm[t0:t0 + tb, :].rearrange("(n p) f -> p n f", p=tp),
                  vtm_st[0:tp, 0:nt, :, :].rearrange("p n h d -> p n (h d)"), out_reg=sreg("s_vtm", t0, t0 + tb))
            P.dma("sp", c.s_gb[t0:t0 + tb, :].rearrange("(n p) f -> p n f", p=tp), gb_st[0:tp, 0:nt, :],
                  out_reg=sreg("s_gb", t0, t0 + tb))


def gdn_pass2(c, layer, j):
    P = c.P
    TB = 256
    H = 16
    c.WA.reset(0)
    c.FA.reset(0)
    wout = c.WA.get([16, 1024])
    load_weight(c, wout, c.gdn_out[j], 16, 1024)
    qk = c.WA.get([16, TB])
    zs = c.WA.get([16, TB])
    ktm = c.WA.get([4, 8, 128])
    vtm = c.WA.get([4, 16, 128])
    ogT = c.WA.get([16, TB])
    kbT = c.WA.get([16, 64])
    qdT = c.WA.get([16, 64])
    rhsB = c.WA.get([2, 16, 64])
    bv = c.WA.get([16, 128])
    kbg = c.WA.get([16, 128])
    kdec = c.WA.get([16, 128])
    Rb = c.WA.get([16, 64])
    attnT = c.WA.get([16, 64])
    vnew = c.WA.get([16, 128])
    nwT = c.WA.get([16, 64])
    Sb = c.WA.get([16, 128])
    hb = c.FA.get([8, TB])
    S = c.FA.get([16, 128])
    gb = c.FA.get([4, 32])
    gam = c.FA.get([16])
    Gam = c.FA.get([16])
    last = c.FA.get([16])
    elm = c.FA.get([16])
    bG = c.FA.get([16])
    wn = c.FA.get([1])
    rhsG = c.FA.get([16, 64])
    Dm = c.FA.get([16, 64])
    DT = c.FA.get([16, 64])
    tA = c.FA.get([16, 64])
    tB = c.FA.get([16, 64])
    Rf = c.FA.get([16, 64])
    P.dma("sp", wn, c.gdn_norm[j].rearrange("(p o) -> p o", o=1))
    pb = c.pb

    def hp(ap3, n=2):
        return ap3.rearrange("p (a b) x -> p a b x", b=2)

    for s in range(NS):
        P.memset("pool", S, 0.0)
        P.memset("pool", Sb, 0.0)
        for (o, tb) in seq_blocks(TB)[:c.knblk]:
            t0 = s * TS + o
            Cc = min(tb, 64)
            nch = tb // Cc
            P.dma("sp", qk[:, :, 0:tb], c.s_fmv[:, 0:16, t0:t0 + tb], in_reg=sreg("s_fm", t0, t0 + tb))
            P.dma("sp", zs[:, :, 0:tb], c.s_fmv[:, 16:32, t0:t0 + tb], in_reg=sreg("s_fm", t0, t0 + tb))
            P.dma("act", ktm[0:Cc, 0:nch, :, :].rearrange("p n h d -> p n (h d)"),
                  c.s_ktm[t0:t0 + tb, :].rearrange("(n p) f -> p n f", p=Cc), in_reg=sreg("s_ktm", t0, t0 + tb))
            P.dma("act", vtm[0:Cc, 0:nch, :, :].rearrange("p n h d -> p n (h d)"),
                  c.s_vtm[t0:t0 + tb, :].rearrange("(n p) f -> p n f", p=Cc), in_reg=sreg("s_vtm", t0, t0 + tb))
            P.dma("act", gb[0:Cc, 0:nch, :], c.s_gb[t0:t0 + tb, :].rearrange("(n p) f -> p n f", p=Cc),
                  in_reg=sreg("s_gb", t0, t0 + tb))
            P.dma("sp", hb[:, :, 0:tb], c.hTv[:, :, t0:t0 + tb], in_reg=hreg(t0, t0 + tb))
            L = 5 if Cc == 64 else 3
            for n in range(nch):
                cs = slice(n * Cc, (n + 1) * Cc)
                beta = gb[0:Cc, n, 0:16]
                g = gb[0:Cc, n, 16:32]
                qT = qk[:, 0:8, cs]
                kT = qk[:, 8:16, cs]
                P.mm(pb[0][0:Cc, 0:16], c.Ui[0:Cc, 0:Cc], g)
                P.mm(pb[0][:, 16:32], c.ones[0:Cc, :], g)
                P.copy("dve", gam[0:Cc, :], pb[0][0:Cc, 0:16])
                P.act(Gam[0:Cc, :], pb[0][0:Cc, 0:16], AF.Exp)
                P.act(last[:, :], pb[0][:, 16:32], AF.Exp)
                P.tt("dve", elm[0:Cc, :], pb[0][0:Cc, 16:32], gam[0:Cc, :], ALU.subtract)
                P.act(elm[0:Cc, :], elm[0:Cc, :], AF.Exp)
                P.tt("pool", bG[0:Cc, :], beta, Gam[0:Cc, :], ALU.mult)
                if c.kp2 == 'A':
                    continue
                P.tt("pool", rhsG[0:Cc, :, 0:Cc], bcast(c.Ui[0:Cc, 0:Cc], [Cc, H, Cc], 1), bcast(g, [Cc, H, Cc], 2), ALU.mult)
                P.tt("pool", rhsB[0:Cc, 0, :, 0:Cc], bcast(c.ident[0:Cc, 0:Cc], [Cc, H, Cc], 1), bcast(beta, [Cc, H, Cc], 2), ALU.mult)
                P.tt("pool", rhsB[0:Cc, 1, :, 0:Cc], bcast(c.ident[0:Cc, 0:Cc], [Cc, H, Cc], 1), bcast(Gam[0:Cc, :], [Cc, H, Cc], 2), ALU.mult)
                for hh in range(2):
                    hs = slice(hh * 8, hh * 8 + 8)
                    P.mm(pb[1 + hh][:, 0:8 * Cc].rearrange("p (a b) -> p a b", a=8), c.onesb[0:Cc, :], rhsB[0:Cc, 0, hs, 0:Cc])
                    P.mm(pb[3 + hh][:, 0:8 * Cc].rearrange("p (a b) -> p a b", a=8), c.onesb[0:Cc, :], rhsB[0:Cc, 1, hs, 0:Cc])
                    P.mm(pb[5 + hh][0:Cc, 0:8 * Cc].rearrange("p (a b) -> p a b", a=8), c.ones[0:Cc, 0:Cc], rhsG[0:Cc, hs, 0:Cc])
                for hh in range(2):
                    hs = slice(hh * 8, hh * 8 + 8)
                    hq = slice(hh * 4, hh * 4 + 4)
                    Bp = pb[1 + hh][:, 0:8 * Cc].rearrange("p (a b x) -> p a b x", a=4, b=2)
                    Gp = pb[3 + hh][:, 0:8 * Cc].rearrange("p (a b x) -> p a b x", a=4, b=2)
                    P.tt("dve", hp(kbT[:, hs, 0:Cc]), Bp, bcast(kT[:, hq, :], [128, 4, 2, Cc], 2), ALU.mult)
                    P.tt("dve", hp(qdT[:, hs, 0:Cc]), Gp, bcast(qT[:, hq, :], [128, 4, 2, Cc], 2), ALU.mult)
                if c.kp2 == 'B':
                    continue
                for hh in range(2):
                    hs = slice(hh * 8, hh * 8 + 8)
                    Gps = pb[5 + hh][0:Cc, 0:8 * Cc].rearrange("p (a b) -> p a b", a=8)
                    gbc = bcast(gam[0:Cc, hs], [Cc, 8, Cc], 2)
                    P.stt("dve", tA[0:Cc, hs, 0:Cc], Gps, -1.0, gbc, ALU.mult, ALU.add)
                    P.ts("pool", tA[0:Cc, hs, 0:Cc], tA[0:Cc, hs, 0:Cc], 0.0, ALU.min)
                    P.act(Dm[0:Cc, hs, 0:Cc], tA[0:Cc, hs, 0:Cc], AF.Exp)
                    P.tt("dve", tB[0:Cc, hs, 0:Cc], Gps, gbc, ALU.subtract)
                    P.ts("pool", tB[0:Cc, hs, 0:Cc], tB[0:Cc, hs, 0:Cc], 0.0, ALU.min)
                    P.act(DT[0:Cc, hs, 0:Cc], tB[0:Cc, hs, 0:Cc], AF.Exp)
                if c.kp2 == 'C':
                    continue
                for h in range(H):
                    hh, hl = h // 8, h % 8
                    P.mm(pb[1 + hh][0:Cc, hl * Cc:(hl + 1) * Cc], kbT[:, h, 0:Cc], kT[:, h // 2, :])
                    P.mm(pb[3 + hh][0:Cc, hl * Cc:(hl + 1) * Cc], kT[:, h // 2, :], kbT[:, h, 0:Cc])
                for hq_ in range(8):
                    P.mm(pb[0][0:Cc, hq_ * Cc:(hq_ + 1) * Cc], kT[:, hq_, :], qT[:, hq_, :])
                for hh in range(2):
                    hs = slice(hh * 8, hh * 8 + 8)
                    KKb = pb[1 + hh][0:Cc, 0:8 * Cc].rearrange("p (a b) -> p a b", a=8)
                    KKbT = pb[3 + hh][0:Cc, 0:8 * Cc].rearrange("p (a b) -> p a b", a=8)
                    P.stt("dve", tA[0:Cc, hs, 0:Cc], KKb, -1.0, bcast(c.Ls[0:Cc, 0:Cc], [Cc, 8, Cc], 1), ALU.mult, ALU.mult)
                    P.tt("dve", tA[0:Cc, hs, 0:Cc], tA[0:Cc, hs, 0:Cc], Dm[0:Cc, hs, 0:Cc], ALU.mult)
                    P.stt("dve", tB[0:Cc, hs, 0:Cc], KKbT, -1.0, bcast(c.Us[0:Cc, 0:Cc], [Cc, 8, Cc], 1), ALU.mult, ALU.mult)
                    P.tt("dve", tB[0:Cc, hs, 0:Cc], tB[0:Cc, hs, 0:Cc], DT[0:Cc, hs, 0:Cc], ALU.mult)
                    P.tt("dve", Rf[0:Cc, hs, 0:Cc], tB[0:Cc, hs, 0:Cc], bcast(c.ident[0:Cc, 0:Cc], [Cc, 8, Cc], 1), ALU.add)
                P.tt("pool", DT[0:Cc, :, 0:Cc], DT[0:Cc, :, 0:Cc], bcast(c.Ui[0:Cc, 0:Cc], [Cc, H, Cc], 1), ALU.mult)
                QKT = pb[0][0:Cc, 0:8 * Cc].rearrange("p (a x) -> p a x", a=8)
                P.tt("dve", hp(attnT[0:Cc, :, 0:Cc]), bcast(QKT, [Cc, 8, 2, Cc], 2), hp(DT[0:Cc, :, 0:Cc]), ALU.mult)
                if c.kp2 == 'D':
                    continue
                Xf = [tB, DT]
                Yf = [tA, Dm]
                cur = 0
                for l in range(1, L + 1):
                    nxt = 1 - cur
                    for h in range(H):
                        hh, hl = h // 8, h % 8
                        P.mm(pb[1 + hh][0:Cc, hl * Cc:(hl + 1) * Cc], Xf[cur][0:Cc, h, 0:Cc], Yf[cur][0:Cc, h, 0:Cc])
                    if l < L:
                        for h in range(H):
                            hh, hl = h // 8, h % 8
                            P.mm(pb[3 + hh][0:Cc, hl * Cc:(hl + 1) * Cc], Yf[cur][0:Cc, h, 0:Cc], Xf[cur][0:Cc, h, 0:Cc])
                    for hh in range(2):
                        hs = slice(hh * 8, hh * 8 + 8)
                        P.copy("act", Yf[nxt][0:Cc, hs, 0:Cc], pb[1 + hh][0:Cc, 0:8 * Cc].rearrange("p (a b) -> p a b", a=8))
                        if l < L:
                            P.copy("dve", Xf[nxt][0:Cc, hs, 0:Cc], pb[3 + hh][0:Cc, 0:8 * Cc].rearrange("p (a b) -> p a b", a=8))
                    for h in range(H):
                        hh, hl = h // 8, h % 8
                        P.mm(pb[5 + hh][0:Cc, hl * Cc:(hl + 1) * Cc], Yf[nxt][0:Cc, h, 0:Cc], Rf[0:Cc, h, 0:Cc])
                    for hh in range(2):
                        hs = slice(hh * 8, hh * 8 + 8)
                        P.tt("dve", Rf[0:Cc, hs, 0:Cc], pb[5 + hh][0:Cc, 0:8 * Cc].rearrange("p (a b) -> p a b", a=8), Rf[0:Cc, hs, 0:Cc], ALU.add)
                    cur = nxt
                P.copy("act", Rb[0:Cc, :, 0:Cc], Rf[0:Cc, :, 0:Cc])
                if c.kp2 == 'E':
                    continue
                P.tt("pool", bv[0:Cc, :, :], vtm[0:Cc, n, :, :], bcast(beta, [Cc, H, 128], 2), ALU.mult)
                P.tt("pool", hp(kbg[0:Cc, :, :]), bcast(ktm[0:Cc, n, :, :], [Cc, 8, 2, 128], 2), hp(bcast(bG[0:Cc, :], [Cc, H, 128], 2)), ALU.mult)
                P.tt("pool", hp(kdec[0:Cc, :, :]), bcast(ktm[0:Cc, n, :, :], [Cc, 8, 2, 128], 2), hp(bcast(elm[0:Cc, :], [Cc, H, 128], 2)), ALU.mult)
                if c.kp2 == 'F1':
                    continue
                for h in range(H):
                    hh, hl = h // 8, h % 8
                    P.mm(pb[1 + hh][:, hl * Cc:(hl + 1) * Cc], kbg[0:Cc, h, :], Rb[0:Cc, h, 0:Cc])
                for hh in range(2):
                    hs = slice(hh * 8, hh * 8 + 8)
                    P.ts("dve", nwT[:, hs, 0:Cc], pb[1 + hh][:, 0:8 * Cc].rearrange("p (a b) -> p a b", a=8), -1.0, ALU.mult)
                if c.kp2 == 'F2':
                    continue
                for hg in range(4):
                    ps = pb[3 + hg % 2]
                    for hl in range(4):
                        h = hg * 4 + hl
                        P.mm(ps[0:Cc, hl * 128:(hl + 1) * 128], Rb[0:Cc, h, 0:Cc], bv[0:Cc, h, :], start=True, stop=False)
                        P.mm(ps[0:Cc, hl * 128:(hl + 1) * 128], nwT[:, h, 0:Cc], Sb[:, h, :], start=False, stop=True)
                    P.copy("act" if hg % 2 == 0 else "dve", vnew[0:Cc, hg * 4:hg * 4 + 4, :], ps[0:Cc, :].rearrange("p (a b) -> p a b", a=4))
                if c.kp2 == 'F3':
                    continue
                for h in range(H):
                    hh, hl = h // 8, h % 8
                    P.mm(pb[5 + hh][:, hl * Cc:(hl + 1) * Cc], Sb[:, h, :], qdT[:, h, 0:Cc], start=True, stop=False)
                    P.mm(pb[5 + hh][:, hl * Cc:(hl + 1) * Cc], vnew[0:Cc, h, :], attnT[0:Cc, h, 0:Cc], start=False, stop=True)
                if c.kp2 == 'F4':
                    continue
                for hg in range(4):
                    ps = pb[3 + hg % 2]
                    for hl in range(4):
                        h = hg * 4 + hl
                        P.mm(ps[:, hl * 128:(hl + 1) * 128], kdec[0:Cc, h, :], vnew[0:Cc, h, :])
                    for hl in range(4):
                        h = hg * 4 + hl
                        P.stt("dve", S[:, h, :], S[:, h, :], last[:, h:h + 1], ps[:, hl * 128:(hl + 1) * 128], ALU.mult, ALU.add)
                P.copy("act", Sb[:, :, :], S[:, :, :])
                if c.kp2 == 'F':
                    continue
                for hh in range(2):
                    hs = slice(hh * 8, hh * 8 + 8)
                    oT = pb[5 + hh][:, 0:8 * Cc].rearrange("p (a b) -> p a b", a=8)
                    sqv = tA[:, hs, 0:Cc]
                    P.act(sqv, oT, AF.Square)
                    P.mm(pb[1 + hh][:, 0:8 * Cc].rearrange("p (a b) -> p a b", a=8), c.onesH[:], sqv)
                    rs = tB[:, hs, 0:Cc]
                    P.act(rs, pb[1 + hh][:, 0:8 * Cc].rearrange("p (a b) -> p a b", a=8), AF.Ln, bias=c.eps)
                    P.act(rs, rs, AF.Exp, scale=-0.5)
                    P.tt("dve", rs, oT, rs, ALU.mult)
                    P.stt("pool", ogT[:, hs, cs], rs, wn[:, 0:1], zs[:, hs, cs], ALU.mult, ALU.mult)
            for m in range(8):
                ps = pb[m % 2]
                for h in range(H):
                    P.mm(ps[:, 0:tb], wout[:, h, m * 128:(m + 1) * 128], ogT[:, h, 0:tb], start=(h == 0), stop=(h == H - 1))
                P.tt("dve", hb[:, m, 0:tb], hb[:, m, 0:tb], ps[:, 0:tb], ALU.add)
            P.dma("act", c.hTv[:, :, t0:t0 + tb], hb[:, :, 0:tb], out_reg=hreg(t0, t0 + tb))


def mlstm_layer(c, layer, j):
    mlstm_pass1(c, layer, j)
    if c.kstop == "p1":
        return
    mlstm_pass2(c, layer, j)


def mlstm_pass1(c, layer, j):
    P = c.P
    TB = 256
    c.WA.reset(0)
    c.FA.reset(0)
    win = c.WA.get([8, 3088])
    load_weight(c, win, c.mlstm_in[j], 8, 3088)
    hn = c.WA.get([8, TB])
    fm_st = c.WA.get([16, TB])
    ktm_st = c.WA.get([2, 512])
    vtm_st = c.WA.get([2, 1024])
    vb = [c.WA.get([TB]) for _ in range(2)]
    wcol = c.FA.get([8])
    load_vec_cols(c, wcol, c.attn_norm[layer], 8)
    gbias = c.FA.get([16])
    P.dma("sp", gbias, c.mlstm_gate_bias[j].partition_broadcast(128))
    hb = c.FA.get([8, TB])
    sq = [c.FA.get([TB]) for _ in range(2)]
    rstd = c.FA.get([TB])
    gb_st = c.FA.get([2, 16])
    gt = c.FA.get([16])
    for s in range(NS):
        for (o, tb) in seq_blocks(TB):
            t0 = s * TS + o
            nt = (tb + 127) // 128
            tp = min(tb, 128)
            P.dma("sp", hb[:, :, 0:tb], c.hTv[:, :, t0:t0 + tb], in_reg=hreg(t0, t0 + tb))
            rmsnorm_block(c, hb[:, :, 0:tb], tb, wcol, hn[:, :, 0:tb], sq, rstd, c.pb[6])
            for m in range(24):
                ps = c.pb[m % 2]
                for k in range(8):
                    P.mm(ps[:, 0:tb], win[:, k, m * 128:(m + 1) * 128], hn[:, k, 0:tb], start=(k == 0), stop=(k == 7))
                if m < 4:
                    P.copy("act", fm_st[:, m, 0:tb], ps[:, 0:tb])
                elif m < 8:
                    P.ts("dve", fm_st[:, m, 0:tb], ps[:, 0:tb], 0.125, ALU.mult)
                    for ti in range(nt):
                        pt = c.pT[0:tp, (m % 8) * 128:(m % 8) * 128 + 128]
                        P.transpose(pt, fm_st[:, m, ti * 128:ti * 128 + tp], c.identb[:])
                        P.copy("dve", ktm_st[0:tp, ti, (m - 4) * 128:(m - 3) * 128], pt)
                elif m < 16:
                    vbb = vb[m % 2]
                    P.copy("act", vbb[:, 0:tb], ps[:, 0:tb])
                    for ti in range(nt):
                        pt = c.pT[0:tp, (m % 8) * 128:(m % 8) * 128 + 128]
                        P.transpose(pt, vbb[:, ti * 128:ti * 128 + tp], c.identb[:])
                        P.copy("dve", vtm_st[0:tp, ti, (m - 8) * 128:(m - 7) * 128], pt)
                else:
                    P.act(fm_st[:, m - 8, 0:tb], ps[:, 0:tb], AF.Sigmoid)
            for ti in range(nt):
                ps = c.pb[4 + ti % 2]
                for k in range(8):
                    P.mm(ps[0:tp, 0:16], hn[:, k, ti * 128:ti * 128 + tp], win[:, k, 3072:3088], start=(k == 0), stop=(k == 7))
                P.tt("dve", gt[0:tp, :], ps[0:tp, 0:16], gbias[0:tp, :], ALU.add)
                P.act(gt[0:tp, :], gt[0:tp, :], AF.Tanh, scale=1.0 / 15.0)
                P.ts("dve", gb_st[0:tp, ti, 0:8], gt[0:tp, 0:8], 15.0, ALU.mult)
                P.act(gt[0:tp, 8:16], gt[0:tp, 8:16], AF.Exp, scale=-15.0)
                P.act(gt[0:tp, 8:16], gt[0:tp, 8:16], AF.Ln, bias=1.0)
                P.ts("dve", gb_st[0:tp, ti, 8:16], gt[0:tp, 8:16], -1.0, ALU.mult)
            P.dma("act", c.s_fmv[:, 0:16, t0:t0 + tb], fm_st[:, :, 0:tb], out_reg=sreg("s_fm", t0, t0 + tb))
            P.dma("sp", c.s_ktm[t0:t0 + tb, 0:512].rearrange("(n p) f -> p n f", p=tp), ktm_st[0:tp, 0:nt, :],
                  out_reg=sreg("s_ktm", t0, t0 + tb))
            P.dma("sp", c.s_vtm[t0:t0 + tb, 0:1024].rearrange("(n p) f -> p n f", p=tp), vtm_st[0:tp, 0:nt, :],
                  out_reg=sreg("s_vtm", t0, t0 + tb))
            P.dma("sp", c.s_gb[t0:t0 + tb, 0:16].rearrange("(n p) f -> p n f", p=tp), gb_st[0:tp, 0:nt, :],
                  out_reg=sreg("s_gb", t0, t0 + tb))


def mlstm_pass2(c, layer, j):
    P = c.P
    TB = 256
    H = 8
    c.WA.reset(0)
    c.FA.reset(0)
    wout = c.WA.get([8, 1024])
    load_weight(c, wout, c.mlstm_out[j], 8, 1024)
    fm = c.WA.get([16, TB])
    ktm = c.WA.get([4, 8, 64])
    v2 = c.WA.get([4, 8, 256])
    ogT = c.WA.get([8, TB])
    qdT = c.WA.get([4, 64])
    rhsB = c.WA.get([8, 64])
    kdecz = c.WA.get([8, 128])
    attnT = c.WA.get([8, 64])
    Sb = c.WA.get([4, 256])
    hb = c.FA.get([8, TB])
    S = c.FA.get([4, 256])
    gb = c.FA.get([4, 16])
    gam = c.FA.get([8])
    Gam = c.FA.get([8])
    last = c.FA.get([8])
    elm = c.FA.get([8])
    bj = c.FA.get([8])
    wn = c.FA.get([8])
    rhsG = c.FA.get([8, 64])
    DT = c.FA.get([8, 64])
    tA = c.FA.get([8, 64])
    tB = c.FA.get([8, 64])
    load_vec_cols(c, wn, c.mlstm_norm[j], 8)
    pb = c.pb
    P.memset("pool", v2[:, :, :, 128:256], 1.0)
    P.memset("pool", kdecz, 0.0)
    for s in range(NS):
        P.memset("pool", S, 0.0)
        P.memset("pool", Sb, 0.0)
        for (o, tb) in seq_blocks(TB)[:c.knblk]:
            t0 = s * TS + o
            Cc = min(tb, 64)
            nch = tb // Cc
            P.dma("sp", fm[:, :, 0:tb], c.s_fmv[:, 0:16, t0:t0 + tb], in_reg=sreg("s_fm", t0, t0 + tb))
            P.dma("act", ktm[0:Cc, 0:nch, :, :].rearrange("p n h d -> p n (h d)"),
                  c.s_ktm[t0:t0 + tb, 0:512].rearrange("(n p) f -> p n f", p=Cc), in_reg=sreg("s_ktm", t0, t0 + tb))
            for n in range(nch):
                P.dma("act", v2[0:Cc, n, :, 0:128],
                      c.s_vtm[t0 + n * Cc:t0 + (n + 1) * Cc, 0:1024].rearrange("p (h d) -> p h d", h=8),
                      in_reg=sreg("s_vtm", t0, t0 + tb))
            P.dma("act", gb[0:Cc, 0:nch, :], c.s_gb[t0:t0 + tb, 0:16].rearrange("(n p) f -> p n f", p=Cc),
                  in_reg=sreg("s_gb", t0, t0 + tb))
            P.dma("sp", hb[:, :, 0:tb], c.hTv[:, :, t0:t0 + tb], in_reg=hreg(t0, t0 + tb))
            for n in range(nch):
                cs = slice(n * Cc, (n + 1) * Cc)
                ipre = gb[0:Cc, n, 0:8]
                lf = gb[0:Cc, n, 8:16]
                P.mm(pb[0][0:Cc, 0:8], c.Ui[0:Cc, 0:Cc], lf)
                P.mm(pb[0][:, 8:16], c.ones[0:Cc, :], lf)
                P.copy("dve", gam[0:Cc, :], pb[0][0:Cc, 0:8])
                P.act(Gam[0:Cc, :], pb[0][0:Cc, 0:8], AF.Exp)
                P.act(last[:, :], pb[0][:, 8:16], AF.Exp)
                P.tt("dve", elm[0:Cc, :], pb[0][0:Cc, 8:16], gam[0:Cc, :], ALU.subtract)
                P.tt("dve", elm[0:Cc, :], elm[0:Cc, :], ipre, ALU.add)
                P.act(elm[0:Cc, :], elm[0:Cc, :], AF.Exp)
                P.tt("pool", rhsG[0:Cc, :, 0:Cc], bcast(c.Ui[0:Cc, 0:Cc], [Cc, H, Cc], 1), bcast(lf, [Cc, H, Cc], 2), ALU.mult)
                P.tt("pool", rhsB[0:Cc, :, 0:Cc], bcast(c.ident[0:Cc, 0:Cc], [Cc, H, Cc], 1), bcast(Gam[0:Cc, :], [Cc, H, Cc], 2), ALU.mult)
                P.mm(pb[1][:, 0:8 * Cc].rearrange("p (a b) -> p a b", a=8), c.onesb[0:Cc, :], rhsB[0:Cc, :, 0:Cc])
                P.mm(pb[2][0:Cc, 0:8 * Cc].rearrange("p (a b) -> p a b", a=8), c.ones[0:Cc, 0:Cc], rhsG[0:Cc, :, 0:Cc])
                Gm4 = pb[1][:, 0:8 * Cc].rearrange("p (a b x) -> p a b x", a=4, b=2)
                for b_ in range(2):
                    pr = slice(b_ * 64, (b_ + 1) * 64)
                    P.tt("dve", qdT[pr, :, 0:Cc], Gm4[pr, :, b_, :], fm[pr, 0:4, cs], ALU.mult)
                Gps = pb[2][0:Cc, 0:8 * Cc].rearrange("p (a b) -> p a b", a=8)
                P.tt("dve", tB[0:Cc, :, 0:Cc], Gps, bcast(gam[0:Cc, :], [Cc, H, Cc], 2), ALU.subtract)
                P.stt("dve", tB[0:Cc, :, 0:Cc], tB[0:Cc, :, 0:Cc], 0.0, bcast(ipre, [Cc, H, Cc], 2), ALU.min, ALU.add)
                P.act(DT[0:Cc, :, 0:Cc], tB[0:Cc, :, 0:Cc], AF.Exp)
                P.tt("pool", DT[0:Cc, :, 0:Cc], DT[0:Cc, :, 0:Cc], bcast(c.Ui[0:Cc, 0:Cc], [Cc, H, Cc], 1), ALU.mult)
                for h in range(H):
                    a_, b_ = h // 2, h % 2
                    pr = slice(b_ * 64, (b_ + 1) * 64)
                    P.mm(pb[3][0:Cc, h * Cc:(h + 1) * Cc], fm[pr, 4 + a_, cs], fm[pr, a_, cs])
                P.tt("dve", attnT[0:Cc, :, 0:Cc], pb[3][0:Cc, 0:8 * Cc].rearrange("p (a b) -> p a b", a=8), DT[0:Cc, :, 0:Cc], ALU.mult)
                for b_ in range(2):
                    kz = kdecz[0:Cc, :, :].rearrange("p (a b) x -> p a b x", b=2)[:, :, b_, b_ * 64:(b_ + 1) * 64]
                    kin = ktm[0:Cc, n, :, :].rearrange("p (a b) x -> p a b x", b=2)[:, :, b_, :]
                    ein = elm[0:Cc, :].rearrange("p (a b) -> p a b", b=2)[:, :, b_]
                    P.tt("pool", kz, kin, bcast(ein, [Cc, 4, 64], 2), ALU.mult)
                for h in range(H):
                    a_, b_ = h // 2, h % 2
                    pr = slice(b_ * 64, (b_ + 1) * 64)
                    P.mm(pb[4][:, h * Cc:(h + 1) * Cc], Sb[pr, a_, 0:128], qdT[pr, a_, 0:Cc], start=True, stop=False)
                    P.mm(pb[4][:, h * Cc:(h + 1) * Cc], v2[0:Cc, n, h, 0:128], attnT[0:Cc, h, 0:Cc], start=False, stop=True)
                    P.mm(pb[5][:, h * Cc:(h + 1) * Cc], Sb[pr, a_, 128:256], qdT[pr, a_, 0:Cc], start=True, stop=False)
                    P.mm(pb[5][:, h * Cc:(h + 1) * Cc], v2[0:Cc, n, h, 128:256], attnT[0:Cc, h, 0:Cc], start=False, stop=True)
                for a_ in range(4):
                    ps = pb[1 + a_ % 2]
                    P.mm(ps[:, 0:256], kdecz[0:Cc, 2 * a_, :], v2[0:Cc, n, 2 * a_, :], start=True, stop=False)
                    P.mm(ps[:, 0:256], kdecz[0:Cc, 2 * a_ + 1, :], v2[0:Cc, n, 2 * a_ + 1, :], start=False, stop=True)
                    for b_ in range(2):
                        pr = slice(b_ * 64, (b_ + 1) * 64)
                        h = 2 * a_ + b_
                        P.stt("dve", S[pr, a_, :], S[pr, a_, :], last[pr, h:h + 1], ps[pr, 0:256], ALU.mult, ALU.add)
                P.copy("act", Sb[:, :, :], S[:, :, :])
                num = pb[4][:, 0:8 * Cc].rearrange("p (a b) -> p a b", a=8)
                den = pb[5][:, 0:8 * Cc].rearrange("p (a b) -> p a b", a=8)
                dn = tA[:, :, 0:Cc]
                P.act(dn, den, AF.Abs)
                P.ts("pool", dn, dn, 1.0, ALU.max)
                P.recip(dn, dn)
                hh_ = tB[:, :, 0:Cc]
                P.tt("dve", hh_, num, dn, ALU.mult)
                P.act(dn, hh_, AF.Square)
                P.mm(pb[6][:, 0:8 * Cc].rearrange("p (a b) -> p a b", a=8), c.onesH[:], dn)
                P.act(dn, pb[6][:, 0:8 * Cc].rearrange("p (a b) -> p a b", a=8), AF.Ln, bias=c.eps)
                P.act(dn, dn, AF.Exp, scale=-0.5)
                P.tt("pool", hh_, hh_, dn, ALU.mult)
                P.tt("pool", hh_, hh_, bcast(wn[:, :], [128, H, Cc], 2), ALU.mult)
                P.tt("pool", ogT[:, :, cs], hh_, fm[:, 8:16, cs], ALU.mult)
            for m in range(8):
                ps = pb[m % 2]
                for h in range(H):
                    P.mm(ps[:, 0:tb], wout[:, h, m * 128:(m + 1) * 128], ogT[:, h, 0:tb], start=(h == 0), stop=(h == H - 1))
                P.tt("dve", hb[:, m, 0:tb], hb[:, m, 0:tb], ps[:, 0:tb], ALU.add)
            P.dma("act", c.hTv[:, :, t0:t0 + tb], hb[:, :, 0:tb], out_reg=hreg(t0, t0 + tb))


def mla_layer(c, layer, j):
    mla_pass1(c, layer, j)
    if c.kstop == "p1":
        return
    mla_pass2(c, layer, j)


def mla_pass1(c, layer, j):
    import math
    P = c.P
    TB = 256
    c.WA.reset(0)
    c.FA.reset(0)
    win = c.WA.get([8, 672])
    wuq = c.WA.get([3, 1536])
    wukv = c.WA.get([2, 2048])
    wkr = c.WA.get([8, 96])
    load_weight(c, win, c.mla_in[j], 8, 672)
    load_weight(c, wuq, c.mla_uq[j], 3, 1536)
    load_weight(c, wukv, c.mla_ukv[j], 2, 2048)
    P.memset("pool", wkr, 0.0)
    P.copy("pool", wkr[:, :, 64:96], win[:, :, 640:672])
    hn = c.WA.get([8, TB])
    qlat = c.WA.get([3, TB])
    kvlat = c.WA.get([2, TB])
    qk_st = c.WA.get([32, TB])
    vtm_st = c.WA.get([2, 16, 128])
    wcol = c.FA.get([8])
    load_vec_cols(c, wcol, c.attn_norm[layer], 8)
    qn_w = c.FA.get([3])
    load_vec_cols(c, qn_w, c.mla_q_norm[j], 3)
    kvn_w = c.FA.get([2])
    load_vec_cols(c, kvn_w, c.mla_kv_norm[j], 2)
    hw = c.FA.get([2])
    P.dma("sp", hw[0:96, 0:1], c.mla_q_head_norm[j].rearrange("(p o) -> p o", o=1))
    P.dma("sp", hw[0:96, 1:2], c.mla_k_head_norm[j].rearrange("(p o) -> p o", o=1))
    P.ts("dve", hw[0:96, 0:1], hw[0:96, 0:1], 96.0 ** -0.5, ALU.mult)
    ones384 = c.FA.get([128])
    ones256 = c.FA.get([128])
    ones96 = c.FA.get([128])
    P.memset("pool", ones384, 1.0 / 384.0)
    P.memset("pool", ones256, 1.0 / 256.0)
    P.memset("pool", ones96, 1.0 / 96.0)
    Rm = c.FA.get([128])
    P.memset("pool", Rm, 0.0)
    P.copy("pool", Rm[64:96, 80:96], c.ident[64:96, 64:80])
    P.copy("pool", Rm[64:96, 64:80], c.ident[64:96, 80:96])
    P.ts("pool", Rm[64:96, 64:80], Rm[64:96, 64:80], -1.0, ALU.mult)
    Cf = c.FA.get([TS])
    Sf = c.FA.get([TS])
    fa_mark = c.FA.off
    pos_i = c.FA.get([TS]).bitcast(mybir.dt.int32)
    posf = c.FA.get([TS])
    pidx_i = c.FA.get([1]).bitcast(mybir.dt.int32)
    pf = c.FA.get([4])
    P.op("pool", lambda h: h.iota(pos_i[:, 0:TX], [[1, TX]], base=NM, channel_multiplier=0), writes=[pos_i[:, 0:TX]])
    P.op("pool", lambda h: h.iota(pos_i[:, TX:TS], [[1, NM]], base=0, channel_multiplier=0), writes=[pos_i[:, TX:TS]])
    P.op("pool", lambda h: h.iota(pidx_i[:, 0:1], [[1, 1]], base=0, channel_multiplier=1), writes=[pidx_i[:, 0:1]])
    P.copy("dve", posf, pos_i)
    P.copy("dve", pf[:, 0:1], pidx_i[:, 0:1])
    P.ts("dve", pf[:, 1:2], pf[:, 0:1], 80.0, ALU.is_ge, -16.0, ALU.mult)
    P.tt("dve", pf[:, 1:2], pf[:, 1:2], pf[:, 0:1], ALU.add)
    P.ts("dve", pf[:, 1:2], pf[:, 1:2], -64.0, ALU.add)
    P.act(pf[:, 2:3], pf[:, 1:2], AF.Exp, scale=-math.log(10000.0) / 16.0)
    P.ts("dve", pf[:, 2:3], pf[:, 2:3], 1.0 / (2.0 * math.pi), ALU.mult)
    P.ts("dve", posf, posf, pf[:, 2:3], ALU.mult)
    P.copy("dve", pos_i, posf)
    P.copy("dve", Cf, pos_i)
    P.tt("dve", posf, posf, Cf, ALU.subtract)
    P.act(Sf, posf, AF.Sin, scale=math.pi)
    P.act(Cf, posf, AF.Sin, scale=math.pi / 2.0)
    P.tt("dve", Cf, Cf, Cf, ALU.mult)
    P.ts("dve", Cf, Cf, -2.0, ALU.mult, 1.0, ALU.add)
    P.tt("dve", posf, Sf, Cf, ALU.mult)
    P.ts("dve", posf, posf, 2.0, ALU.mult)
    P.tt("dve", Cf, Sf, Sf, ALU.mult)
    P.ts("dve", Cf, Cf, -2.0, ALU.mult, 1.0, ALU.add)
    P.copy("dve", Sf, posf)
    P.memset("pool", Cf[0:64, :], 1.0)
    P.memset("pool", Sf[0:64, :], 0.0)
    c.FA.off = fa_mark
    hb = c.FA.get([8, TB])
    sq = [c.FA.get([TB]) for _ in range(2)]
    rstd = c.FA.get([TB])
    lat = c.FA.get([3, TB])
    kr = c.FA.get([TB])
    kp = c.FA.get([TB])
    qn = c.FA.get([TB])
    t1 = c.FA.get([TB])
    QS = 96.0 ** -0.5
    wv = wukv[:, :, :].rearrange("p k (h x) -> p k h x", x=128)
    P.memset("pool", vtm_st[:, :, :, 64:128], 1.0)

    def head_norm_rope(src_ps, wi, dst, cols, tb, scale):
        P.act(sq[0][0:96, 0:tb], src_ps, AF.Square)
        P.mm(c.pb[5][0:96, 0:tb], ones96[0:96, 0:96], sq[0][0:96, 0:tb])
        P.act(rstd[0:96, 0:tb], c.pb[5][0:96, 0:tb], AF.Ln, bias=c.eps[0:96, :])
        P.act(rstd[0:96, 0:tb], rstd[0:96, 0:tb], AF.Exp, scale=-0.5)
        P.stt("dve", qn[0:96, 0:tb], src_ps, hw[0:96, wi:wi + 1], rstd[0:96, 0:tb], ALU.mult, ALU.mult)
        P.mm(c.pb[4][0:96, 0:tb], Rm[0:96, 0:96], qn[0:96, 0:tb])
        P.tt("dve", t1[0:96, 0:tb], c.pb[4][0:96, 0:tb], Sf[0:96, cols], ALU.mult)
        P.tt("pool", qn[0:96, 0:tb], qn[0:96, 0:tb], Cf[0:96, cols], ALU.mult)
        P.tt("dve", dst, qn[0:96, 0:tb], t1[0:96, 0:tb], ALU.add)

    for s in range(NS):
        for (o, tb) in seq_blocks(TB):
            t0 = s * TS + o
            nt = (tb + 127) // 128
            tp = min(tb, 128)
            cols = slice(o, o + tb)
            P.dma("sp", hb[:, :, 0:tb], c.hTv[:, :, t0:t0 + tb], in_reg=hreg(t0, t0 + tb))
            rmsnorm_block(c, hb[:, :, 0:tb], tb, wcol, hn[:, :, 0:tb], sq, rstd, c.pb[6])
            for m in range(5):
                ps = c.pb[m % 2]
                for k in range(8):
                    P.mm(ps[:, 0:tb], win[:, k, m * 128:(m + 1) * 128], hn[:, k, 0:tb], start=(k == 0), stop=(k == 7))
                P.copy("act", lat[:, m % 3, 0:tb], ps[:, 0:tb])
                if m == 2:
                    rmsnorm_block(c, lat[:, 0:3, 0:tb], tb, qn_w, qlat[:, :, 0:tb], sq, rstd, c.pb[6], nk=3, ones=ones384)
                if m == 4:
                    rmsnorm_block(c, lat[:, 0:2, 0:tb], tb, kvn_w, kvlat[:, :, 0:tb], sq, rstd, c.pb[6], nk=2, ones=ones256)
            ps = c.pb[2]
            for k in range(8):
                P.mm(ps[0:96, 0:tb], wkr[:, k, :], hn[:, k, 0:tb], start=(k == 0), stop=(k == 7))
            P.copy("act", kr[64:96, 0:tb], ps[64:96, 0:tb])
            for h in range(16):
                ps = c.pb[h % 2]
                for k in range(3):
                    P.mm(ps[0:96, 0:tb], wuq[:, k, h * 96:(h + 1) * 96], qlat[:, k, 0:tb], start=(k == 0), stop=(k == 2))
                head_norm_rope(ps[0:96, 0:tb], 0, qk_st[0:96, h, 0:tb], cols, tb, QS)
                ps2 = c.pb[2 + h % 2]
                for k in range(2):
                    P.mm(ps2[0:64, 0:tb], wukv[:, k, h * 128:h * 128 + 64], kvlat[:, k, 0:tb], start=(k == 0), stop=(k == 1))
                P.copy("act", kp[0:64, 0:tb], ps2[0:64, 0:tb])
                P.copy("pool", kp[64:96, 0:tb], kr[64:96, 0:tb])
                head_norm_rope(kp[0:96, 0:tb], 1, qk_st[0:96, 16 + h, 0:tb], cols, tb, 1.0)
            for ti in range(nt):
                for g in range(2):
                    ps = c.pb[g]
                    for k in range(2):
                        P.mm(ps[0:tp, :].rearrange("p (h x) -> p h x", h=8), kvlat[:, k, ti * 128:ti * 128 + tp],
                             wv[:, k, g * 8:(g + 1) * 8, 64:128], start=(k == 0), stop=(k == 1))
                    P.copy("act" if g == 0 else "dve", vtm_st[0:tp, ti, g * 8:(g + 1) * 8, 0:64],
                           ps[0:tp, :].rearrange("p (h x) -> p h x", h=8))
            P.dma("act", c.s_fmv[0:96, :, t0:t0 + tb], qk_st[0:96, :, 0:tb], out_reg=sreg("s_fm", t0, t0 + tb))
            P.dma("sp", c.s_vtm[t0:t0 + tb, :].rearrange("(n p) f -> p n f", p=tp),
                  vtm_st[0:tp, 0:nt, :, :].rearrange("p n h d -> p n (h d)"), out_reg=sreg("s_vtm", t0, t0 + tb))


def mla_pass2(c, layer, j):
    P = c.P
    c.WA.reset(0)
    c.FA.reset(0)
    wout = c.WA.get([16, 1024])
    P = c.P
    sv = c.mla_out[j].rearrange("(h p) f -> p h f", p=64)
    for h0 in range(0, 16, 4):
        P.dma("pool", wout[0:64, h0:h0 + 4, :], sv[:, h0:h0 + 4, :])
    og = c.WA.get([16, TS])
    kTs = [c.WA.get([TS]) for _ in range(2)]
    qTs = [c.WA.get([TS]) for _ in range(2)]
    vxs = [c.WA.get([16, 128]) for _ in range(2)]
    vms = [c.WA.get([128]) for _ in range(2)]
    pts = [c.WA.get([512]) for _ in range(3)]
    Uib = c.WA.get([128])
    P.copy("dve", Uib, c.Ui[:])
    rr = c.FA.get([512])
    cp = c.FA.get([512])
    hb = c.FA.get([8, 512])
    pb = c.pb
    npt = 0
    for s in range(NS):
        sb = s * TS
        for h in range(16):
            kT, qT, vx, vm = kTs[h % 2], qTs[h % 2], vxs[h % 2], vms[h % 2]
            P.dma("sp", qT[0:96, :], c.s_fmv[0:96, h, sb:sb + TS], in_reg=sreg("s_fm", sb, sb + TS))
            P.dma("sp", kT[0:96, :], c.s_fmv[0:96, 16 + h, sb:sb + TS], in_reg=sreg("s_fm", sb, sb + TS))
            P.dma("act", vx, c.s_vtm[sb:sb + TX, h * 128:(h + 1) * 128].rearrange("(n p) f -> p n f", p=128),
                  in_reg=sreg("s_vtm", sb, sb + TS))
            P.dma("act", vm[0:NM, :], c.s_vtm[sb + TX:sb + TS, h * 128:(h + 1) * 128], in_reg=sreg("s_vtm", sb, sb + TS))
            qblocks = [(TX, NM)] + [(qb * 512, 512) for qb in range(4)]
            for bi, (q0, nq) in enumerate(qblocks):
                acc = pb[4 + bi % 2]
                st = pb[npt % 4]
                pt = pts[npt % 3]
                npt += 1
                P.mm(st[0:NM, 0:nq], kT[0:96, TX:TS], qT[0:96, q0:q0 + nq])
                P.act(pt[0:NM, 0:nq], st[0:NM, 0:nq], AF.Exp)
                if q0 == TX:
                    P.tt("pool", pt[0:NM, 0:NM], pt[0:NM, 0:NM], Uib[0:NM, 0:NM], ALU.mult)
                P.mm(acc[:, 0:nq], vm[0:NM, :], pt[0:NM, 0:nq], start=True, stop=(q0 == TX))
                if q0 != TX:
                    nkt = q0 // 128 + 4
                    for kt in range(nkt):
                        jd = kt - q0 // 128
                        c0 = 128 * jd if jd > 0 else 0
                        st = pb[npt % 4]
                        pt = pts[npt % 3]
                        npt += 1
                        P.mm(st[:, c0:nq], kT[0:96, kt * 128:(kt + 1) * 128], qT[0:96, q0 + c0:q0 + nq])
                        P.act(pt[:, c0:nq], st[:, c0:nq], AF.Exp)
                        if jd >= 0:
                            P.tt("pool", pt[:, c0:c0 + 128], pt[:, c0:c0 + 128], Uib, ALU.mult)
                        P.mm(acc[:, c0:nq], vx[:, kt, :], pt[:, c0:nq], start=False, stop=(kt == nkt - 1))
                P.recip(rr[64:128, 0:nq], acc[64:128, 0:nq])
                P.memset("pool", rr[0:64, 0:nq], 0.0)
                st = pb[npt % 4]
                npt += 1
                P.mm(st[0:64, 0:nq], c.ident[:, 64:128], rr[:, 0:nq])
                P.copy("act", cp[0:64, 0:nq], st[0:64, 0:nq])
                P.tt("dve", og[0:64, h, q0:q0 + nq], acc[0:64, 0:nq], cp[0:64, 0:nq], ALU.mult)
        for (q0, nq) in [(TX, NM)] + [(qb * 512, 512) for qb in range(4)]:
            t0 = sb + q0
            P.dma("sp", hb[:, :, 0:nq], c.hTv[:, :, t0:t0 + nq], in_reg=hreg(t0, t0 + nq))
            for m in range(8):
                ps = pb[6] if m % 2 == 0 else pb[0]
                for h in range(16):
                    P.mm(ps[:, 0:nq], wout[0:64, h, m * 128:(m + 1) * 128], og[0:64, h, q0:q0 + nq], start=(h == 0), stop=(h == 15))
                P.tt("dve", hb[:, m, 0:nq], hb[:, m, 0:nq], ps[:, 0:nq], ALU.add)
            P.dma("act", c.hTv[:, :, t0:t0 + nq], hb[:, :, 0:nq], out_reg=hreg(t0, t0 + nq))


_NC_CACHE = {}
W_NAMES = ["meta_tokens", "attn_norm", "ffn_norm", "ff_up", "ff_down", "gdn_in", "gdn_conv", "gdn_a_log",
           "gdn_dt_bias", "gdn_norm", "gdn_out", "mlstm_in", "mlstm_gate_bias", "mlstm_norm", "mlstm_out",
           "mla_in", "mla_q_norm", "mla_uq", "mla_kv_norm", "mla_ukv", "mla_q_head_norm", "mla_k_head_norm",
           "mla_out"]


def kernel(**inputs):
    nc = build_program()
    x = np.ascontiguousarray(np.asarray(inputs["x"], dtype=np.float32))
    shared = {k: np.ascontiguousarray(np.asarray(inputs[k], dtype=np.float32)) for k in W_NAMES}
    in_maps = []
    for core in range(8):
        m = dict(shared)
        m["x"] = x[core * NS:(core + 1) * NS]
        in_maps.append(m)
    res = run_bass_kernel_spmd(nc, in_maps, core_ids=list(range(8)))
    return np.concatenate([r["y"] for r in res.results], axis=0)
```

```python
import contextlib
import numpy as np
import concourse.bass as bass
import concourse.mybir as mybir

F32 = mybir.dt.float32
BF16 = mybir.dt.bfloat16
AF = mybir.ActivationFunctionType
ALU = mybir.AluOpType
AX = mybir.AxisListType

SEM_LIMIT = 30000
_ISZ = {}


def _isz(dt):
    k = str(dt)
    if k not in _ISZ:
        _ISZ[k] = {"dt.float32": 4, "dt.bfloat16": 2, "dt.int32": 4, "dt.uint32": 4,
                   "dt.float16": 2, "dt.uint8": 1, "dt.int8": 1, "dt.uint16": 2,
                   "dt.int16": 2}.get(k, 4)
    return _ISZ[k]


class SemCounter:
    def __init__(self, prog, name):
        self.prog = prog
        self.name = name
        self.retired = []
        self.sem = None
        self.val = 0
        self.n = 0

    def _new(self):
        self.sem = self.prog.new_sem(f"{self.name}_{self.n}")
        self.n += 1
        self.val = 0

    def next(self, inc):
        if self.sem is None:
            self._new()
        elif self.val + inc > SEM_LIMIT:
            self.retired.append((self.sem, self.val))
            self._new()
        self.val += inc
        return self.sem, self.val


class Ev:
    __slots__ = ("sem", "val", "clock", "eng", "dma_ctr")

    def __init__(self, sem, val, clock, eng, dma_ctr=None):
        self.sem = sem
        self.val = val
        self.clock = clock
        self.eng = eng
        self.dma_ctr = dma_ctr


class Eng:
    def __init__(self, prog, name):
        self.name = name
        self.ops = []
        self.clock = {}
        self.ctr = SemCounter(prog, "s" + name)


def region(ap):
    t = ap.tensor
    name = t.name
    pat = ap.ap
    isz = _isz(ap.dtype)
    space = str(ap.space) if hasattr(ap, "space") else ""
    off = ap.offset
    if "DRAM" in space.upper() or "HBM" in space.upper():
        lo = off
        hi = off
        for st, cnt in pat:
            if st >= 0:
                hi += st * (cnt - 1)
            else:
                lo += st * (cnt - 1)
        return (name, 0, 1, lo * isz, (hi + 1) * isz)
    if "PSUM" in space.upper():
        return (name, 0, 128, 0, 1 << 30, "psum")
    pstride, pcnt = pat[0]
    if pstride == 0:
        pstride = 1 << 40
    p0 = ap.start_partition()
    fo = off - p0 * pstride if pstride < (1 << 40) else off
    lo = fo
    hi = fo
    for st, cnt in pat[1:]:
        if st >= 0:
            hi += st * (cnt - 1)
        else:
            lo += st * (cnt - 1)
    return (name, p0, p0 + pcnt, lo * isz, (hi + 1) * isz)


class Prog:
    def __init__(self, nc, same_engine_sync=True):
        self.nc = nc
        self.stack = contextlib.ExitStack()
        self.engs = {n: Eng(self, n) for n in ("pe", "act", "dve", "pool", "sp")}
        self.recs = {}
        self.dma_ctrs = {}
        self.same_engine_sync = same_engine_sync
        self.nsem = 0
        self.n_ops = 0
        self.pe_last = {}

    def new_sem(self, name):
        self.nsem += 1
        return self.stack.enter_context(self.nc.semaphore(name))

    def sbuf(self, name, shape, dt):
        return self.stack.enter_context(self.nc.sbuf_tensor(name, list(shape), dt))

    def psum(self, name, shape, dt=F32):
        return self.stack.enter_context(self.nc.psum_tensor(name, list(shape), dt))

    def _conflicts(self, reg, is_write, eng=None):
        name, p0, p1, f0, f1 = reg[:5]
        out = []
        if len(reg) > 5:
            for r in self.recs.get(name, ()):
                if r[5].eng != eng:
                    out.append((r[5], True))
                elif is_write or r[4]:
                    out.append((r[5], r[4]))
            return out
        for r in self.recs.get(name, ()):
            if r[0] < p1 and p0 < r[1] and r[2] < f1 and f0 < r[3]:
                if is_write or r[4]:
                    out.append((r[5], r[4]))
        return out

    def _record(self, reg, is_write, ev):
        name, p0, p1, f0, f1 = reg[:5]
        lst = self.recs.setdefault(name, [])
        if len(reg) > 5:
            lst[:] = [r for r in lst if not (r[5].eng == ev.eng and r[4] == is_write)]
            lst.append([p0, p1, f0, f1, is_write, ev])
            return
        if is_write:
            lst[:] = [r for r in lst if not (p0 <= r[0] and r[1] <= p1 and f0 <= r[2] and r[3] <= f1)]
        else:
            lst[:] = [r for r in lst if not ((not r[4]) and r[5].eng == ev.eng and r[5].dma_ctr is None
                                             and ev.dma_ctr is None
                                             and r[0] == p0 and r[1] == p1 and r[2] == f0 and r[3] == f1)]
        lst.append([p0, p1, f0, f1, is_write, ev])

    def _gather_waits(self, eng, reads, writes):
        e = self.engs[eng]
        need = {}
        deps = []
        for reg in reads:
            for ev, was_write in self._conflicts(reg, False, eng):
                deps.append((ev, True))
        for reg in writes:
            for ev, was_write in self._conflicts(reg, True, eng):
                deps.append((ev, was_write))
        for ev, hard in deps:
            if ev.dma_ctr is None and ev.eng == eng:
                if eng in ("pe", "sp"):
                    continue
                if not hard or not self.same_engine_sync:
                    continue
            pairs = []
            if ev.dma_ctr is not None:
                c = ev.dma_ctr
                pairs.extend(c.retired)
                pairs.append((c.sem, c.val))
            else:
                pairs.append((ev.sem, ev.val))
            for sem, val in pairs:
                if e.clock.get(id(sem), 0) >= val:
                    continue
                k = id(sem)
                if k not in need or need[k][1] < val:
                    need[k] = (sem, val)
            for k, v in ev.clock.items():
                if e.clock.get(k, 0) < v:
                    e.clock[k] = v
        waits = list(need.values())
        for sem, val in waits:
            if e.clock.get(id(sem), 0) < val:
                e.clock[id(sem)] = val
        return waits

    def op(self, eng, fn, reads=(), writes=(), extra_reads=(), extra_writes=()):
        e = self.engs[eng]
        rr = [region(a) for a in reads] + list(extra_reads)
        ww = [region(a) for a in writes] + list(extra_writes)
        waits = self._gather_waits(eng, rr, ww)
        sem, val = e.ctr.next(1)
        clock = dict(e.clock)
        clock[id(sem)] = val
        ev = Ev(sem, val, clock, eng)
        e.ops.append((waits, fn, sem, 1))
        for reg in rr:
            self._record(reg, False, ev)
        for reg in ww:
            self._record(reg, True, ev)
        self.n_ops += 1
        return ev

    def dma(self, q, out, in_, extra_reads=(), extra_writes=(), sem_key=None, in_reg=None, out_reg=None, **kw):
        e = self.engs[q]
        rr = [in_reg if in_reg is not None else region(in_)] + list(extra_reads)
        ww = [out_reg if out_reg is not None else region(out)] + list(extra_writes)
        waits = self._gather_waits(q, rr, ww)
        key = sem_key or out.tensor.name
        if key not in self.dma_ctrs:
            self.dma_ctrs[key] = SemCounter(self, "d" + key[:20])
        c = self.dma_ctrs[key]
        sem, val = c.next(16)
        clock = dict(e.clock)
        ev = Ev(sem, val, clock, q, dma_ctr=c)

        def fn(engh, out=out, in_=in_, kw=kw):
            return engh.dma_start(out=out, in_=in_, **kw)
        e.ops.append((waits, fn, sem, 16))
        for reg in rr:
            self._record(reg, False, ev)
        for reg in ww:
            self._record(reg, True, ev)
        self.n_ops += 1
        return ev

    def final_wait(self, eng="sp"):
        e = self.engs[eng]
        waits = []
        for c in self.dma_ctrs.values():
            for sem, val in c.retired + [(c.sem, c.val)]:
                if e.clock.get(id(sem), 0) < val:
                    waits.append((sem, val))
                    e.clock[id(sem)] = val
        for n, o in self.engs.items():
            if o.ctr.sem is not None and n != eng:
                for sem, val in o.ctr.retired + [(o.ctr.sem, o.ctr.val)]:
                    if e.clock.get(id(sem), 0) < val:
                        waits.append((sem, val))
                        e.clock[id(sem)] = val
        e.ops.append((waits, None, None, 0))

    def emit(self):
        nc = self.nc
        with nc.Block() as block:
            def run(name):
                def body(h):
                    for waits, fn, sem, inc in self.engs[name].ops:
                        for s, v in waits:
                            h.wait_ge(s, v)
                        if fn is not None:
                            ins = fn(h)
                            ins.then_inc(sem, inc)
                return body
            if self.engs["sp"].ops:
                block.sync(run("sp"))
            if self.engs["act"].ops:
                block.scalar(run("act"))
            if self.engs["dve"].ops:
                block.vector(run("dve"))
            if self.engs["pool"].ops:
                block.gpsimd(run("pool"))
            if self.engs["pe"].ops:
                block.tensor(run("pe"))
        self.stack.close()

    def mm(self, out, lhsT, rhs, start=True, stop=True, **kw):
        rd = [lhsT, rhs]
        bank = out.tensor.name
        lr = region(lhsT)
        g0, g1 = lr[1] // 32, (lr[2] + 31) // 32
        prev = self.pe_last.get(bank)
        e = self.engs["pe"]
        extra = []
        if prev is not None and (prev[1] <= g0 or g1 <= prev[0]):
            sem, val = prev[2].sem, prev[2].val
            if e.clock.get(id(sem), 0) < val:
                extra.append((sem, val))
                e.clock[id(sem)] = val
        ev = self.op("pe", lambda h: h.matmul(out, lhsT, rhs, start=start, stop=stop, **kw),
                     reads=rd if start else rd + [out], writes=[out])
        if extra:
            w, fn, sm, inc = e.ops[-1]
            e.ops[-1] = (list(w) + extra, fn, sm, inc)
        self.pe_last[bank] = (g0, g1, ev)
        return ev

    def transpose(self, out, in_, ident):
        return self.op("pe", lambda h: h.transpose(out, in_, ident), reads=[in_, ident], writes=[out])

    def act(self, out, in_, func, bias=None, scale=None, accum_out=None, eng="act"):
        kw = {}
        rd = [in_]
        wr = [out]
        if bias is not None:
            kw["bias"] = bias
            if not isinstance(bias, (int, float)):
                rd.append(bias)
        if scale is not None:
            kw["scale"] = scale
            if not isinstance(scale, (int, float)):
                rd.append(scale)
        if accum_out is not None:
            kw["accum_out"] = accum_out
            wr.append(accum_out)
        return self.op("act", lambda h: h.activation(out, in_, func, **kw), reads=rd, writes=wr)

    def tt(self, eng, out, in0, in1, op):
        return self.op(eng, lambda h: h.tensor_tensor(out, in0, in1, op), reads=[in0, in1], writes=[out])

    def ts(self, eng, out, in0, s1, op0, s2=None, op1=None, accum_out=None):
        rd = [in0]
        if not isinstance(s1, (int, float)):
            rd.append(s1)
        if s2 is not None and not isinstance(s2, (int, float)):
            rd.append(s2)
        wr = [out]
        kw = {}
        if s2 is None:
            s2 = 0.0
            op1 = ALU.add
        if op1 is not None:
            kw["op1"] = op1
        if accum_out is not None:
            kw["accum_out"] = accum_out
            wr.append(accum_out)
        return self.op(eng, lambda h: h.tensor_scalar(out, in0, s1, s2, op0, **kw), reads=rd, writes=wr)

    def stt(self, eng, out, in0, scalar, in1, op0, op1):
        rd = [in0, in1]
        if not isinstance(scalar, (int, float)):
            rd.append(scalar)
        return self.op("dve", lambda h: h.scalar_tensor_tensor(out, in0, scalar, in1, op0, op1),
                       reads=rd, writes=[out])

    def copy(self, eng, out, in_):
        if eng == "act":
            return self.op("act", lambda h: h.copy(out, in_), reads=[in_], writes=[out])
        return self.op(eng, lambda h: h.tensor_copy(out, in_), reads=[in_], writes=[out])

    def memset(self, eng, out, val):
        return self.op(eng, lambda h: h.memset(out, val), reads=[], writes=[out])

    def recip(self, out, in_):
        return self.op("dve", lambda h: h.reciprocal(out, in_), reads=[in_], writes=[out])

    def reduce(self, eng, out, in_, op, axis=None):
        axis = axis or AX.X
        return self.op(eng, lambda h: h.tensor_reduce(out, in_, axis, op), reads=[in_], writes=[out])

from concourse.bass_utils import run_bass_kernel_spmd

NS = 2
TX = 2048
NM = 16
TS = TX + NM
NTOK = NS * TS
D = 1024
KC = 8
EPS = 1e-6
WA_N = 66304
FA_N = 11800


def bcast(ap, shape, axis):
    return ap.unsqueeze(axis).broadcast_to(list(shape))


class Arena:
    def __init__(self, t, n, base=0):
        self.t = t
        self.n = n
        self.base = base
        self.off = base

    def reset(self, base=None):
        if base is not None:
            self.base = base
        self.off = self.base

    def get(self, shape, parts=128):
        size = 1
        for s in shape:
            size *= s
        assert self.off + size <= self.n, (self.off, size, self.n)
        a = self.t[0:parts, self.off:self.off + size]
        self.off += size
        if len(shape) == 2:
            a = a.rearrange("p (a b) -> p a b", a=shape[0])
        elif len(shape) == 3:
            a = a.rearrange("p (a b c) -> p a b c", a=shape[0], b=shape[1])
        elif len(shape) == 4:
            a = a.rearrange("p (a b c d) -> p a b c d", a=shape[0], b=shape[1], c=shape[2])
        return a


class C:
    pass


def seq_blocks(TB):
    out = [(TX, NM)]
    for b in range(TX // TB):
        out.append((b * TB, TB))
    return out


def build_program(n_layers=4, debug=False, stop_after=None):
    nc = bass.Bass("TRN2", target_bir_lowering=False)
    P = Prog(nc)
    c = C()
    c.P = P
    c.nc = nc
    dt = nc.dram_tensor

    def din(name, shape):
        return dt(name, list(shape), F32, kind="ExternalInput").ap()

    c.x = din("x", [NS, TX, D])
    c.meta = din("meta_tokens", [NM, D])
    c.attn_norm = din("attn_norm", [4, D])
    c.ffn_norm = din("ffn_norm", [4, D])
    c.ff_up = din("ff_up", [4, D, 4096])
    c.ff_down = din("ff_down", [4, 4096, D])
    c.gdn_in = din("gdn_in", [2, D, 6176])
    c.gdn_conv = din("gdn_conv", [2, 4, 4096])
    c.gdn_a_log = din("gdn_a_log", [2, 16])
    c.gdn_dt_bias = din("gdn_dt_bias", [2, 16])
    c.gdn_norm = din("gdn_norm", [2, 128])
    c.gdn_out = din("gdn_out", [2, 2048, D])
    c.mlstm_in = din("mlstm_in", [1, D, 3088])
    c.mlstm_gate_bias = din("mlstm_gate_bias", [1, 16])
    c.mlstm_norm = din("mlstm_norm", [1, D])
    c.mlstm_out = din("mlstm_out", [1, D, D])
    c.mla_in = din("mla_in", [1, D, 672])
    c.mla_q_norm = din("mla_q_norm", [1, 384])
    c.mla_uq = din("mla_uq", [1, 384, 1536])
    c.mla_kv_norm = din("mla_kv_norm", [1, 256])
    c.mla_ukv = din("mla_ukv", [1, 256, 2048])
    c.mla_q_head_norm = din("mla_q_head_norm", [1, 96])
    c.mla_k_head_norm = din("mla_k_head_norm", [1, 96])
    c.mla_out = din("mla_out", [1, D, D])
    c.y = dt("y", [NS, TX, D], F32, kind="ExternalOutput").ap()
    c.hT = dt("hT", [KC, 128, NTOK], F32, kind=("ExternalOutput" if debug else "Internal")).ap()
    c.hTv = c.hT.rearrange("c p t -> p c t")
    if debug:
        c.dbg = dt("dbg", [8, KC, 128, NTOK], F32, kind="ExternalOutput").ap()
    c.s_fm = dt("s_fm", [32, 128, NTOK], BF16, kind="Internal").ap()
    c.s_fmv = c.s_fm.rearrange("c p t -> p c t")
    c.s_ktm = dt("s_ktm", [NTOK, 1024], BF16, kind="Internal").ap()
    c.s_vtm = dt("s_vtm", [NTOK, 2048], BF16, kind="Internal").ap()
    c.s_gb = dt("s_gb", [NTOK, 32], F32, kind="Internal").ap()

    c.WAt = P.sbuf("WA", [128, WA_N], BF16)
    c.FAt = P.sbuf("FA", [128, FA_N], F32)
    c.WA = Arena(c.WAt, WA_N)
    c.FA = Arena(c.FAt, FA_N)
    c.ident = P.sbuf("ident", [128, 128], F32)
    c.identb = P.sbuf("identb", [128, 128], BF16)
    c.Ui = P.sbuf("Ui", [128, 128], F32)
    c.Us = P.sbuf("Us", [128, 128], F32)
    c.Ls = P.sbuf("Ls", [128, 128], F32)
    c.ones = P.sbuf("ones", [128, 128], F32)
    c.onesb = P.sbuf("onesb", [128, 128], BF16)
    c.onesD = P.sbuf("onesD", [128, 128], F32)
    c.onesH = P.sbuf("onesH", [128, 128], F32)
    c.small = P.sbuf("small", [128, 256], F32)
    c.ffn_hn = P.sbuf("ffn_hn", [128, 8, 256], BF16)
    c.ffn_a = P.sbuf("ffn_a", [128, 32, 256], BF16)
    c.pb = [P.psum(f"pb{i}", [128, 512], F32) for i in range(7)]
    c.pT = P.psum("pT", [128, 1024], BF16)

    def sel(t, pattern, op, cm):
        P.op("pool", lambda h: h.affine_select(out=t[:], in_=t[:], pattern=pattern, compare_op=op,
                                               fill=0.0, base=0, channel_multiplier=cm),
             reads=[t[:]], writes=[t[:]])
    for t in (c.ident, c.Ui, c.Us, c.Ls, c.ones):
        P.memset("pool", t[:], 1.0)
    P.memset("pool", c.onesD[:], 1.0 / 1024.0)
    P.memset("pool", c.onesH[:], 1.0 / 128.0)
    P.memset("pool", c.onesb[:], 1.0)
    sel(c.ident, [[-1, 128]], ALU.is_equal, 1)
    sel(c.Ui, [[1, 128]], ALU.is_ge, -1)
    sel(c.Us, [[1, 128]], ALU.is_gt, -1)
    sel(c.Ls, [[-1, 128]], ALU.is_gt, 1)
    P.copy("dve", c.identb[:], c.ident[:])
    c.eps = c.small[:, 0:1]
    P.memset("pool", c.eps, EPS)

    import os
    c.kstop = os.environ.get("K_STOP", "")
    c.kp2 = os.environ.get("K_P2", "")
    c.knblk = int(os.environ.get("K_NBLK", "100"))
    phase_input(c)
    for layer in range(n_layers if c.kstop != "input" else 0):
        kind, j = layer % 3, layer // 3
        if kind == 0:
            gdn_layer(c, layer, j)
        elif kind == 1:
            mlstm_layer(c, layer, j)
        else:
            mla_layer(c, layer, j)
        if debug:
            P.dma("sp", c.dbg[2 * layer], c.hT, in_reg=hreg(0, NTOK))
        if stop_after == ("mix", layer):
            break
        ffn_layer(c, layer)
        if debug:
            P.dma("sp", c.dbg[2 * layer + 1], c.hT, in_reg=hreg(0, NTOK))
    if not debug:
        phase_output(c)
    P.final_wait("sp")
    P.emit()
    return nc


def hreg(t0, t1):
    return ("hT", 0, 1, t0, t1)


def sreg(name, t0, t1):
    return (name, 0, 1, t0, t1)


def phase_input(c):
    P = c.P
    c.FA.reset(0)
    xt = [c.FA.get([1024]) for _ in range(2)]
    hblk = [c.FA.get([8, 128]) for _ in range(2)]
    n = 0
    for s in range(NS):
        for i in range(TX // 128 + 1):
            tp = 128 if i < TX // 128 else NM
            src = c.x[s, i * 128:(i + 1) * 128, :] if tp == 128 else c.meta
            t0 = s * TS + i * 128
            xb, hb = xt[n % 2], hblk[n % 2]
            P.dma("sp", xb[0:tp, :], src)
            for g in range(2):
                ps = c.pb[(n * 2 + g) % 4]
                for j in range(4):
                    k = g * 4 + j
                    P.mm(ps[:, j * 128:j * 128 + tp], xb[0:tp, k * 128:(k + 1) * 128], c.ident[0:tp, 0:tp])
                src_ps = ps[:].rearrange("p (a b) -> p a b", a=4)[:, :, 0:tp]
                P.copy("dve" if g == 0 else "act", hb[:, g * 4:(g + 1) * 4, 0:tp], src_ps)
            P.dma("act", c.hTv[:, :, t0:t0 + tp], hb[:, :, 0:tp], out_reg=hreg(t0, t0 + tp))
            n += 1


def phase_output(c):
    P = c.P
    c.FA.reset(0)
    hblk = [c.FA.get([8, 128]) for _ in range(2)]
    yt = [c.FA.get([1024]) for _ in range(2)]
    n = 0
    for s in range(NS):
        for i in range(TX // 128):
            t0 = s * TS + i * 128
            hb, yb = hblk[n % 2], yt[n % 2]
            P.dma("sp", hb, c.hTv[:, :, t0:t0 + 128], in_reg=hreg(t0, t0 + 128))
            for g in range(2):
                ps = c.pb[(n * 2 + g) % 4]
                for j in range(4):
                    k = g * 4 + j
                    P.mm(ps[:, j * 128:(j + 1) * 128], hb[:, k, :], c.ident[:])
                P.copy("dve" if g == 0 else "act", yb[:, g * 512:(g + 1) * 512], ps[:])
            P.dma("act", c.y[s, i * 128:(i + 1) * 128, :], yb)
            n += 1


def load_vec_cols(c, dst, src_1d, nchunk):
    P = c.P
    tmp = c.small[0:nchunk, 128:256]
    P.dma("sp", tmp, src_1d.rearrange("(k p) -> k p", p=128))
    ps = c.pb[6]
    P.mm(ps[:, 0:nchunk], tmp, c.ident[0:nchunk, 0:nchunk])
    P.copy("dve", dst, ps[:, 0:nchunk])


def load_weight(c, dst, src2d, kchunks, cols, q="pool"):
    P = c.P
    sv = src2d.rearrange("(k p) f -> p k f", p=128)
    step = max(1, 8192 // cols)
    for k0 in range(0, kchunks, step):
        k1 = min(kchunks, k0 + step)
        P.dma(q, dst[:, k0:k1, :], sv[:, k0:k1, :])


def rmsnorm_block(c, hb, TB, wcol, hn, sq, rstd, ps, nk=KC, ones=None, eng2="dve"):
    P = c.P
    ones = ones if ones is not None else c.onesD
    for k in range(nk):
        s = sq[k % 2]
        P.act(s[:, 0:TB], hb[:, k, :], AF.Square)
        P.mm(ps[:, 0:TB], ones[:], s[:, 0:TB], start=(k == 0), stop=(k == nk - 1))
    P.act(rstd[:, 0:TB], ps[:, 0:TB], AF.Ln, bias=c.eps)
    P.act(rstd[:, 0:TB], rstd[:, 0:TB], AF.Exp, scale=-0.5)
    for k in range(nk):
        P.stt(eng2 if k % 2 == 0 else "dve", hn[:, k, :], hb[:, k, :], wcol[:, k:k + 1], rstd[:, 0:TB], ALU.mult, ALU.mult)


def ffn_layer(c, layer):
    P = c.P
    TB = 256
    c.WA.reset(0)
    c.FA.reset(0)
    wup = c.WA.get([8, 4096])
    wdn = c.WA.get([32, 1024])
    load_weight(c, wup, c.ff_up[layer], 8, 4096)
    load_weight(c, wdn, c.ff_down[layer], 32, 1024)
    wcol = c.FA.get([8])
    load_vec_cols(c, wcol, c.ffn_norm[layer], 8)
    hbs = [c.FA.get([8, TB]) for _ in range(2)]
    sq = [c.FA.get([TB]) for _ in range(2)]
    rstd = c.FA.get([TB])
    rl = [c.FA.get([TB]) for _ in range(2)]
    hn = c.ffn_hn
    a = c.ffn_a
    blocks = []
    for s in range(NS):
        for (o, tb) in seq_blocks(TB):
            blocks.append((s * TS + o, tb))
    for bi, (t0, tb) in enumerate(blocks):
        hb = hbs[bi % 2]
        P.dma("sp", hb[:, :, 0:tb], c.hTv[:, :, t0:t0 + tb], in_reg=hreg(t0, t0 + tb))
        rmsnorm_block(c, hb[:, :, 0:tb], tb, wcol, hn[:, :, 0:tb], sq, rstd, c.pb[6])
        for m in range(32):
            ps = c.pb[m % 3]
            for k in range(8):
                P.mm(ps[:, 0:tb], wup[:, k, m * 128:(m + 1) * 128], hn[:, k, 0:tb], start=(k == 0), stop=(k == 7))
            r = rl[m % 2]
            P.act(r[:, 0:tb], ps[:, 0:tb], AF.Relu)
            P.tt("pool", a[:, m, 0:tb], r[:, 0:tb], r[:, 0:tb], ALU.mult)
        for n in range(8):
            ps = c.pb[3 + n % 3]
            for m in range(32):
                P.mm(ps[:, 0:tb], wdn[:, m, n * 128:(n + 1) * 128], a[:, m, 0:tb], start=(m == 0), stop=(m == 31))
            P.tt("dve", hb[:, n, 0:tb], hb[:, n, 0:tb], ps[:, 0:tb], ALU.add)
        P.dma("act", c.hTv[:, :, t0:t0 + tb], hb[:, :, 0:tb], out_reg=hreg(t0, t0 + tb))


def gdn_layer(c, layer, j):
    gdn_pass1(c, layer, j)
    if c.kstop == "p1":
        return
    gdn_pass2(c, layer, j)


def gdn_pass1(c, layer, j):
    P = c.P
    TB = 256
    c.WA.reset(0)
    c.FA.reset(0)
    win = c.WA.get([8, 6176])
    load_weight(c, win, c.gdn_in[j], 8, 6176)
    hn = c.WA.get([8, TB])
    qk_st = c.WA.get([16, TB])
    zs_st = c.WA.get([16, TB])
    ktm_st = c.WA.get([2, 8, 128])
    vtm_st = c.WA.get([2, 16, 128])
    wcol = c.FA.get([8])
    load_vec_cols(c, wcol, c.attn_norm[layer], 8)
    wconv = c.FA.get([32, 4])
    cw = c.FA.get([4096])
    P.dma("sp", cw[0:4, :], c.gdn_conv[j])
    for m in range(32):
        ps = c.pb[6]
        P.mm(ps[:, 0:4], cw[0:4, m * 128:(m + 1) * 128], c.ident[0:4, 0:4])
        P.copy("dve", wconv[:, m, :], ps[:, 0:4])
    c.FA.off -= 4096
    dtb = c.FA.get([16])
    nalog = c.FA.get([16])
    P.dma("sp", dtb, c.gdn_dt_bias[j].partition_broadcast(128))
    P.dma("sp", nalog, c.gdn_a_log[j].partition_broadcast(128))
    P.act(nalog, nalog, AF.Exp)
    P.ts("dve", nalog, nalog, -1.0, ALU.mult)
    hbs = [c.FA.get([8, TB]) for _ in range(1)]
    sq = [c.FA.get([TB]) for _ in range(2)]
    rstd = c.FA.get([TB])
    pre = [c.FA.get([TB + 3]) for _ in range(4)]
    yv = [c.FA.get([TB]) for _ in range(4)]
    sv = [c.FA.get([TB]) for _ in range(4)]
    rns = [c.FA.get([TB]) for _ in range(4)]
    sq2 = [c.FA.get([TB]) for _ in range(4)]
    halo = c.FA.get([32, 3])
    gb_st = c.FA.get([2, 32])
    gt = c.FA.get([16])
    vb = [c.WAt[:, WA_N - 512:WA_N - 256], c.WAt[:, WA_N - 256:WA_N]]
    assert c.WA.off <= WA_N - 512
    QSC = 128.0 ** -0.5
    for s in range(NS):
        P.memset("pool", halo, 0.0)
        for (o, tb) in seq_blocks(TB):
            t0 = s * TS + o
            nt = (tb + 127) // 128
            tp = min(tb, 128)
            hb = hbs[0]
            P.dma("sp", hb[:, :, 0:tb], c.hTv[:, :, t0:t0 + tb], in_reg=hreg(t0, t0 + tb))
            rmsnorm_block(c, hb[:, :, 0:tb], tb, wcol, hn[:, :, 0:tb], sq, rstd, c.pb[6])
            for m in range(32):
                ps = c.pb[m % 3]
                for k in range(8):
                    P.mm(ps[:, 0:tb], win[:, k, m * 128:(m + 1) * 128], hn[:, k, 0:tb], start=(k == 0), stop=(k == 7))
                pr, y, sx = pre[m % 4], yv[m % 4], sv[m % 4]
                rn = rns[m % 4]
                P.copy("pool", pr[:, 0:3], halo[:, m, :])
                P.copy("act", pr[:, 3:3 + tb], ps[:, 0:tb])
                P.copy("pool", halo[:, m, :], pr[:, tb:tb + 3])
                P.ts("dve", y[:, 0:tb], pr[:, 0:tb], wconv[:, m, 0:1], ALU.mult)
                for i in range(1, 4):
                    P.stt("dve", y[:, 0:tb], pr[:, i:i + tb], wconv[:, m, i:i + 1], y[:, 0:tb], ALU.mult, ALU.add)
                if m < 16:
                    P.act(sx[:, 0:tb], y[:, 0:tb], AF.Silu)
                    sqq = sq2[m % 4]
                    P.act(sqq[:, 0:tb], sx[:, 0:tb], AF.Square)
                    ps2 = c.pb[3 + m % 2]
                    P.mm(ps2[:, 0:tb], c.ones[:], sqq[:, 0:tb])
                    P.act(rn[:, 0:tb], ps2[:, 0:tb], AF.Ln, bias=c.eps)
                    P.act(rn[:, 0:tb], rn[:, 0:tb], AF.Exp, scale=-0.5)
                    P.stt("pool", qk_st[:, m, 0:tb], sx[:, 0:tb], (QSC if m < 8 else 1.0), rn[:, 0:tb], ALU.mult, ALU.mult)
                    if m >= 8:
                        for ti in range(nt):
                            P.transpose(c.pT[0:tp, (m % 8) * 128:(m % 8) * 128 + 128], qk_st[:, m, ti * 128:ti * 128 + tp], c.identb[:])
                            P.copy("dve", ktm_st[0:tp, ti, m - 8, :], c.pT[0:tp, (m % 8) * 128:(m % 8) * 128 + 128])
                else:
                    vbb = vb[m % 2]
                    P.act(vbb[:, 0:tb], y[:, 0:tb], AF.Silu)
                    for ti in range(nt):
                        P.transpose(c.pT[0:tp, (m % 8) * 128:(m % 8) * 128 + 128], vbb[:, ti * 128:ti * 128 + tp], c.identb[:])
                        P.copy("dve", vtm_st[0:tp, ti, m - 16, :], c.pT[0:tp, (m % 8) * 128:(m % 8) * 128 + 128])
            for m in range(16):
                ps = c.pb[m % 3]
                for k in range(8):
                    P.mm(ps[:, 0:tb], win[:, k, 4096 + m * 128:4096 + (m + 1) * 128], hn[:, k, 0:tb], start=(k == 0), stop=(k == 7))
                P.act(zs_st[:, m, 0:tb], ps[:, 0:tb], AF.Silu)
            for ti in range(nt):
                ps = c.pb[5]
                for k in range(8):
                    P.mm(ps[0:tp, 0:32], hn[:, k, ti * 128:ti * 128 + tp], win[:, k, 6144:6176], start=(k == 0), stop=(k == 7))
                P.act(gb_st[0:tp, ti, 0:16], ps[0:tp, 0:16], AF.Sigmoid)
                P.tt("dve", gt[0:tp, :], ps[0:tp, 16:32], dtb[0:tp, :], ALU.add)
                P.act(gt[0:tp, :], gt[0:tp, :], AF.Exp)
                P.act(gt[0:tp, :], gt[0:tp, :], AF.Ln, bias=1.0)
                P.tt("dve", gb_st[0:tp, ti, 16:32], gt[0:tp, :], nalog[0:tp, :], ALU.mult)
            P.dma("act", c.s_fmv[:, 0:16, t0:t0 + tb], qk_st[:, :, 0:tb], out_reg=sreg("s_fm", t0, t0 + tb))
            P.dma("act", c.s_fmv[:, 16:32, t0:t0 + tb], zs_st[:, :, 0:tb], out_reg=sreg("s_fm", t0, t0 + tb))
            P.dma("sp", c.s_ktm[t0:t0 + tb, :].rearrange("(n p) f -> p n f", p=tp),
                  ktm_st[0:tp, 0:nt, :, :].rearrange("p n h d -> p n (h d)"), out_reg=sreg("s_ktm", t0, t0 + tb))
            P.dma("sp", c.s_vtm[t0:t0 + tb, :].rearrange("(n p) f -> p n f", p=tp),
                  vtm_st[0:tp, 0:nt, :, :].rearrange("p n h d -> p n (h d)"), out_reg=sreg("s_vtm", t0, t0 + tb))
            P.dma("sp", c.s_gb[t0:t0 + tb, :].rearrange("(n p) f -> p n f", p=tp), gb_st[0:tp, 0:nt, :],
                  out_reg=sreg("s_gb", t0, t0 + tb))


def gdn_pass2(c, layer, j):
    P = c.P
    TB = 256
    H = 16
    c.WA.reset(0)
    c.FA.reset(0)
    wout = c.WA.get([16, 1024])
    load_weight(c, wout, c.gdn_out[j], 16, 1024)
    qk = c.WA.get([16, TB])
    zs = c.WA.get([16, TB])
    ktm = c.WA.get([4, 8, 128])
    vtm = c.WA.get([4, 16, 128])
    ogT = c.WA.get([16, TB])
    kbT = c.WA.get([16, 64])
    qdT = c.WA.get([16, 64])
    rhsB = c.WA.get([2, 16, 64])
    bv = c.WA.get([16, 128])
    kbg = c.WA.get([16, 128])
    kdec = c.WA.get([16, 128])
    Rb = c.WA.get([16, 64])
    attnT = c.WA.get([16, 64])
    vnew = c.WA.get([16, 128])
    nwT = c.WA.get([16, 64])
    Sb = c.WA.get([16, 128])
    hb = c.FA.get([8, TB])
    S = c.FA.get([16, 128])
    gb = c.FA.get([4, 32])
    gam = c.FA.get([16])
    Gam = c.FA.get([16])
    last = c.FA.get([16])
    elm = c.FA.get([16])
    bG = c.FA.get([16])
    wn = c.FA.get([1])
    rhsG = c.FA.get([16, 64])
    Dm = c.FA.get([16, 64])
    DT = c.FA.get([16, 64])
    tA = c.FA.get([16, 64])
    tB = c.FA.get([16, 64])
    Rf = c.FA.get([16, 64])
    P.dma("sp", wn, c.gdn_norm[j].rearrange("(p o) -> p o", o=1))
    pb = c.pb

    def hp(ap3, n=2):
        return ap3.rearrange("p (a b) x -> p a b x", b=2)

    for s in range(NS):
        P.memset("pool", S, 0.0)
        P.memset("pool", Sb, 0.0)
        for (o, tb) in seq_blocks(TB)[:c.knblk]:
            t0 = s * TS + o
            Cc = min(tb, 64)
            nch = tb // Cc
            P.dma("sp", qk[:, :, 0:tb], c.s_fmv[:, 0:16, t0:t0 + tb], in_reg=sreg("s_fm", t0, t0 + tb))
            P.dma("sp", zs[:, :, 0:tb], c.s_fmv[:, 16:32, t0:t0 + tb], in_reg=sreg("s_fm", t0, t0 + tb))
            P.dma("act", ktm[0:Cc, 0:nch, :, :].rearrange("p n h d -> p n (h d)"),
                  c.s_ktm[t0:t0 + tb, :].rearrange("(n p) f -> p n f", p=Cc), in_reg=sreg("s_ktm", t0, t0 + tb))
            P.dma("act", vtm[0:Cc, 0:nch, :, :].rearrange("p n h d -> p n (h d)"),
                  c.s_vtm[t0:t0 + tb, :].rearrange("(n p) f -> p n f", p=Cc), in_reg=sreg("s_vtm", t0, t0 + tb))
            P.dma("act", gb[0:Cc, 0:nch, :], c.s_gb[t0:t0 + tb, :].rearrange("(n p) f -> p n f", p=Cc),
                  in_reg=sreg("s_gb", t0, t0 + tb))
            P.dma("sp", hb[:, :, 0:tb], c.hTv[:, :, t0:t0 + tb], in_reg=hreg(t0, t0 + tb))
            L = 5 if Cc == 64 else 3
            for n in range(nch):
                cs = slice(n * Cc, (n + 1) * Cc)
                beta = gb[0:Cc, n, 0:16]
                g = gb[0:Cc, n, 16:32]
                qT = qk[:, 0:8, cs]
                kT = qk[:, 8:16, cs]
                P.mm(pb[0][0:Cc, 0:16], c.Ui[0:Cc, 0:Cc], g)
                P.mm(pb[0][:, 16:32], c.ones[0:Cc, :], g)
                P.copy("dve", gam[0:Cc, :], pb[0][0:Cc, 0:16])
                P.act(Gam[0:Cc, :], pb[0][0:Cc, 0:16], AF.Exp)
                P.act(last[:, :], pb[0][:, 16:32], AF.Exp)
                P.tt("dve", elm[0:Cc, :], pb[0][0:Cc, 16:32], gam[0:Cc, :], ALU.subtract)
                P.act(elm[0:Cc, :], elm[0:Cc, :], AF.Exp)
                P.tt("pool", bG[0:Cc, :], beta, Gam[0:Cc, :], ALU.mult)
                if c.kp2 == 'A':
                    continue
                P.tt("pool", rhsG[0:Cc, :, 0:Cc], bcast(c.Ui[0:Cc, 0:Cc], [Cc, H, Cc], 1), bcast(g, [Cc, H, Cc], 2), ALU.mult)
                P.tt("pool", rhsB[0:Cc, 0, :, 0:Cc], bcast(c.ident[0:Cc, 0:Cc], [Cc, H, Cc], 1), bcast(beta, [Cc, H, Cc], 2), ALU.mult)
                P.tt("pool", rhsB[0:Cc, 1, :, 0:Cc], bcast(c.ident[0:Cc, 0:Cc], [Cc, H, Cc], 1), bcast(Gam[0:Cc, :], [Cc, H, Cc], 2), ALU.mult)
                for hh in range(2):
                    hs = slice(hh * 8, hh * 8 + 8)
                    P.mm(pb[1 + hh][:, 0:8 * Cc].rearrange("p (a b) -> p a b", a=8), c.onesb[0:Cc, :], rhsB[0:Cc, 0, hs, 0:Cc])
                    P.mm(pb[3 + hh][:, 0:8 * Cc].rearrange("p (a b) -> p a b", a=8), c.onesb[0:Cc, :], rhsB[0:Cc, 1, hs, 0:Cc])
                    P.mm(pb[5 + hh][0:Cc, 0:8 * Cc].rearrange("p (a b) -> p a b", a=8), c.ones[0:Cc, 0:Cc], rhsG[0:Cc, hs, 0:Cc])
                for hh in range(2):
                    hs = slice(hh * 8, hh * 8 + 8)
                    hq = slice(hh * 4, hh * 4 + 4)
                    Bp = pb[1 + hh][:, 0:8 * Cc].rearrange("p (a b x) -> p a b x", a=4, b=2)
                    Gp = pb[3 + hh][:, 0:8 * Cc].rearrange("p (a b x) -> p a b x", a=4, b=2)
                    P.tt("dve", hp(kbT[:, hs, 0:Cc]), Bp, bcast(kT[:, hq, :], [128, 4, 2, Cc], 2), ALU.mult)
                    P.tt("dve", hp(qdT[:, hs, 0:Cc]), Gp, bcast(qT[:, hq, :], [128, 4, 2, Cc], 2), ALU.mult)
                if c.kp2 == 'B':
                    continue
                for hh in range(2):
                    hs = slice(hh * 8, hh * 8 + 8)
                    Gps = pb[5 + hh][0:Cc, 0:8 * Cc].rearrange("p (a b) -> p a b", a=8)
                    gbc = bcast(gam[0:Cc, hs], [Cc, 8, Cc], 2)
                    P.stt("dve", tA[0:Cc, hs, 0:Cc], Gps, -1.0, gbc, ALU.mult, ALU.add)
                    P.ts("dve", tA[0:Cc, hs, 0:Cc], tA[0:Cc, hs, 0:Cc], 0.0, ALU.min)
                    P.act(Dm[0:Cc, hs, 0:Cc], tA[0:Cc, hs, 0:Cc], AF.Exp)
                    P.tt("dve", tB[0:Cc, hs, 0:Cc], Gps, gbc, ALU.subtract)
                    P.ts("pool", tB[0:Cc, hs, 0:Cc], tB[0:Cc, hs, 0:Cc], 0.0, ALU.min)
                    P.act(DT[0:Cc, hs, 0:Cc], tB[0:Cc, hs, 0:Cc], AF.Exp)
                if c.kp2 == 'C':
                    continue
                for h in range(H):
                    hh, hl = h // 8, h % 8
                    P.mm(pb[1 + hh][0:Cc, hl * Cc:(hl + 1) * Cc], kbT[:, h, 0:Cc], kT[:, h // 2, :])
                    P.mm(pb[3 + hh][0:Cc, hl * Cc:(hl + 1) * Cc], kT[:, h // 2, :], kbT[:, h, 0:Cc])
                for hq_ in range(8):
                    P.mm(pb[0][0:Cc, hq_ * Cc:(hq_ + 1) * Cc], kT[:, hq_, :], qT[:, hq_, :])
                for hh in range(2):
                    hs = slice(hh * 8, hh * 8 + 8)
                    KKb = pb[1 + hh][0:Cc, 0:8 * Cc].rearrange("p (a b) -> p a b", a=8)
                    KKbT = pb[3 + hh][0:Cc, 0:8 * Cc].rearrange("p (a b) -> p a b", a=8)
                    P.stt("dve", tA[0:Cc, hs, 0:Cc], KKb, -1.0, bcast(c.Ls[0:Cc, 0:Cc], [Cc, 8, Cc], 1), ALU.mult, ALU.mult)
                    P.tt("dve", tA[0:Cc, hs, 0:Cc], tA[0:Cc, hs, 0:Cc], Dm[0:Cc, hs, 0:Cc], ALU.mult)
                    P.stt("dve", tB[0:Cc, hs, 0:Cc], KKbT, -1.0, bcast(c.Us[0:Cc, 0:Cc], [Cc, 8, Cc], 1), ALU.mult, ALU.mult)
                    P.tt("dve", tB[0:Cc, hs, 0:Cc], tB[0:Cc, hs, 0:Cc], DT[0:Cc, hs, 0:Cc], ALU.mult)
                    P.tt("dve", Rf[0:Cc, hs, 0:Cc], tB[0:Cc, hs, 0:Cc], bcast(c.ident[0:Cc, 0:Cc], [Cc, 8, Cc], 1), ALU.add)
                P.tt("pool", DT[0:Cc, :, 0:Cc], DT[0:Cc, :, 0:Cc], bcast(c.Ui[0:Cc, 0:Cc], [Cc, H, Cc], 1), ALU.mult)
                QKT = pb[0][0:Cc, 0:8 * Cc].rearrange("p (a x) -> p a x", a=8)
                P.tt("dve", hp(attnT[0:Cc, :, 0:Cc]), bcast(QKT, [Cc, 8, 2, Cc], 2), hp(DT[0:Cc, :, 0:Cc]), ALU.mult)
                if c.kp2 == 'D':
                    continue
                Xf = [tB, DT]
                Yf = [tA, Dm]
                cur = 0
                for l in range(1, L + 1):
                    nxt = 1 - cur
                    for h in range(H):
                        hh, hl = h // 8, h % 8
                        P.mm(pb[1 + hh][0:Cc, hl * Cc:(hl + 1) * Cc], Xf[cur][0:Cc, h, 0:Cc], Yf[cur][0:Cc, h, 0:Cc])
                    if l < L:
                        for h in range(H):
                            hh, hl = h // 8, h % 8
                            P.mm(pb[3 + hh][0:Cc, hl * Cc:(hl + 1) * Cc], Yf[cur][0:Cc, h, 0:Cc], Xf[cur][0:Cc, h, 0:Cc])
                    for hh in range(2):
                        hs = slice(hh * 8, hh * 8 + 8)
                        P.copy("act", Yf[nxt][0:Cc, hs, 0:Cc], pb[1 + hh][0:Cc, 0:8 * Cc].rearrange("p (a b) -> p a b", a=8))
                        if l < L:
                            P.copy("dve", Xf[nxt][0:Cc, hs, 0:Cc], pb[3 + hh][0:Cc, 0:8 * Cc].rearrange("p (a b) -> p a b", a=8))
                    for h in range(H):
                        hh, hl = h // 8, h % 8
                        P.mm(pb[5 + hh][0:Cc, hl * Cc:(hl + 1) * Cc], Yf[nxt][0:Cc, h, 0:Cc], Rf[0:Cc, h, 0:Cc])
                    for hh in range(2):
                        hs = slice(hh * 8, hh * 8 + 8)
                        P.tt("dve", Rf[0:Cc, hs, 0:Cc], pb[5 + hh][0:Cc, 0:8 * Cc].rearrange("p (a b) -> p a b", a=8), Rf[0:Cc, hs, 0:Cc], ALU.add)
                    cur = nxt
                P.copy("act", Rb[0:Cc, :, 0:Cc], Rf[0:Cc, :, 0:Cc])
                if c.kp2 == 'E':
                    continue
                P.tt("dve", bv[0:Cc, :, :], vtm[0:Cc, n, :, :], bcast(beta, [Cc, H, 128], 2), ALU.mult)
                P.tt("pool", hp(kbg[0:Cc, :, :]), bcast(ktm[0:Cc, n, :, :], [Cc, 8, 2, 128], 2), hp(bcast(bG[0:Cc, :], [Cc, H, 128], 2)), ALU.mult)
                P.tt("pool", hp(kdec[0:Cc, :, :]), bcast(ktm[0:Cc, n, :, :], [Cc, 8, 2, 128], 2), hp(bcast(elm[0:Cc, :], [Cc, H, 128], 2)), ALU.mult)
                if c.kp2 == 'F1':
                    continue
                for h in range(H):
                    hh, hl = h // 8, h % 8
                    P.mm(pb[1 + hh][:, hl * Cc:(hl + 1) * Cc], kbg[0:Cc, h, :], Rb[0:Cc, h, 0:Cc])
                for hh in range(2):
                    hs = slice(hh * 8, hh * 8 + 8)
                    P.ts("dve", nwT[:, hs, 0:Cc], pb[1 + hh][:, 0:8 * Cc].rearrange("p (a b) -> p a b", a=8), -1.0, ALU.mult)
                if c.kp2 == 'F2':
                    continue
                for hg in range(4):
                    ps = pb[3 + hg % 2]
                    for hl in range(4):
                        h = hg * 4 + hl
                        P.mm(ps[0:Cc, hl * 128:(hl + 1) * 128], Rb[0:Cc, h, 0:Cc], bv[0:Cc, h, :], start=True, stop=False)
                        P.mm(ps[0:Cc, hl * 128:(hl + 1) * 128], nwT[:, h, 0:Cc], Sb[:, h, :], start=False, stop=True)
                    P.copy("act" if hg % 2 == 0 else "dve", vnew[0:Cc, hg * 4:hg * 4 + 4, :], ps[0:Cc, :].rearrange("p (a b) -> p a b", a=4))
                if c.kp2 == 'F3':
                    continue
                for h in range(H):
                    hh, hl = h // 8, h % 8
                    P.mm(pb[5 + hh][:, hl * Cc:(hl + 1) * Cc], Sb[:, h, :], qdT[:, h, 0:Cc], start=True, stop=False)
                    P.mm(pb[5 + hh][:, hl * Cc:(hl + 1) * Cc], vnew[0:Cc, h, :], attnT[0:Cc, h, 0:Cc], start=False, stop=True)
                if c.kp2 == 'F4':
                    continue
                for hg in range(4):
                    ps = pb[3 + hg % 2]
                    for hl in range(4):
                        h = hg * 4 + hl
                        P.mm(ps[:, hl * 128:(hl + 1) * 128], kdec[0:Cc, h, :], vnew[0:Cc, h, :])
                    for hl in range(4):
                        h = hg * 4 + hl
                        P.stt("dve", S[:, h, :], S[:, h, :], last[:, h:h + 1], ps[:, hl * 128:(hl + 1) * 128], ALU.mult, ALU.add)
                P.copy("act", Sb[:, :, :], S[:, :, :])
                if c.kp2 == 'F':
                    continue
                for hh in range(2):
                    hs = slice(hh * 8, hh * 8 + 8)
                    oT = pb[5 + hh][:, 0:8 * Cc].rearrange("p (a b) -> p a b", a=8)
                    sqv = tA[:, hs, 0:Cc]
                    P.act(sqv, oT, AF.Square)
                    P.mm(pb[1 + hh][:, 0:8 * Cc].rearrange("p (a b) -> p a b", a=8), c.onesH[:], sqv)
                    rs = tB[:, hs, 0:Cc]
                    P.act(rs, pb[1 + hh][:, 0:8 * Cc].rearrange("p (a b) -> p a b", a=8), AF.Ln, bias=c.eps)
                    P.act(rs, rs, AF.Exp, scale=-0.5)
                    P.tt("dve", rs, oT, rs, ALU.mult)
                    P.stt("pool", ogT[:, hs, cs], rs, wn[:, 0:1], zs[:, hs, cs], ALU.mult, ALU.mult)
            for m in range(8):
                ps = pb[m % 2]
                for h in range(H):
                    P.mm(ps[:, 0:tb], wout[:, h, m * 128:(m + 1) * 128], ogT[:, h, 0:tb], start=(h == 0), stop=(h == H - 1))
                P.tt("dve", hb[:, m, 0:tb], hb[:, m, 0:tb], ps[:, 0:tb], ALU.add)
            P.dma("act", c.hTv[:, :, t0:t0 + tb], hb[:, :, 0:tb], out_reg=hreg(t0, t0 + tb))


def mlstm_layer(c, layer, j):
    mlstm_pass1(c, layer, j)
    if c.kstop == "p1":
        return
    mlstm_pass2(c, layer, j)


def mlstm_pass1(c, layer, j):
    P = c.P
    TB = 256
    c.WA.reset(0)
    c.FA.reset(0)
    win = c.WA.get([8, 3088])
    load_weight(c, win, c.mlstm_in[j], 8, 3088)
    hn = c.WA.get([8, TB])
    fm_st = c.WA.get([16, TB])
    ktm_st = c.WA.get([2, 512])
    vtm_st = c.WA.get([2, 1024])
    vb = [c.WA.get([TB]) for _ in range(2)]
    wcol = c.FA.get([8])
    load_vec_cols(c, wcol, c.attn_norm[layer], 8)
    gbias = c.FA.get([16])
    P.dma("sp", gbias, c.mlstm_gate_bias[j].partition_broadcast(128))
    hb = c.FA.get([8, TB])
    sq = [c.FA.get([TB]) for _ in range(2)]
    rstd = c.FA.get([TB])
    gb_st = c.FA.get([2, 16])
    gt = c.FA.get([16])
    for s in range(NS):
        for (o, tb) in seq_blocks(TB):
            t0 = s * TS + o
            nt = (tb + 127) // 128
            tp = min(tb, 128)
            P.dma("sp", hb[:, :, 0:tb], c.hTv[:, :, t0:t0 + tb], in_reg=hreg(t0, t0 + tb))
            rmsnorm_block(c, hb[:, :, 0:tb], tb, wcol, hn[:, :, 0:tb], sq, rstd, c.pb[6])
            for m in range(24):
                ps = c.pb[m % 2]
                for k in range(8):
                    P.mm(ps[:, 0:tb], win[:, k, m * 128:(m + 1) * 128], hn[:, k, 0:tb], start=(k == 0), stop=(k == 7))
                if m < 4:
                    P.copy("act", fm_st[:, m, 0:tb], ps[:, 0:tb])
                elif m < 8:
                    P.ts("dve", fm_st[:, m, 0:tb], ps[:, 0:tb], 0.125, ALU.mult)
                    for ti in range(nt):
                        pt = c.pT[0:tp, (m % 8) * 128:(m % 8) * 128 + 128]
                        P.transpose(pt, fm_st[:, m, ti * 128:ti * 128 + tp], c.identb[:])
                        P.copy("dve", ktm_st[0:tp, ti, (m - 4) * 128:(m - 3) * 128], pt)
                elif m < 16:
                    vbb = vb[m % 2]
                    P.copy("act", vbb[:, 0:tb], ps[:, 0:tb])
                    for ti in range(nt):
                        pt = c.pT[0:tp, (m % 8) * 128:(m % 8) * 128 + 128]
                        P.transpose(pt, vbb[:, ti * 128:ti * 128 + tp], c.identb[:])
                        P.copy("dve", vtm_st[0:tp, ti, (m - 8) * 128:(m - 7) * 128], pt)
                else:
                    P.act(fm_st[:, m - 8, 0:tb], ps[:, 0:tb], AF.Sigmoid)
            for ti in range(nt):
                ps = c.pb[4 + ti % 2]
                for k in range(8):
                    P.mm(ps[0:tp, 0:16], hn[:, k, ti * 128:ti * 128 + tp], win[:, k, 3072:3088], start=(k == 0), stop=(k == 7))
                P.tt("dve", gt[0:tp, :], ps[0:tp, 0:16], gbias[0:tp, :], ALU.add)
                P.act(gt[0:tp, :], gt[0:tp, :], AF.Tanh, scale=1.0 / 15.0)
                P.ts("dve", gb_st[0:tp, ti, 0:8], gt[0:tp, 0:8], 15.0, ALU.mult)
                P.act(gt[0:tp, 8:16], gt[0:tp, 8:16], AF.Exp, scale=-15.0)
                P.act(gt[0:tp, 8:16], gt[0:tp, 8:16], AF.Ln, bias=1.0)
                P.ts("dve", gb_st[0:tp, ti, 8:16], gt[0:tp, 8:16], -1.0, ALU.mult)
            P.dma("act", c.s_fmv[:, 0:16, t0:t0 + tb], fm_st[:, :, 0:tb], out_reg=sreg("s_fm", t0, t0 + tb))
            P.dma("sp", c.s_ktm[t0:t0 + tb, 0:512].rearrange("(n p) f -> p n f", p=tp), ktm_st[0:tp, 0:nt, :],
                  out_reg=sreg("s_ktm", t0, t0 + tb))
            P.dma("sp", c.s_vtm[t0:t0 + tb, 0:1024].rearrange("(n p) f -> p n f", p=tp), vtm_st[0:tp, 0:nt, :],
                  out_reg=sreg("s_vtm", t0, t0 + tb))
            P.dma("sp", c.s_gb[t0:t0 + tb, 0:16].rearrange("(n p) f -> p n f", p=tp), gb_st[0:tp, 0:nt, :],
                  out_reg=sreg("s_gb", t0, t0 + tb))


def mlstm_pass2(c, layer, j):
    P = c.P
    TB = 256
    H = 8
    c.WA.reset(0)
    c.FA.reset(0)
    wout = c.WA.get([8, 1024])
    load_weight(c, wout, c.mlstm_out[j], 8, 1024)
    fm = c.WA.get([16, TB])
    ktm = c.WA.get([4, 8, 64])
    v2 = c.WA.get([4, 8, 256])
    ogT = c.WA.get([8, TB])
    qdT = c.WA.get([4, 64])
    rhsB = c.WA.get([8, 64])
    kdecz = c.WA.get([8, 128])
    attnT = c.WA.get([8, 64])
    Sb = c.WA.get([4, 256])
    hb = c.FA.get([8, TB])
    S = c.FA.get([4, 256])
    gb = c.FA.get([4, 16])
    gam = c.FA.get([8])
    Gam = c.FA.get([8])
    last = c.FA.get([8])
    elm = c.FA.get([8])
    bj = c.FA.get([8])
    wn = c.FA.get([8])
    rhsG = c.FA.get([8, 64])
    DT = c.FA.get([8, 64])
    tA = c.FA.get([8, 64])
    tB = c.FA.get([8, 64])
    load_vec_cols(c, wn, c.mlstm_norm[j], 8)
    pb = c.pb
    P.memset("pool", v2[:, :, :, 128:256], 1.0)
    P.memset("pool", kdecz, 0.0)
    for s in range(NS):
        P.memset("pool", S, 0.0)
        P.memset("pool", Sb, 0.0)
        for (o, tb) in seq_blocks(TB)[:c.knblk]:
            t0 = s * TS + o
            Cc = min(tb, 64)
            nch = tb // Cc
            P.dma("sp", fm[:, :, 0:tb], c.s_fmv[:, 0:16, t0:t0 + tb], in_reg=sreg("s_fm", t0, t0 + tb))
            P.dma("act", ktm[0:Cc, 0:nch, :, :].rearrange("p n h d -> p n (h d)"),
                  c.s_ktm[t0:t0 + tb, 0:512].rearrange("(n p) f -> p n f", p=Cc), in_reg=sreg("s_ktm", t0, t0 + tb))
            for n in range(nch):
                P.dma("act", v2[0:Cc, n, :, 0:128],
                      c.s_vtm[t0 + n * Cc:t0 + (n + 1) * Cc, 0:1024].rearrange("p (h d) -> p h d", h=8),
                      in_reg=sreg("s_vtm", t0, t0 + tb))
            P.dma("act", gb[0:Cc, 0:nch, :], c.s_gb[t0:t0 + tb, 0:16].rearrange("(n p) f -> p n f", p=Cc),
                  in_reg=sreg("s_gb", t0, t0 + tb))
            P.dma("sp", hb[:, :, 0:tb], c.hTv[:, :, t0:t0 + tb], in_reg=hreg(t0, t0 + tb))
            for n in range(nch):
                cs = slice(n * Cc, (n + 1) * Cc)
                ipre = gb[0:Cc, n, 0:8]
                lf = gb[0:Cc, n, 8:16]
                P.mm(pb[0][0:Cc, 0:8], c.Ui[0:Cc, 0:Cc], lf)
                P.mm(pb[0][:, 8:16], c.ones[0:Cc, :], lf)
                P.copy("dve", gam[0:Cc, :], pb[0][0:Cc, 0:8])
                P.act(Gam[0:Cc, :], pb[0][0:Cc, 0:8], AF.Exp)
                P.act(last[:, :], pb[0][:, 8:16], AF.Exp)
                P.tt("dve", elm[0:Cc, :], pb[0][0:Cc, 8:16], gam[0:Cc, :], ALU.subtract)
                P.tt("dve", elm[0:Cc, :], elm[0:Cc, :], ipre, ALU.add)
                P.act(elm[0:Cc, :], elm[0:Cc, :], AF.Exp)
                P.tt("pool", rhsG[0:Cc, :, 0:Cc], bcast(c.Ui[0:Cc, 0:Cc], [Cc, H, Cc], 1), bcast(lf, [Cc, H, Cc], 2), ALU.mult)
                P.tt("pool", rhsB[0:Cc, :, 0:Cc], bcast(c.ident[0:Cc, 0:Cc], [Cc, H, Cc], 1), bcast(Gam[0:Cc, :], [Cc, H, Cc], 2), ALU.mult)
                P.mm(pb[1][:, 0:8 * Cc].rearrange("p (a b) -> p a b", a=8), c.onesb[0:Cc, :], rhsB[0:Cc, :, 0:Cc])
                P.mm(pb[2][0:Cc, 0:8 * Cc].rearrange("p (a b) -> p a b", a=8), c.ones[0:Cc, 0:Cc], rhsG[0:Cc, :, 0:Cc])
                Gm4 = pb[1][:, 0:8 * Cc].rearrange("p (a b x) -> p a b x", a=4, b=2)
                for b_ in range(2):
                    pr = slice(b_ * 64, (b_ + 1) * 64)
                    P.tt("dve", qdT[pr, :, 0:Cc], Gm4[pr, :, b_, :], fm[pr, 0:4, cs], ALU.mult)
                Gps = pb[2][0:Cc, 0:8 * Cc].rearrange("p (a b) -> p a b", a=8)
                P.tt("dve", tB[0:Cc, :, 0:Cc], Gps, bcast(gam[0:Cc, :], [Cc, H, Cc], 2), ALU.subtract)
                P.stt("dve", tB[0:Cc, :, 0:Cc], tB[0:Cc, :, 0:Cc], 0.0, bcast(ipre, [Cc, H, Cc], 2), ALU.min, ALU.add)
                P.act(DT[0:Cc, :, 0:Cc], tB[0:Cc, :, 0:Cc], AF.Exp)
                P.tt("pool", DT[0:Cc, :, 0:Cc], DT[0:Cc, :, 0:Cc], bcast(c.Ui[0:Cc, 0:Cc], [Cc, H, Cc], 1), ALU.mult)
                for h in range(H):
                    a_, b_ = h // 2, h % 2
                    pr = slice(b_ * 64, (b_ + 1) * 64)
                    P.mm(pb[3][0:Cc, h * Cc:(h + 1) * Cc], fm[pr, 4 + a_, cs], fm[pr, a_, cs])
                P.tt("dve", attnT[0:Cc, :, 0:Cc], pb[3][0:Cc, 0:8 * Cc].rearrange("p (a b) -> p a b", a=8), DT[0:Cc, :, 0:Cc], ALU.mult)
                for b_ in range(2):
                    kz = kdecz[0:Cc, :, :].rearrange("p (a b) x -> p a b x", b=2)[:, :, b_, b_ * 64:(b_ + 1) * 64]
                    kin = ktm[0:Cc, n, :, :].rearrange("p (a b) x -> p a b x", b=2)[:, :, b_, :]
                    ein = elm[0:Cc, :].rearrange("p (a b) -> p a b", b=2)[:, :, b_]
                    P.tt("pool", kz, kin, bcast(ein, [Cc, 4, 64], 2), ALU.mult)
                for h in range(H):
                    a_, b_ = h // 2, h % 2
                    pr = slice(b_ * 64, (b_ + 1) * 64)
                    P.mm(pb[4][:, h * Cc:(h + 1) * Cc], Sb[pr, a_, 0:128], qdT[pr, a_, 0:Cc], start=True, stop=False)
                    P.mm(pb[4][:, h * Cc:(h + 1) * Cc], v2[0:Cc, n, h, 0:128], attnT[0:Cc, h, 0:Cc], start=False, stop=True)
                    P.mm(pb[5][:, h * Cc:(h + 1) * Cc], Sb[pr, a_, 128:256], qdT[pr, a_, 0:Cc], start=True, stop=False)
                    P.mm(pb[5][:, h * Cc:(h + 1) * Cc], v2[0:Cc, n, h, 128:256], attnT[0:Cc, h, 0:Cc], start=False, stop=True)
                for a_ in range(4):
                    ps = pb[1 + a_ % 2]
                    P.mm(ps[:, 0:256], kdecz[0:Cc, 2 * a_, :], v2[0:Cc, n, 2 * a_, :], start=True, stop=False)
                    P.mm(ps[:, 0:256], kdecz[0:Cc, 2 * a_ + 1, :], v2[0:Cc, n, 2 * a_ + 1, :], start=False, stop=True)
                    for b_ in range(2):
                        pr = slice(b_ * 64, (b_ + 1) * 64)
                        h = 2 * a_ + b_
                        P.stt("dve", S[pr, a_, :], S[pr, a_, :], last[pr, h:h + 1], ps[pr, 0:256], ALU.mult, ALU.add)
                P.copy("act", Sb[:, :, :], S[:, :, :])
                num = pb[4][:, 0:8 * Cc].rearrange("p (a b) -> p a b", a=8)
                den = pb[5][:, 0:8 * Cc].rearrange("p (a b) -> p a b", a=8)
                dn = tA[:, :, 0:Cc]
                P.act(dn, den, AF.Abs)
                P.ts("pool", dn, dn, 1.0, ALU.max)
                P.recip(dn, dn)
                hh_ = tB[:, :, 0:Cc]
                P.tt("dve", hh_, num, dn, ALU.mult)
                P.act(dn, hh_, AF.Square)
                P.mm(pb[6][:, 0:8 * Cc].rearrange("p (a b) -> p a b", a=8), c.onesH[:], dn)
                P.act(dn, pb[6][:, 0:8 * Cc].rearrange("p (a b) -> p a b", a=8), AF.Ln, bias=c.eps)
                P.act(dn, dn, AF.Exp, scale=-0.5)
                P.tt("pool", hh_, hh_, dn, ALU.mult)
                P.tt("pool", hh_, hh_, bcast(wn[:, :], [128, H, Cc], 2), ALU.mult)
                P.tt("pool", ogT[:, :, cs], hh_, fm[:, 8:16, cs], ALU.mult)
            for m in range(8):
                ps = pb[m % 2]
                for h in range(H):
                    P.mm(ps[:, 0:tb], wout[:, h, m * 128:(m + 1) * 128], ogT[:, h, 0:tb], start=(h == 0), stop=(h == H - 1))
                P.tt("dve", hb[:, m, 0:tb], hb[:, m, 0:tb], ps[:, 0:tb], ALU.add)
            P.dma("act", c.hTv[:, :, t0:t0 + tb], hb[:, :, 0:tb], out_reg=hreg(t0, t0 + tb))


def mla_layer(c, layer, j):
    mla_pass1(c, layer, j)
    if c.kstop == "p1":
        return
    mla_pass2(c, layer, j)


def mla_pass1(c, layer, j):
    import math
    P = c.P
    TB = 256
    c.WA.reset(0)
    c.FA.reset(0)
    win = c.WA.get([8, 672])
    wuq = c.WA.get([3, 1536])
    wukv = c.WA.get([2, 2048])
    wkr = c.WA.get([8, 96])
    load_weight(c, win, c.mla_in[j], 8, 672)
    load_weight(c, wuq, c.mla_uq[j], 3, 1536)
    load_weight(c, wukv, c.mla_ukv[j], 2, 2048)
    P.memset("pool", wkr, 0.0)
    P.copy("pool", wkr[:, :, 64:96], win[:, :, 640:672])
    hn = c.WA.get([8, TB])
    qlat = c.WA.get([3, TB])
    kvlat = c.WA.get([2, TB])
    qk_st = c.WA.get([32, TB])
    vtm_st = c.WA.get([2, 16, 128])
    wcol = c.FA.get([8])
    load_vec_cols(c, wcol, c.attn_norm[layer], 8)
    qn_w = c.FA.get([3])
    load_vec_cols(c, qn_w, c.mla_q_norm[j], 3)
    kvn_w = c.FA.get([2])
    load_vec_cols(c, kvn_w, c.mla_kv_norm[j], 2)
    hw = c.FA.get([2])
    P.dma("sp", hw[0:96, 0:1], c.mla_q_head_norm[j].rearrange("(p o) -> p o", o=1))
    P.dma("sp", hw[0:96, 1:2], c.mla_k_head_norm[j].rearrange("(p o) -> p o", o=1))
    P.ts("dve", hw[0:96, 0:1], hw[0:96, 0:1], 96.0 ** -0.5, ALU.mult)
    ones384 = c.FA.get([128])
    ones256 = c.FA.get([128])
    ones96 = c.FA.get([128])
    P.memset("pool", ones384, 1.0 / 384.0)
    P.memset("pool", ones256, 1.0 / 256.0)
    P.memset("pool", ones96, 1.0 / 96.0)
    Rm = c.FA.get([128])
    P.memset("pool", Rm, 0.0)
    P.copy("pool", Rm[64:96, 80:96], c.ident[64:96, 64:80])
    P.copy("pool", Rm[64:96, 64:80], c.ident[64:96, 80:96])
    P.ts("pool", Rm[64:96, 64:80], Rm[64:96, 64:80], -1.0, ALU.mult)
    Cf = c.FA.get([TS])
    Sf = c.FA.get([TS])
    fa_mark = c.FA.off
    pos_i = c.FA.get([TS]).bitcast(mybir.dt.int32)
    posf = c.FA.get([TS])
    pidx_i = c.FA.get([1]).bitcast(mybir.dt.int32)
    pf = c.FA.get([4])
    P.op("pool", lambda h: h.iota(pos_i[:, 0:TX], [[1, TX]], base=NM, channel_multiplier=0), writes=[pos_i[:, 0:TX]])
    P.op("pool", lambda h: h.iota(pos_i[:, TX:TS], [[1, NM]], base=0, channel_multiplier=0), writes=[pos_i[:, TX:TS]])
    P.op("pool", lambda h: h.iota(pidx_i[:, 0:1], [[1, 1]], base=0, channel_multiplier=1), writes=[pidx_i[:, 0:1]])
    P.copy("dve", posf, pos_i)
    P.copy("dve", pf[:, 0:1], pidx_i[:, 0:1])
    P.ts("dve", pf[:, 1:2], pf[:, 0:1], 80.0, ALU.is_ge, -16.0, ALU.mult)
    P.tt("dve", pf[:, 1:2], pf[:, 1:2], pf[:, 0:1], ALU.add)
    P.ts("dve", pf[:, 1:2], pf[:, 1:2], -64.0, ALU.add)
    P.act(pf[:, 2:3], pf[:, 1:2], AF.Exp, scale=-math.log(10000.0) / 16.0)
    P.ts("dve", pf[:, 2:3], pf[:, 2:3], 1.0 / (2.0 * math.pi), ALU.mult)
    P.ts("dve", posf, posf, pf[:, 2:3], ALU.mult)
    P.copy("dve", pos_i, posf)
    P.copy("dve", Cf, pos_i)
    P.tt("dve", posf, posf, Cf, ALU.subtract)
    P.act(Sf, posf, AF.Sin, scale=math.pi)
    P.act(Cf, posf, AF.Sin, scale=math.pi / 2.0)
    P.tt("dve", Cf, Cf, Cf, ALU.mult)
    P.ts("dve", Cf, Cf, -2.0, ALU.mult, 1.0, ALU.add)
    P.tt("dve", posf, Sf, Cf, ALU.mult)
    P.ts("dve", posf, posf, 2.0, ALU.mult)
    P.tt("dve", Cf, Sf, Sf, ALU.mult)
    P.ts("dve", Cf, Cf, -2.0, ALU.mult, 1.0, ALU.add)
    P.copy("dve", Sf, posf)
    P.memset("pool", Cf[0:64, :], 1.0)
    P.memset("pool", Sf[0:64, :], 0.0)
    c.FA.off = fa_mark
    hb = c.FA.get([8, TB])
    sq = [c.FA.get([TB]) for _ in range(2)]
    rstd = c.FA.get([TB])
    lat = c.FA.get([3, TB])
    kr = c.FA.get([TB])
    kp = c.FA.get([TB])
    qn = c.FA.get([TB])
    t1 = c.FA.get([TB])
    QS = 96.0 ** -0.5
    wv = wukv[:, :, :].rearrange("p k (h x) -> p k h x", x=128)
    P.memset("pool", vtm_st[:, :, :, 64:128], 1.0)

    rs2 = [rstd, c.FA.get([TB])]
    qn2 = [qn, c.FA.get([TB])]
    t12 = [t1, c.FA.get([TB])]
    cnt = [0]

    def head_norm_rope(src_ps, wi, dst, cols, tb, scale):
        i = cnt[0] % 2
        cnt[0] += 1
        sq_, rstd_, qn_, t1_ = sq[i], rs2[i], qn2[i], t12[i]
        pst, prt = (c.pb[5], c.pb[4]) if i == 0 else (c.pb[6], c.pb[4])
        P.act(sq_[0:96, 0:tb], src_ps, AF.Square)
        P.mm(pst[0:96, 0:tb], ones96[0:96, 0:96], sq_[0:96, 0:tb])
        P.act(rstd_[0:96, 0:tb], pst[0:96, 0:tb], AF.Ln, bias=c.eps[0:96, :])
        P.act(rstd_[0:96, 0:tb], rstd_[0:96, 0:tb], AF.Exp, scale=-0.5)
        P.stt("dve", qn_[0:96, 0:tb], src_ps, hw[0:96, wi:wi + 1], rstd_[0:96, 0:tb], ALU.mult, ALU.mult)
        P.mm(prt[0:96, 0:tb], Rm[0:96, 0:96], qn_[0:96, 0:tb])
        P.tt("dve", t1_[0:96, 0:tb], prt[0:96, 0:tb], Sf[0:96, cols], ALU.mult)
        P.tt("pool", qn_[0:96, 0:tb], qn_[0:96, 0:tb], Cf[0:96, cols], ALU.mult)
        P.tt("dve", dst, qn_[0:96, 0:tb], t1_[0:96, 0:tb], ALU.add)

    for s in range(NS):
        for (o, tb) in seq_blocks(TB):
            t0 = s * TS + o
            nt = (tb + 127) // 128
            tp = min(tb, 128)
            cols = slice(o, o + tb)
            P.dma("sp", hb[:, :, 0:tb], c.hTv[:, :, t0:t0 + tb], in_reg=hreg(t0, t0 + tb))
            rmsnorm_block(c, hb[:, :, 0:tb], tb, wcol, hn[:, :, 0:tb], sq, rstd, c.pb[6])
            for m in range(5):
                ps = c.pb[m % 2]
                for k in range(8):
                    P.mm(ps[:, 0:tb], win[:, k, m * 128:(m + 1) * 128], hn[:, k, 0:tb], start=(k == 0), stop=(k == 7))
                P.copy("act", lat[:, m % 3, 0:tb], ps[:, 0:tb])
                if m == 2:
                    rmsnorm_block(c, lat[:, 0:3, 0:tb], tb, qn_w, qlat[:, :, 0:tb], sq, rstd, c.pb[6], nk=3, ones=ones384)
                if m == 4:
                    rmsnorm_block(c, lat[:, 0:2, 0:tb], tb, kvn_w, kvlat[:, :, 0:tb], sq, rstd, c.pb[6], nk=2, ones=ones256)
            ps = c.pb[2]
            for k in range(8):
                P.mm(ps[0:96, 0:tb], wkr[:, k, :], hn[:, k, 0:tb], start=(k == 0), stop=(k == 7))
            P.copy("act", kr[64:96, 0:tb], ps[64:96, 0:tb])
            for h in range(16):
                ps = c.pb[h % 2]
                for k in range(3):
                    P.mm(ps[0:96, 0:tb], wuq[:, k, h * 96:(h + 1) * 96], qlat[:, k, 0:tb], start=(k == 0), stop=(k == 2))
                head_norm_rope(ps[0:96, 0:tb], 0, qk_st[0:96, h, 0:tb], cols, tb, QS)
                ps2 = c.pb[2 + h % 2]
                for k in range(2):
                    P.mm(ps2[0:64, 0:tb], wukv[:, k, h * 128:h * 128 + 64], kvlat[:, k, 0:tb], start=(k == 0), stop=(k == 1))
                P.copy("act", kp[0:64, 0:tb], ps2[0:64, 0:tb])
                P.copy("pool", kp[64:96, 0:tb], kr[64:96, 0:tb])
                head_norm_rope(kp[0:96, 0:tb], 1, qk_st[0:96, 16 + h, 0:tb], cols, tb, 1.0)
            for ti in range(nt):
                for g in range(2):
                    ps = c.pb[g]
                    for k in range(2):
                        P.mm(ps[0:tp, :].rearrange("p (h x) -> p h x", h=8), kvlat[:, k, ti * 128:ti * 128 + tp],
                             wv[:, k, g * 8:(g + 1) * 8, 64:128], start=(k == 0), stop=(k == 1))
                    P.copy("act" if g == 0 else "dve", vtm_st[0:tp, ti, g * 8:(g + 1) * 8, 0:64],
                           ps[0:tp, :].rearrange("p (h x) -> p h x", h=8))
            P.dma("act", c.s_fmv[0:96, :, t0:t0 + tb], qk_st[0:96, :, 0:tb], out_reg=sreg("s_fm", t0, t0 + tb))
            P.dma("sp", c.s_vtm[t0:t0 + tb, :].rearrange("(n p) f -> p n f", p=tp),
                  vtm_st[0:tp, 0:nt, :, :].rearrange("p n h d -> p n (h d)"), out_reg=sreg("s_vtm", t0, t0 + tb))


def mla_pass2(c, layer, j):
    P = c.P
    c.WA.reset(0)
    c.FA.reset(0)
    wout = c.WA.get([16, 1024])
    P = c.P
    sv = c.mla_out[j].rearrange("(h p) f -> p h f", p=64)
    for h0 in range(0, 16, 4):
        P.dma("pool", wout[0:64, h0:h0 + 4, :], sv[:, h0:h0 + 4, :])
    og = c.WA.get([16, TS])
    kTs = [c.WA.get([TS]) for _ in range(2)]
    qTs = [c.WA.get([TS]) for _ in range(2)]
    vxs = [c.WA.get([16, 128]) for _ in range(2)]
    vms = [c.WA.get([128]) for _ in range(2)]
    pts = [c.WA.get([512]) for _ in range(3)]
    Uib = c.WA.get([128])
    P.copy("dve", Uib, c.Ui[:])
    rr = c.FA.get([512])
    cp = c.FA.get([512])
    hb = c.FA.get([8, 512])
    pb = c.pb
    npt = 0
    for s in range(NS):
        sb = s * TS
        for h in range(16):
            kT, qT, vx, vm = kTs[h % 2], qTs[h % 2], vxs[h % 2], vms[h % 2]
            P.dma("sp", qT[0:96, :], c.s_fmv[0:96, h, sb:sb + TS], in_reg=sreg("s_fm", sb, sb + TS))
            P.dma("sp", kT[0:96, :], c.s_fmv[0:96, 16 + h, sb:sb + TS], in_reg=sreg("s_fm", sb, sb + TS))
            P.dma("act", vx, c.s_vtm[sb:sb + TX, h * 128:(h + 1) * 128].rearrange("(n p) f -> p n f", p=128),
                  in_reg=sreg("s_vtm", sb, sb + TS))
            P.dma("act", vm[0:NM, :], c.s_vtm[sb + TX:sb + TS, h * 128:(h + 1) * 128], in_reg=sreg("s_vtm", sb, sb + TS))
            qblocks = [(TX, NM)] + [(qb * 512, 512) for qb in range(4)]
            for bi, (q0, nq) in enumerate(qblocks):
                acc = pb[4 + bi % 2]
                st = pb[npt % 4]
                pt = pts[npt % 3]
                npt += 1
                P.mm(st[0:NM, 0:nq], kT[0:96, TX:TS], qT[0:96, q0:q0 + nq])
                P.act(pt[0:NM, 0:nq], st[0:NM, 0:nq], AF.Exp)
                if q0 == TX:
                    P.tt("pool", pt[0:NM, 0:NM], pt[0:NM, 0:NM], Uib[0:NM, 0:NM], ALU.mult)
                P.mm(acc[:, 0:nq], vm[0:NM, :], pt[0:NM, 0:nq], start=True, stop=(q0 == TX))
                if q0 != TX:
                    nkt = q0 // 128 + 4
                    for kt in range(nkt):
                        jd = kt - q0 // 128
                        c0 = 128 * jd if jd > 0 else 0
                        st = pb[npt % 4]
                        pt = pts[npt % 3]
                        npt += 1
                        P.mm(st[:, c0:nq], kT[0:96, kt * 128:(kt + 1) * 128], qT[0:96, q0 + c0:q0 + nq])
                        P.act(pt[:, c0:nq], st[:, c0:nq], AF.Exp)
                        if jd >= 0:
                            P.tt("pool", pt[:, c0:c0 + 128], pt[:, c0:c0 + 128], Uib, ALU.mult)
                        P.mm(acc[:, c0:nq], vx[:, kt, :], pt[:, c0:nq], start=False, stop=(kt == nkt - 1))
                P.recip(rr[64:128, 0:nq], acc[64:128, 0:nq])
                P.memset("pool", rr[0:64, 0:nq], 0.0)
                st = pb[npt % 4]
                npt += 1
                P.mm(st[0:64, 0:nq], c.ident[:, 64:128], rr[:, 0:nq])
                P.copy("act", cp[0:64, 0:nq], st[0:64, 0:nq])
                P.tt("dve", og[0:64, h, q0:q0 + nq], acc[0:64, 0:nq], cp[0:64, 0:nq], ALU.mult)
        for (q0, nq) in [(TX, NM)] + [(qb * 512, 512) for qb in range(4)]:
            t0 = sb + q0
            P.dma("sp", hb[:, :, 0:nq], c.hTv[:, :, t0:t0 + nq], in_reg=hreg(t0, t0 + nq))
            for m in range(8):
                ps = pb[6] if m % 2 == 0 else pb[0]
                for h in range(16):
                    P.mm(ps[:, 0:nq], wout[0:64, h, m * 128:(m + 1) * 128], og[0:64, h, q0:q0 + nq], start=(h == 0), stop=(h == 15))
                P.tt("dve", hb[:, m, 0:nq], hb[:, m, 0:nq], ps[:, 0:nq], ALU.add)
            P.dma("act", c.hTv[:, :, t0:t0 + nq], hb[:, :, 0:nq], out_reg=hreg(t0, t0 + nq))


_NC_CACHE = {}
W_NAMES = ["meta_tokens", "attn_norm", "ffn_norm", "ff_up", "ff_down", "gdn_in", "gdn_conv", "gdn_a_log",
           "gdn_dt_bias", "gdn_norm", "gdn_out", "mlstm_in", "mlstm_gate_bias", "mlstm_norm", "mlstm_out",
           "mla_in", "mla_q_norm", "mla_uq", "mla_kv_norm", "mla_ukv", "mla_q_head_norm", "mla_k_head_norm",
           "mla_out"]


def kernel(**inputs):
    nc = build_program()
    x = np.ascontiguousarray(np.asarray(inputs["x"], dtype=np.float32))
    shared = {k: np.ascontiguousarray(np.asarray(inputs[k], dtype=np.float32)) for k in W_NAMES}
    in_maps = []
    for core in range(8):
        m = dict(shared)
        m["x"] = x[core * NS:(core + 1) * NS]
        in_maps.append(m)
    res = run_bass_kernel_spmd(nc, in_maps, core_ids=list(range(8)))
    return np.concatenate([r["y"] for r in res.results], axis=0)
```
